# Optimizing a Trainium2 kernel written in Bass

```python
import math
import jax
import jax.numpy as jnp
from jax import lax
import numpy as np

D_MODEL = 1024
BATCH = 16
SEQ = 2048
DEPTH = 4
DEC_BATCH = 4
DEC_SEQ = 8192
PAST_LEN = 128

N_MIXERS = 3
N_LAYERS_A = (DEPTH + 2) // 3
N_LAYERS_B = (DEPTH + 1) // 3
N_LAYERS_C = DEPTH // 3

A_GROUPS = ((128, 1), (512, 4), (2048, 16))
A_N_GROUPS = len(A_GROUPS)
A_HEADS = 16
A_HEAD_DIM = D_MODEL // A_HEADS
A_WIDTH = A_HEADS * A_HEAD_DIM
ROPE_THETA = 10000.0

HY_WIDTH = D_MODEL
HY_SHORT = 3
HY_EMB = 33
HY_BANDS = (HY_EMB - 1) // 2
HY_FILTER_HIDDEN = 64
HY_DECAY_TARGET = 1e-2
HY_DECAY_STRONG_PCT = 0.3
HY_DECAY_WEAK_PCT = 1.5

C_CHUNK = 128
C_WIDTH = D_MODEL
C_GROUPS = 8
C_GROUP_DIM = C_WIDTH // C_GROUPS

MEM_LEN = 256
X_HEADS = 4
X_HEAD_DIM = D_MODEL // X_HEADS

D_FF = -(-8 * D_MODEL // (3 * 256)) * 256

EPS = 1e-6
NEG_INF = -1e30

kernel_name = 'hybrid_dilated_hyena_sgu_encoder'


def rms_norm(x, g):
    xf = x.astype(jnp.float32)
    y = xf * lax.rsqrt(jnp.mean(xf * xf, axis=-1, keepdims=True) + EPS)
    return (y * g.astype(jnp.float32)).astype(x.dtype)


def layer_norm(x, g, b):
    xf = x.astype(jnp.float32)
    mu = jnp.mean(xf, axis=-1, keepdims=True)
    xc = xf - mu
    y = xc * lax.rsqrt(jnp.mean(xc * xc, axis=-1, keepdims=True) + EPS)
    return (y * g.astype(jnp.float32) + b.astype(jnp.float32)).astype(x.dtype)


def rope_tables(seq_len, dim):
    inv = ROPE_THETA ** (-jnp.arange(0, dim, 2, dtype=jnp.float32) / dim)
    ang = jnp.arange(seq_len, dtype=jnp.float32)[:, None] * inv[None, :]
    return jnp.cos(ang), jnp.sin(ang)


def apply_rope(x, cos, sin):
    xf = x.astype(jnp.float32)
    half = xf.shape[-1] // 2
    x1, x2 = xf[..., :half], xf[..., half:]
    c, s = cos[None, :, None, :], sin[None, :, None, :]
    return jnp.concatenate([x1 * c - x2 * s, x2 * c + x1 * s], axis=-1).astype(x.dtype)


def dilated_window_attention(q, k, v, window, dilation):
    b, s, h, dh = q.shape
    half = window // (2 * dilation)
    blk = half
    n_sub = s // dilation
    n_blk = -(-n_sub // blk)
    pad = n_blk * blk - n_sub
    n = b * dilation

    def to_sub(t):
        t = t.reshape(b, n_sub, dilation, h, dh).transpose(0, 2, 1, 3, 4)
        return t.reshape(n, n_sub, h, dh)

    def band(t):
        tp = jnp.pad(t, ((0, 0), (blk, blk + pad), (0, 0), (0, 0))).reshape(n, n_blk + 2, blk, h, dh)
        return jnp.concatenate([tp[:, :-2], tp[:, 1:-1], tp[:, 2:]], axis=2)

    qb = jnp.pad(to_sub(q), ((0, 0), (0, pad), (0, 0), (0, 0))).reshape(n, n_blk, blk, h, dh)
    kb = band(to_sub(k))
    vb = band(to_sub(v))
    scores = jnp.einsum('nbqhd,nbkhd->nbhqk', qb, kb).astype(jnp.float32) * (dh ** -0.5)
    qi = jnp.arange(n_blk)[:, None, None] * blk + jnp.arange(blk)[None, :, None]
    kj = jnp.arange(n_blk)[:, None, None] * blk - blk + jnp.arange(3 * blk)[None, None, :]
    valid = (jnp.abs(qi - kj) <= half) & (kj >= 0) & (kj < n_sub)
    scores = jnp.where(valid[None, :, None], scores, NEG_INF)
    lse = jax.nn.logsumexp(scores, axis=-1)
    p = jnp.exp(scores - lse[..., None])
    out = jnp.einsum('nbhqk,nbkhd->nbqhd', p, vb.astype(jnp.float32))
    out = out.reshape(n, n_blk * blk, h, dh)[:, :n_sub]
    lse = lse.transpose(0, 1, 3, 2).reshape(n, n_blk * blk, h)[:, :n_sub]

    def from_sub(t):
        t = t.reshape((b, dilation, n_sub) + t.shape[2:])
        t = jnp.swapaxes(t, 1, 2)
        return t.reshape((b, s) + t.shape[3:])

    return from_sub(out), from_sub(lse)


def mixer_dilated(xn, w_in, w_out):
    b, s, _ = xn.shape
    qkv = (xn @ w_in).reshape(b, s, A_N_GROUPS, 3, A_HEADS, A_HEAD_DIM)
    cos, sin = rope_tables(s, A_HEAD_DIM)
    outs, lses = [], []
    for g, (window, dilation) in enumerate(A_GROUPS):
        q = apply_rope(qkv[:, :, g, 0], cos, sin)
        k = apply_rope(qkv[:, :, g, 1], cos, sin)
        o, l = dilated_window_attention(q, k, qkv[:, :, g, 2], window, dilation)
        outs.append(o)
        lses.append(l)
    wts = jax.nn.softmax(jnp.stack(lses), axis=0)
    o = jnp.einsum('gbsh,gbshd->bshd', wts, jnp.stack(outs))
    return o.reshape(b, s, A_WIDTH).astype(xn.dtype) @ w_out


def short_conv(u, w, bias):
    up = jnp.pad(u, ((0, 0), (1, 1), (0, 0)))
    return up[:, :-2] * w[0] + up[:, 1:-1] * w[1] + up[:, 2:] * w[2] + bias


def hyena_position_features(seq_len):
    t = jnp.linspace(0.0, 1.0, seq_len, dtype=jnp.float32)[:, None]
    w = 2.0 * math.pi * jnp.arange(seq_len, dtype=jnp.float32)[:, None] / seq_len
    f = jnp.linspace(1e-4, HY_BANDS - 1, HY_BANDS, dtype=jnp.float32)[None, :]
    return jnp.concatenate([t, jnp.cos(f * w), -jnp.sin(f * w)], axis=-1)


def hyena_decay(seq_len):
    t = jnp.linspace(0.0, 1.0, seq_len, dtype=jnp.float32)[:, None]
    max_decay = math.log(HY_DECAY_TARGET) / HY_DECAY_STRONG_PCT
    min_decay = math.log(HY_DECAY_TARGET) / HY_DECAY_WEAK_PCT
    deltas = jnp.linspace(min_decay, max_decay, HY_WIDTH, dtype=jnp.float32)
    return jnp.exp(-t * jnp.abs(deltas)[None, :])


def hyena_filters(seq_len, f_w1, f_b1, f_w2, f_b2, f_w3, f_b3, f_wout, f_freq):
    f32 = jnp.float32
    freq = f_freq.astype(f32)
    z = hyena_position_features(seq_len)
    hid = jnp.sin(freq * (z @ f_w1.astype(f32) + f_b1.astype(f32)))
    hid = jnp.sin(freq * (hid @ f_w2.astype(f32) + f_b2.astype(f32)))
    hid = jnp.sin(freq * (hid @ f_w3.astype(f32) + f_b3.astype(f32)))
    h = (hid @ f_wout.astype(f32)).reshape(seq_len, 2, HY_WIDTH) * hyena_decay(seq_len)[:, None, :]
    h_fwd, h_bwd = h[:, 0], h[:, 1]
    h_full = jnp.concatenate(
        [h_fwd[:1] + h_bwd[:1], h_fwd[1:], jnp.zeros((1, HY_WIDTH), f32), h_bwd[1:][::-1]], axis=0)
    return h_full / jnp.sum(jnp.abs(h_full), axis=0, keepdims=True)


def long_conv(v, h_full):
    seq_len = v.shape[1]
    vf = jnp.fft.rfft(v.astype(jnp.float32), n=2 * seq_len, axis=1)
    hf = jnp.fft.rfft(h_full, n=2 * seq_len, axis=0)
    return jnp.fft.irfft(vf * hf[None], n=2 * seq_len, axis=1)[:, :seq_len]


def mixer_hyena(xn, w_in, conv_w, conv_b, f_w1, f_b1, f_w2, f_b2, f_w3, f_b3, f_wout, f_freq, bias_d, w_out):
    b, s, _ = xn.shape
    u = short_conv(xn @ w_in, conv_w, conv_b)
    x0, x1, v = jnp.split(u, 3, axis=-1)
    h_full = hyena_filters(s, f_w1, f_b1, f_w2, f_b2, f_w3, f_b3, f_wout, f_freq)
    v = v * x1
    v = (long_conv(v, h_full) + bias_d.astype(jnp.float32) * v.astype(jnp.float32)).astype(xn.dtype)
    return (v * x0) @ w_out


def mixer_sgu(xn, w_in, ln_g, ln_b, w_s, b_s, w_out):
    b, s, _ = xn.shape
    z = jax.nn.gelu(xn @ w_in, approximate=False)
    zu, zv = jnp.split(z, 2, axis=-1)
    zv = layer_norm(zv, ln_g, ln_b).reshape(b, s // C_CHUNK, C_CHUNK, C_GROUPS, C_GROUP_DIM)
    sv = jnp.einsum('hpq,bnqhc->bnphc', w_s, zv) + b_s.T[:, :, None]
    return (zu * sv.reshape(b, s, C_WIDTH)) @ w_out


def cross_attention(hn, mem, w_q, w_kv, w_o):
    b, s, _ = hn.shape
    q = (hn @ w_q).reshape(b, s, X_HEADS, X_HEAD_DIM)
    kv = (mem @ w_kv).reshape(b, mem.shape[1], 2, X_HEADS, X_HEAD_DIM)
    sc = jnp.einsum('bshd,bmhd->bhsm', q, kv[:, :, 0]).astype(jnp.float32) * (X_HEAD_DIM ** -0.5)
    p = jax.nn.softmax(sc, axis=-1).astype(hn.dtype)
    o = jnp.einsum('bhsm,bmhd->bshd', p, kv[:, :, 1]).reshape(b, s, X_HEADS * X_HEAD_DIM)
    return o @ w_o


def swiglu(hn, w_gu, w_down):
    gate, up = jnp.split(hn @ w_gu, 2, axis=-1)
    return (jax.nn.silu(gate) * up) @ w_down


def trunk(x, mem, g_mix, g_cross, g_ffn, g_final, a_w_in, a_w_out, b_w_in, b_conv_w, b_conv_b,
          b_f_w1, b_f_b1, b_f_w2, b_f_b2, b_f_w3, b_f_b3, b_f_wout, b_f_freq, b_bias_d, b_w_out,
          c_w_in, c_ln_g, c_ln_b, c_w_s, c_b_s, c_w_out, x_w_q, x_w_kv, x_w_o, f_w_gu, f_w_down):
    for i in range(DEPTH):
        kind, j = i % N_MIXERS, i // N_MIXERS
        xn = rms_norm(x, g_mix[i])
        if kind == 0:
            m = mixer_dilated(xn, a_w_in[j], a_w_out[j])
        elif kind == 1:
            m = mixer_hyena(xn, b_w_in[j], b_conv_w[j], b_conv_b[j], b_f_w1[j], b_f_b1[j], b_f_w2[j],
                            b_f_b2[j], b_f_w3[j], b_f_b3[j], b_f_wout[j], b_f_freq[j], b_bias_d[j], b_w_out[j])
        else:
            m = mixer_sgu(xn, c_w_in[j], c_ln_g[j], c_ln_b[j], c_w_s[j], c_b_s[j], c_w_out[j])
        x = x + m
        x = x + cross_attention(rms_norm(x, g_cross[i]), mem, x_w_q[i], x_w_kv[i], x_w_o[i])
        x = x + swiglu(rms_norm(x, g_ffn[i]), f_w_gu[i], f_w_down[i])
    return rms_norm(x, g_final)


def setup_inputs(seed: int = 0) -> dict:
    key = jax.random.key(seed)
    ks = iter(jax.random.split(key, 40))
    d = D_MODEL
    nA, nB, nC = N_LAYERS_A, N_LAYERS_B, N_LAYERS_C

    def nrm(shape, scale):
        return jax.random.normal(next(ks), shape, jnp.float32) * scale

    def gain(shape):
        return 1.0 + nrm(shape, 0.1)

    return {
        'x_prompt': nrm((BATCH, SEQ, d), 1.0),
        'x_sample': nrm((DEC_BATCH, DEC_SEQ, d), 1.0),
        'mem_prompt': nrm((BATCH, MEM_LEN, d), 1.0),
        'mem_sample': nrm((DEC_BATCH, MEM_LEN, d), 1.0),
        'g_mix': gain((DEPTH, d)),
        'g_cross': gain((DEPTH, d)),
        'g_ffn': gain((DEPTH, d)),
        'g_final': gain((d,)),
        'a_w_in': nrm((nA, d, A_N_GROUPS * 3 * A_WIDTH), d ** -0.5),
        'a_w_out': nrm((nA, A_WIDTH, d), A_WIDTH ** -0.5),
        'b_w_in': nrm((nB, d, 3 * HY_WIDTH), d ** -0.5),
        'b_conv_w': nrm((nB, HY_SHORT, 3 * HY_WIDTH), HY_SHORT ** -0.5),
        'b_conv_b': nrm((nB, 3 * HY_WIDTH), 0.02),
        'b_f_w1': nrm((nB, HY_EMB, HY_FILTER_HIDDEN), HY_EMB ** -0.5),
        'b_f_b1': nrm((nB, HY_FILTER_HIDDEN), 0.02),
        'b_f_w2': nrm((nB, HY_FILTER_HIDDEN, HY_FILTER_HIDDEN), HY_FILTER_HIDDEN ** -0.5),
        'b_f_b2': nrm((nB, HY_FILTER_HIDDEN), 0.02),
        'b_f_w3': nrm((nB, HY_FILTER_HIDDEN, HY_FILTER_HIDDEN), HY_FILTER_HIDDEN ** -0.5),
        'b_f_b3': nrm((nB, HY_FILTER_HIDDEN), 0.02),
        'b_f_wout': nrm((nB, HY_FILTER_HIDDEN, 2 * HY_WIDTH), HY_FILTER_HIDDEN ** -0.5),
        'b_f_freq': gain((nB, HY_FILTER_HIDDEN)),
        'b_bias_d': nrm((nB, HY_WIDTH), 0.5),
        'b_w_out': nrm((nB, HY_WIDTH, d), HY_WIDTH ** -0.5),
        'c_w_in': nrm((nC, d, 2 * C_WIDTH), d ** -0.5),
        'c_ln_g': gain((nC, C_WIDTH)),
        'c_ln_b': nrm((nC, C_WIDTH), 0.02),
        'c_w_s': nrm((nC, C_GROUPS, C_CHUNK, C_CHUNK), C_CHUNK ** -0.5),
        'c_b_s': gain((nC, C_GROUPS, C_CHUNK)),
        'c_w_out': nrm((nC, C_WIDTH, d), C_WIDTH ** -0.5),
        'x_w_q': nrm((DEPTH, d, X_HEADS * X_HEAD_DIM), d ** -0.5),
        'x_w_kv': nrm((DEPTH, d, 2 * X_HEADS * X_HEAD_DIM), d ** -0.5),
        'x_w_o': nrm((DEPTH, X_HEADS * X_HEAD_DIM, d), (X_HEADS * X_HEAD_DIM) ** -0.5),
        'f_w_gu': nrm((DEPTH, d, 2 * D_FF), d ** -0.5),
        'f_w_down': nrm((DEPTH, D_FF, d), D_FF ** -0.5),
    }


def reference(x_prompt, x_sample, mem_prompt, mem_sample, g_mix, g_cross, g_ffn, g_final,
              a_w_in, a_w_out, b_w_in, b_conv_w, b_conv_b, b_f_w1, b_f_b1, b_f_w2, b_f_b2,
              b_f_w3, b_f_b3, b_f_wout, b_f_freq, b_bias_d, b_w_out, c_w_in, c_ln_g, c_ln_b,
              c_w_s, c_b_s, c_w_out, x_w_q, x_w_kv, x_w_o, f_w_gu, f_w_down):
    weights = dict(g_mix=g_mix, g_cross=g_cross, g_ffn=g_ffn, g_final=g_final,
                   a_w_in=a_w_in, a_w_out=a_w_out, b_w_in=b_w_in, b_conv_w=b_conv_w, b_conv_b=b_conv_b,
                   b_f_w1=b_f_w1, b_f_b1=b_f_b1, b_f_w2=b_f_w2, b_f_b2=b_f_b2, b_f_w3=b_f_w3,
                   b_f_b3=b_f_b3, b_f_wout=b_f_wout, b_f_freq=b_f_freq, b_bias_d=b_bias_d,
                   b_w_out=b_w_out, c_w_in=c_w_in, c_ln_g=c_ln_g, c_ln_b=c_ln_b, c_w_s=c_w_s,
                   c_b_s=c_b_s, c_w_out=c_w_out, x_w_q=x_w_q, x_w_kv=x_w_kv, x_w_o=x_w_o,
                   f_w_gu=f_w_gu, f_w_down=f_w_down)
    y_prompt = trunk(x_prompt, mem_prompt, **weights)
    y_sample = trunk(x_sample, mem_sample, **weights)
    return (y_prompt, y_sample)
```

```python
import contextlib
import math
import numpy as np
import ml_dtypes
import concourse.bass as bass
import concourse.mybir as mybir
from concourse.bass_utils import run_bass_kernel_spmd

F32 = mybir.dt.float32
BF16 = mybir.dt.bfloat16
AF = mybir.ActivationFunctionType
ALU = mybir.AluOpType
AX = mybir.AxisListType

D = 1024
KC = 8
DFF = 2816
FC = 22
MEM = 256
EPS = 1e-6
NFFT = 16384
TWO_PI = 2.0 * math.pi

DBG = {}
EPOCH = 30000
NRING = 16
DEPOCH = 1800


class Res:
    __slots__ = ("lw", "rd", "name", "excl")

    def __init__(self, name="", excl=False):
        self.lw = None
        self.rd = {}
        self.name = name
        self.excl = excl


class T:
    def __init__(self, t, name="", mm=False):
        self.t = t
        self.r = Res(name)
        self.mm = mm

    def __getitem__(self, idx):
        return self.t[idx]


def _res(x):
    return x.r if isinstance(x, T) else x


class _Rec:
    def __init__(self):
        self.call = None

    def __getattr__(self, name):
        def f(*a, **k):
            self.call = (name, a, k)
            return self
        return f


class Emitter:
    ENGS = ("pe", "act", "dve", "pool", "sp")

    def __init__(self, nc, stack):
        self.nc = nc
        self.stack = stack
        self.ops = {e: [] for e in self.ENGS}
        self.ccount = {e: 0 for e in self.ENGS}
        self.dcount = {e: 0 for e in self.ENGS}
        self.seen = {e: {} for e in self.ENGS}
        self.sems = {}
        self.nsem = 0

    def sem(self, key):
        s = self.sems.get(key)
        if s is None:
            s = self.stack.enter_context(self.nc.semaphore("s%d" % self.nsem))
            self.nsem += 1
            self.sems[key] = s
        return s

    def _target(self, ident):
        k, e, i = ident
        if k == "c":
            return ("c", e, i // EPOCH), (i % EPOCH) + 1
        slot = i % NRING
        j = i // NRING
        return ("d", e, slot, j // DEPOCH), 16 * ((j % DEPOCH) + 1)

    def _need(self, eng, ident, waits):
        if ident is None:
            return
        key, val = self._target(ident)
        if self.seen[eng].get(key, 0) >= val:
            return
        self.seen[eng][key] = val
        waits.append((key, val))

    def op(self, eng, fn, r=(), w=(), dma=False, ns=False):
        waits = []
        deps = []
        for x in r:
            res = _res(x)
            deps.append(res.lw)
            if res.excl:
                deps.extend(v for k, v in res.rd.items() if k[0] != eng)
        for x in w:
            res = _res(x)
            deps.append(res.lw)
            deps.extend(res.rd.values())
        for d in deps:
            if d is None:
                continue
            if d[1] == eng and d[0] == "c" and not dma and (eng == "pe" or ns):
                continue
            self._need(eng, d, waits)
        if dma:
            i = self.dcount[eng]
            self.dcount[eng] += 1
            ident = ("d", eng, i)
            if i >= NRING:
                self._need(eng, ("d", eng, i - NRING), waits)
        else:
            i = self.ccount[eng]
            self.ccount[eng] += 1
            ident = ("c", eng, i)
        key, val = self._target(ident)
        rec = _Rec()
        fn(rec)
        assert rec.call is not None
        call = rec.call
        self.ops[eng].append((call, waits, key, 16 if dma else 1))
        for x in r:
            _res(x).rd[(eng, dma)] = ident
        for x in w:
            res = _res(x)
            res.lw = ident
            res.rd = {}
        return ident

    def dma(self, out, in_, r=(), w=(), slow=False, eng="sp"):
        if slow:
            return self.op(eng, lambda e: e.dma_start(out=out, in_=in_, allow_slow_non_contiguous=True),
                           r=r, w=w, dma=True)
        return self.op(eng, lambda e: e.dma_start(out=out, in_=in_), r=r, w=w, dma=True)

    def barrier(self, engs=None):
        for x in (engs or self.ENGS):
            waits = []
            for e in self.ENGS:
                if self.ccount[e] and e != x:
                    self._need(x, ("c", e, self.ccount[e] - 1), waits)
                n = self.dcount[e]
                for i in range(max(0, n - NRING), n):
                    self._need(x, ("d", e, i), waits)
            if self.ccount[x]:
                self._need(x, ("c", x, self.ccount[x] - 1), waits)
            if waits:
                self.ops[x].append((None, waits, None, 0))

    def replay(self):
        nc = self.nc
        for e in self.ENGS:
            for fn, waits, key, inc in self.ops[e]:
                for k, v in waits:
                    self.sem(k)
                if key is not None:
                    self.sem(key)
        engmap = {"pe": "tensor", "act": "scalar", "dve": "vector", "pool": "gpsimd", "sp": "sync"}
        with nc.Block() as block:
            for e in self.ENGS:
                if not self.ops[e]:
                    continue
                ops = self.ops[e]

                def body(engine, ops=ops):
                    for fn, waits, key, inc in ops:
                        for k, v in waits:
                            engine.wait_ge(self.sems[k], v)
                        if fn is not None:
                            name, a, k = fn
                            getattr(engine, name)(*a, **k).then_inc(self.sems[key], inc)

                getattr(block, engmap[e])(body)


class TiledW:
    def __init__(self, ap4, gcols):
        self.ap = ap4
        self.g = gcols

    def rearrange(self, *a, **k):
        return self

    def __getitem__(self, idx):
        sp, sc, sn = idx
        c0, n = sn.start, sn.stop - sn.start
        g, off = c0 // self.g, c0 % self.g
        assert off + n <= self.g, (c0, n, self.g)
        return self.ap[g][:, :, off:off + n]


TILE_SPECS = {
    "a_w_in": ("a_w_in", 8, 512, 0, 9216), "a_w_out": ("a_w_out", 8, 512, 0, 1024),
    "b_w_in": ("b_w_in", 8, 512, 0, 3072), "b_w_out": ("b_w_out", 8, 512, 0, 1024),
    "c_w_in": ("c_w_in", 8, 512, 0, 2048), "c_w_out": ("c_w_out", 8, 512, 0, 1024),
    "x_w_q": ("x_w_q", 8, 512, 0, 1024), "x_w_kv": ("x_w_kv", 8, 512, 0, 2048), "x_w_o": ("x_w_o", 8, 512, 0, 1024),
    "f_w_g": ("f_w_gu", 8, 512, 0, 2816), "f_w_u": ("f_w_gu", 8, 512, 2816, 2816),
    "f_w_down": ("f_w_down", 22, 128, 0, 1024),
}


class Rot:
    def __init__(self, tiles):
        self.tiles = tiles
        self.i = 0

    def get(self):
        t = self.tiles[self.i % len(self.tiles)]
        self.i += 1
        return t


A_GROUPS = ((128, 1), (512, 4), (2048, 16))


def _coarse_index(nt, lseg):
    tiles_per_seg = lseg // 128
    i = np.arange(nt)
    if lseg >= nt * 128:
        return i
    return (i // tiles_per_seg) * (2 * tiles_per_seg) + (i % tiles_per_seg)


def build_consts(nt, lseg):
    Tn = nt * 128
    f32 = np.float32
    c = {}
    c["ident"] = np.eye(128, dtype=f32)
    c["identb"] = np.eye(128, dtype=f32)
    obm = np.zeros((2, 128, 128), f32)
    obm[0, :, 0:64] = 1.0
    obm[1, :, 64:128] = 1.0
    c["obm"] = obm
    R = np.zeros((64, 64), f32)
    for j in range(32):
        R[j + 32, j] = -1.0
        R[j, j + 32] = 1.0
    rb = np.zeros((128, 128), f32)
    rb[:64, :64] = R
    rb[64:, 64:] = R
    c["rblk"] = rb
    pos = (np.arange(Tn) % lseg).astype(np.float64)
    inv = 10000.0 ** (-np.arange(0, 64, 2, dtype=np.float64) / 64)
    inv32 = inv.astype(f32).astype(np.float64)
    ang = (pos.astype(f32)[:, None] * inv32.astype(f32)[None, :]).astype(f32)
    cos = np.cos(ang.astype(np.float64)).astype(f32)
    sin = np.sin(ang.astype(np.float64)).astype(f32)
    rows = np.arange(128) % 32
    c["ropec"] = np.ascontiguousarray(cos[:, rows].T)
    c["ropes"] = np.ascontiguousarray(sin[:, rows].T)
    for gi, (window, d) in enumerate(A_GROUPS):
        nsub = Tn // d
        ntile = nsub // 128
        nseg_sub = lseg // d
        m = np.zeros((ntile, 2, 128, 128), f32)
        kk = np.arange(128)[:, None]
        qq = np.arange(128)[None, :]
        for t in range(ntile):
            nq = 128 * t + qq
            for X in range(2):
                nk = 128 * t - 64 + 128 * X + kk
                ok = (np.abs(nq - nk) <= 64) & (nk >= 0) & (nk < nsub) & ((nq // nseg_sub) == (nk // nseg_sub))
                m[t, X] = ok.astype(f32)
        c["abias%d" % gi] = ((m - 1.0) * 30000.0).astype(f32)
    L = lseg
    tl = np.linspace(0.0, 1.0, L, dtype=f32)
    w = (2.0 * math.pi * np.arange(L, dtype=f32) / L).astype(f32)
    bands = 16
    fb = np.linspace(1e-4, bands - 1, bands, dtype=f32)[None, :]
    fw = (fb * w[:, None]).astype(f32)
    z = np.concatenate([tl[:, None], np.cos(fw.astype(np.float64)).astype(f32),
                        -np.sin(fw.astype(np.float64)).astype(f32)], axis=-1)
    reps = Tn // L
    zfull = np.tile(z, (reps, 1))
    c["zT"] = np.ascontiguousarray(zfull.T)
    tfull = np.tile(tl, reps)
    c["negt"] = np.ascontiguousarray((-tfull).reshape(nt, 128).T)
    vm = np.zeros((128, nt), f32)
    vm[:, : L // 128] = 1.0
    c["vmask"] = vm
    max_decay = math.log(1e-2) / 0.3
    min_decay = math.log(1e-2) / 1.5
    deltas = np.linspace(min_decay, max_decay, D, dtype=f32)
    c["absdelta"] = np.ascontiguousarray(np.broadcast_to(np.abs(deltas)[None, :], (128, D)))
    a = _coarse_index(nt, lseg).astype(np.float64)
    ah = np.arange(nt, dtype=np.float64)
    K1 = np.arange(128, dtype=np.float64)
    hvalid = (np.arange(nt) < L // 128).astype(np.float64)

    def cs(x):
        xm = np.mod(x, 1.0)
        return np.cos(TWO_PI * xm).astype(f32), np.sin(TWO_PI * xm).astype(f32)

    cc, ss = cs(np.outer(a, K1) / 128.0)
    c["wselc"] = cc
    c["wsels"] = -ss
    cc, ss = cs(np.outer(ah, K1) / 128.0)
    c["hselc"] = (cc * hvalid[:, None]).astype(f32)
    c["hsels"] = (-ss * hvalid[:, None]).astype(f32)
    c["hselsn"] = (ss * hvalid[:, None]).astype(f32)
    b = np.arange(128, dtype=np.float64)
    cc, ss = cs(np.outer(K1, b) / NFFT)
    c["twr"] = cc
    c["twi"] = -ss
    c["twin"] = ss
    K2 = np.arange(65, dtype=np.float64)
    cc, ss = cs(np.outer(b, K2) / 128.0)
    c["c2"] = cc
    c["s2"] = ss
    c["ns2"] = -ss
    wk = np.zeros((65, 128), f32)
    for k2 in range(65):
        for k1 in range(128):
            k = k1 + 128 * k2
            if k == 0 or k == NFFT // 2:
                wk[k2, k1] = 1.0 / NFFT
            elif k < NFFT // 2:
                wk[k2, k1] = 2.0 / NFFT
    c["wk"] = wk
    cc, ss = cs(np.outer(K2, b) / 128.0)
    c["ic"] = cc
    c["is"] = ss
    c["nis"] = -ss
    cc, ss = cs(np.outer(b, K1) / NFFT)
    c["tir"] = cc
    c["tii"] = ss
    cc, ss = cs(np.outer(K1, a) / 128.0)
    c["iwc"] = cc
    c["iwsn"] = -ss
    tok = np.arange(Tn)
    mp = ((tok % L) != 0).astype(f32)
    mn = ((tok % L) != (L - 1)).astype(f32)
    c["mp"] = np.ascontiguousarray(mp.reshape(nt, 128).T)
    c["mn"] = np.ascontiguousarray(mn.reshape(nt, 128).T)
    c["mpR"] = np.ascontiguousarray(np.broadcast_to(mp[None, :], (128, Tn)))
    c["mnR"] = np.ascontiguousarray(np.broadcast_to(mn[None, :], (128, Tn)))
    return c


BF_CONSTS = {"rblk", "obm", "identb", "abias0", "abias1", "abias2", "c2", "s2", "ns2", "ic", "is", "nis", "iwc", "iwsn", "wselc", "wsels", "hselc", "hsels", "hselsn"}
BIG_WEIGHTS = ["a_w_in", "a_w_out", "b_w_in", "b_w_out", "c_w_in", "c_w_out", "c_w_s", "x_w_q", "x_w_kv", "x_w_o", "f_w_gu", "f_w_down"]

WEIGHT_NAMES = ["g_mix", "g_cross", "g_ffn", "g_final", "a_w_in", "a_w_out", "b_w_in", "b_conv_w", "b_conv_b",
                "b_f_w1", "b_f_b1", "b_f_w2", "b_f_b2", "b_f_w3", "b_f_b3", "b_f_wout", "b_f_freq", "b_bias_d",
                "b_w_out", "c_w_in", "c_ln_g", "c_ln_b", "c_w_s", "c_b_s", "c_w_out", "x_w_q", "x_w_kv", "x_w_o",
                "f_w_gu", "f_w_down"]


class Builder:
    def __init__(self, nt, kinds, wshapes, cshapes, debug_out=()):
        self.nt = nt
        self.Tn = nt * 128
        self.kinds = kinds
        self.debug_out = set(debug_out)
        nc = bass.Bass("TRN2", target_bir_lowering=False)
        self.nc = nc
        self.W = {}
        for name in WEIGHT_NAMES:
            self.W[name] = nc.dram_tensor(name, list(wshapes[name]), F32, kind="ExternalInput").ap()
        self.C = {}
        for name, shp in cshapes.items():
            self.C[name] = nc.dram_tensor("c_" + name, list(shp), BF16 if name in BF_CONSTS else F32, kind="ExternalInput").ap()
        self.x_in = nc.dram_tensor("x", [self.Tn, D], F32, kind="ExternalInput").ap()
        self.mem_in = nc.dram_tensor("mem", [4, MEM, D], F32, kind="ExternalInput").ap()
        self.y_out = nc.dram_tensor("y", [self.Tn, D], F32, kind="ExternalOutput").ap()
        self.scr = {}
        self.scr_res = {}

    def scratch(self, name, shape, dt=F32):
        if name not in self.scr:
            kind = "ExternalOutput" if name in self.debug_out else "Internal"
            self.scr[name] = self.nc.dram_tensor("scr_" + name, list(shape), dt, kind=kind).ap()
            self.scr_res[name] = Res(name)
        return self.scr[name], self.scr_res[name]

    def dump(self, name, tile, shape):
        if name not in self.debug_out or name in self.scr:
            return
        d, dr = self.scratch(name, shape)
        self.em.dma(d, tile[:], r=[tile], w=[dr])

    def defer(self, fn):
        self.deferred.append(fn)

    def run_deferred(self):
        d, self.deferred = self.deferred, []
        for fn in d:
            fn()

    def reset(self):
        self.run_deferred()
        self.em.barrier()
        self.off = 0
        self.off_mm = 0
        self.psi = 0

    def alloc(self, shape, name="", mm=False):
        n = int(np.prod(shape[1:]))
        if mm:
            assert self.off_mm + n <= self.mm_words, ("mm arena overflow", name, self.off_mm, n)
            ap = self.arena_mm[0:shape[0], self.off_mm:self.off_mm + n]
            self.off_mm += n
        else:
            assert self.off + n <= self.arena_words, ("arena overflow", name, self.off, n)
            ap = self.arena[0:shape[0], self.off:self.off + n]
            self.off += n
        if len(shape) == 3:
            ap = ap.rearrange("p (a b) -> p a b", a=shape[1])
        elif len(shape) == 4:
            ap = ap.rearrange("p (a b c) -> p a b c", a=shape[1], b=shape[2])
        return T(ap, name, mm)

    def rot(self, shape, n, name="", mm=False):
        return Rot([self.alloc(shape, "%s%d" % (name, i), mm) for i in range(n)])

    def psum(self, n):
        assert self.psi + n <= 8
        r = Rot(self.pbanks[self.psi:self.psi + n])
        self.psi += n
        return r

    def build(self):
        nc = self.nc
        with contextlib.ExitStack() as st:
            self.em = Emitter(nc, st)
            em = self.em
            self.arena_words = 16000
            self.mm_words = 74000
            self.arena = st.enter_context(nc.sbuf_tensor("arena", [128, self.arena_words], F32))
            self.arena_mm = st.enter_context(nc.sbuf_tensor("arena_mm", [128, self.mm_words], BF16))
            self.off_mm = 0
            self.pbanks = [T(st.enter_context(nc.psum_tensor("pb%d" % i, [128, 512], F32)), "pb%d" % i)
                           for i in range(8)]
            for pb in self.pbanks:
                pb.r.excl = True
            self.off = 0
            self.psi = 0
            self.deferred = []
            self.program()
            self.run_deferred()
            em.barrier()
            em.replay()
        return nc

    def load_const(self, name, shape=None, slow=False, mm=False):
        ap = self.C[name]
        t = self.alloc(list(ap.shape) if shape is None else shape, name, mm)
        self.em.dma(t[:], ap, w=[t], slow=slow)
        return t

    @staticmethod
    def prefetch_loop(n, load_fn, body_fn):
        nxt = load_fn(0) if n > 0 else None
        for i in range(n):
            cur = nxt
            nxt = load_fn(i + 1) if i + 1 < n else None
            body_fn(i, cur)

    def ones_tile(self):
        t = self.alloc([128, 128], "ones", mm=True)
        self.em.op("dve", lambda e: e.memset(t[:], 1.0), w=[t])
        return t

    def zero_fill(self, t, n):
        self.em.op("pool", lambda e: e.memset(t[:], 0.0), w=[t])

    def load_cols(self, vec_ap, nchunk, name):
        t = self.alloc([128, nchunk], name)
        self.em.dma(t[:], vec_ap.rearrange("(c p) -> p c", p=128), w=[t], slow=True)
        return t

    def load_rep(self, vec_ap, n, name):
        t = self.alloc([128, n], name)
        self.em.dma(t[:], vec_ap.partition_broadcast(128), w=[t], slow=True)
        return t

    def rmsnorm(self, x, xn, sq, gcols, gi, ones, pp, rstd, TB):
        em = self.em
        em.op("act", lambda e: e.activation(sq[:], x[:], AF.Square), r=[x], w=[sq])
        ps = pp.get()
        for c in range(KC):
            em.op("pe", lambda e, c=c: e.matmul(ps[:, 0:TB], ones[:], sq[:, c, :], start=(c == 0), stop=(c == KC - 1)),
                  r=[ones, sq], w=[ps])
        em.op("dve", lambda e: e.tensor_scalar(rstd[:], ps[:, 0:TB], 1.0 / D, EPS, ALU.mult, ALU.add), r=[ps], w=[rstd])
        em.op("act", lambda e: e.activation(rstd[:], rstd[:], AF.Sqrt), r=[rstd], w=[rstd])
        em.op("dve", lambda e: e.reciprocal(rstd[:], rstd[:]), r=[rstd], w=[rstd])
        for c in range(KC):
            em.op("dve", lambda e, c=c: e.scalar_tensor_tensor(xn[:, c, :], x[:, c, :], gcols[:, gi * KC + c:gi * KC + c + 1],
                                                               rstd[:], ALU.mult, ALU.mult),
                  r=[x, gcols, rstd], w=[xn], ns=(c > 0))

    def linear_fm(self, w_ap, col0, ncols, src, kch, TB, wrot, pp, epi, group=512):
        em = self.em
        wv = w_ap.rearrange("(c p) n -> p c n", p=128)
        for g0 in range(0, ncols, group):
            gn = min(group, ncols - g0)
            wt = wrot.get()
            wtv = wt[:, 0:kch * gn].rearrange("p (c n) -> p c n", c=kch)
            em.dma(wtv, wv[:, :, col0 + g0:col0 + g0 + gn], w=[wt])
            self.run_deferred()
            for m in range(gn // 128):
                ps = pp.get()
                for k in range(kch):
                    em.op("pe", lambda e, k=k, m=m, ps=ps, wtv=wtv: e.matmul(
                        ps[:, 0:TB], wtv[:, k, m * 128:(m + 1) * 128], src[:, k, :], start=(k == 0), stop=(k == kch - 1)),
                        r=[wt, src], w=[ps])
                epi((g0 // 128) + m, ps)

    def pass_cast_weights(self):
        em = self.em
        self.reset()
        self.Wb = {}
        ldr = self.rot([128, 4096], 3, "wld")
        cvr = self.rot([128, 4096], 4, "wcv", mm=True)
        n = 0

        def cast(ld, cv, sz):
            nonlocal n
            eng = ("act", "dve", "pool")[n % 3]
            n += 1
            if eng == "act":
                em.op("act", lambda e: e.copy(cv[:, 0:sz], ld[:, 0:sz]), r=[ld], w=[cv])
            else:
                em.op(eng, lambda e: e.tensor_copy(cv[:, 0:sz], ld[:, 0:sz]), r=[ld], w=[cv])

        w = self.W["c_w_s"]
        wb, wbr = self.scratch("wb_c_w_s", list(w.shape), BF16)
        self.Wb["c_w_s"] = wb
        wf = w.rearrange("l h p q -> (l h p q)").rearrange("(p f) -> p f", p=128)
        wbf = wb.rearrange("l h p q -> (l h p q)").rearrange("(p f) -> p f", p=128)
        F = wf.shape[1]
        for f0 in range(0, F, 4096):
            fn = min(4096, F - f0)
            ld, cv = ldr.get(), cvr.get()
            em.dma(ld[:, 0:fn], wf[:, f0:f0 + fn], w=[ld])
            cast(ld, cv, fn)
            em.dma(wbf[:, f0:f0 + fn], cv[:, 0:fn], r=[cv], w=[wbr])
        for oname, (src, kch, gcols, base, ncols) in TILE_SPECS.items():
            w = self.W[src]
            Lw = w.shape[0]
            ng = -(-ncols // gcols)
            wt, wtr = self.scratch("wt_" + oname, [Lw, ng, 128, kch, gcols], BF16)
            self.Wb[oname] = [TiledW(wt[l], gcols) for l in range(Lw)]
            for l in range(Lw):
                wv = w[l].rearrange("(c p) n -> p c n", p=128)
                for g in range(ng):
                    gn = min(gcols, ncols - g * gcols)
                    ld, cv = ldr.get(), cvr.get()
                    ldv = ld[:, 0:kch * gn].rearrange("p (c n) -> p c n", c=kch)
                    cvv = cv[:, 0:kch * gn].rearrange("p (c n) -> p c n", c=kch)
                    em.dma(ldv, wv[:, :, base + g * gcols:base + g * gcols + gn], w=[ld])
                    cast(ld, cv, kch * gn)
                    em.dma(wt[l, g][:, :, 0:gn], cvv, r=[cv], w=[wtr])

    def program(self):
        self.pass_cast_weights()
        self.pass_transpose_in()
        self.pass_mem()
        nA = nB = nC = 0
        for li, kind in enumerate(self.kinds):
            if kind == "a":
                self.mixer_a(li, nA)
                self.pass_post(li, "OT", self.Wb["a_w_out"][nA])
                nA += 1
            elif kind == "b":
                self.mixer_b(li, nB)
                self.pass_post(li, "GT", self.Wb["b_w_out"][nB])
                nB += 1
            elif kind == "c":
                self.pass_post(li, None, self.Wb["c_w_out"][nC], sgu=nC)
                nC += 1
            else:
                self.pass_post(li, None, None)
        self.pass_final()

    def pass_transpose_in(self):
        em = self.em
        self.reset()
        xT, xTr = self.scratch("xT", [D, self.Tn])
        ident = self.load_const("ident")
        xin = self.rot([128, D], 3, "xin")
        stg = self.rot([128, KC, 512], 2, "stg")
        pp = self.psum(8)
        xTv = xT.rearrange("(c p) t -> p c t", p=128)
        for b4 in range(0, self.nt, 4):
            nb = min(4, self.nt - b4)
            s = stg.get()
            for j in range(nb):
                i = b4 + j
                xt = xin.get()
                em.dma(xt[:], self.x_in[i * 128:(i + 1) * 128, :], w=[xt])
                pa, pb = pp.get(), pp.get()
                for c in range(KC):
                    ps = pa if c < 4 else pb
                    em.op("pe", lambda e, c=c, ps=ps, xt=xt: e.transpose(ps[:, (c % 4) * 128:(c % 4 + 1) * 128],
                                                                          xt[:, c * 128:(c + 1) * 128], ident[:]),
                          r=[xt, ident], w=[ps])
                em.op("act", lambda e, s=s, j=j, pa=pa: e.copy(s[:, 0:4, j * 128:(j + 1) * 128],
                                                               pa[:].rearrange("p (c t) -> p c t", c=4)), r=[pa], w=[s])
                em.op("dve", lambda e, s=s, j=j, pb=pb: e.tensor_copy(s[:, 4:8, j * 128:(j + 1) * 128],
                                                                      pb[:].rearrange("p (c t) -> p c t", c=4)), r=[pb], w=[s])
            em.dma(xTv[:, :, b4 * 128:(b4 + nb) * 128], s[:, :, 0:nb * 128], r=[s], w=[xTr])

    def pass_mem(self):
        em = self.em
        self.reset()
        mT, mTr = self.scratch("memT", [4, D, MEM], BF16)
        ident = self.load_const("ident")
        xin = self.rot([128, D], 3, "min")
        stg = self.rot([128, KC, 128], 2, "mstg", mm=True)
        pp = self.psum(8)
        for s4 in range(4):
            for j in range(2):
                xt = xin.get()
                em.dma(xt[:], self.mem_in[s4, j * 128:(j + 1) * 128, :], w=[xt])
                pa, pb = pp.get(), pp.get()
                s = stg.get()
                for c in range(KC):
                    ps = pa if c < 4 else pb
                    em.op("pe", lambda e, c=c, ps=ps, xt=xt: e.transpose(ps[:, (c % 4) * 128:(c % 4 + 1) * 128],
                                                                          xt[:, c * 128:(c + 1) * 128], ident[:]),
                          r=[xt, ident], w=[ps])
                em.op("act", lambda e, s=s, pa=pa: e.copy(s[:, 0:4, :], pa[:].rearrange("p (c t) -> p c t", c=4)), r=[pa], w=[s])
                em.op("dve", lambda e, s=s, pb=pb: e.tensor_copy(s[:, 4:8, :], pb[:].rearrange("p (c t) -> p c t", c=4)), r=[pb], w=[s])
                em.dma(mT[s4].rearrange("(c p) t -> p c t", p=128)[:, :, j * 128:(j + 1) * 128], s[:], r=[s], w=[mTr])

    def pass_post(self, li, mixname, wout_ap, sgu=None):
        em = self.em
        self.reset()
        TB = 512
        nblk = self.Tn // TB
        blk_per_seg = self.Tn // 4 // TB
        xT, xTr = self.scratch("xT", [D, self.Tn])
        xTv = xT.rearrange("(c p) t -> p c t", p=128)
        mT, mTr = self.scratch("memT", [4, D, MEM], BF16)
        if mixname is not None:
            mx, mxr = self.scratch(mixname, [D, self.Tn], BF16)
            mxv = mx.rearrange("(c p) t -> p c t", p=128)
        ones = self.ones_tile()
        gm = self.load_cols(self.W["g_mix"].rearrange("l d -> (l d)"), 4 * KC, "gm")
        gc = self.load_cols(self.W["g_cross"].rearrange("l d -> (l d)"), 4 * KC, "gc")
        gf = self.load_cols(self.W["g_ffn"].rearrange("l d -> (l d)"), 4 * KC, "gf")
        xrot = self.rot([128, KC, TB], 2 if sgu is not None else 3, "x")
        a8 = self.rot([128, KC, TB], 3, "a8", mm=True)
        xn_t = self.alloc([128, KC, TB], "xn", mm=True)
        sq_t = self.alloc([128, KC, TB], "sq", mm=True)
        rstd = self.alloc([128, TB], "rstd")
        wrot = self.rot([128, 4096], 5, "w", mm=True)
        kt = self.alloc([128, KC, MEM], "kt", mm=True)
        vt = self.alloc([128, 2, D], "vt", mm=True)
        memt = self.alloc([128, KC, MEM], "memt", mm=True)
        prot = self.rot([128, 2, TB], 3, "pT", mm=True)
        rden = self.rot([128, TB], 2, "rden")
        hbuf = self.alloc([128, FC, TB], "h", mm=True)
        tmpr = self.rot([128, TB], 3, "tmp")
        pp = self.psum(8)
        if sgu is not None:
            j = sgu
            lngR = self.load_rep(self.W["c_ln_g"][j], D, "lng")
            lnbR = self.load_rep(self.W["c_ln_b"][j], D, "lnb")
            wsT = self.alloc([128, 8, 128], "wsT", mm=True)
            em.dma(wsT[:], self.Wb["c_w_s"][j].rearrange("h p q -> q h p"), w=[wsT], slow=True)
            bsR = self.alloc([128, 8, 128], "bsR")
            em.dma(bsR[:].rearrange("p h q -> p (h q)"),
                   self.W["c_b_s"][j].rearrange("h p -> (h p)").partition_broadcast(128), w=[bsR], slow=True)
            zv = self.rot([128, D], 1, "zv")
            zvn = self.rot([128, D], 1, "zvn", mm=True)
            st6 = self.rot([128, 8], 2, "st6")
        wq = self.Wb["x_w_q"][li]
        wkv = self.Wb["x_w_kv"][li]
        wo = self.Wb["x_w_o"][li]
        wg_t = self.Wb["f_w_g"][li]
        wu_t = self.Wb["f_w_u"][li]
        wdn = self.Wb["f_w_down"][li]
        mtr = self.rot([128, KC, TB], 2, "mt", mm=True) if mixname is not None else None

        def post_load(b):
            t0 = b * TB
            x = xrot.get()
            em.dma(x[:], xTv[:, :, t0:t0 + TB], r=[xTr], w=[x])
            mt = None
            if mixname is not None:
                mt = mtr.get()
                em.dma(mt[:], mxv[:, :, t0:t0 + TB], r=[mxr], w=[mt])
            return x, mt

        nxt = post_load(0)
        for b in range(nblk):
            t0 = b * TB
            x, mt = nxt
            if sgu is None:
                nxt = post_load(b + 1) if b + 1 < nblk else None

            def resid(m, ps, x=x):
                em.op("dve", lambda e: e.tensor_tensor(x[:, m, :], x[:, m, :], ps[:, 0:TB], ALU.add), r=[x, ps], w=[x], ns=True)

            if mixname is not None:
                self.linear_fm(wout_ap, 0, D, mt, KC, TB, wrot, pp, resid)
            elif sgu is not None:
                j = sgu
                self.rmsnorm(x, xn_t, sq_t, gm, li, ones, pp, rstd, TB)
                zu = a8.get()

                def epi_zu(m, ps, zu=zu):
                    em.op("act", lambda e: e.activation(zu[:, m, :], ps[:, 0:TB], AF.Gelu), r=[ps], w=[zu])
                self.linear_fm(self.Wb["c_w_in"][j], 0, D, xn_t, KC, TB, wrot, pp, epi_zu)
                nxt = post_load(b + 1) if b + 1 < nblk else None
                gate = a8.get()
                wv = self.Wb["c_w_in"][j].rearrange("(c p) n -> p c n", p=128)
                wts = []
                for h2 in range(2):
                    wt = wrot.get()
                    wtv = wt[:].rearrange("p (c n) -> p c n", c=KC)
                    em.dma(wtv, wv[:, :, D + h2 * 512:D + (h2 + 1) * 512], w=[wt])
                    wts.append((wt, wtv))
                for tl in range(TB // 128):
                    z = zv.get()
                    for h2 in range(2):
                        wt, wtv = wts[h2]
                        ps = pp.get()
                        for k in range(KC):
                            em.op("pe", lambda e, k=k, ps=ps, wtv=wtv, tl=tl: e.matmul(
                                ps[:], xn_t[:, k, tl * 128:(tl + 1) * 128], wtv[:, k, :], start=(k == 0), stop=(k == KC - 1)),
                                r=[xn_t, wt], w=[ps])
                        em.op("act", lambda e, ps=ps, z=z, h2=h2: e.activation(z[:, h2 * 512:(h2 + 1) * 512], ps[:], AF.Gelu),
                              r=[ps], w=[z])
                    s6 = st6.get()
                    zn = zvn.get()
                    em.op("dve", lambda e, z=z, s6=s6: e.reduce_sum(s6[:, 0:1], z[:], axis=AX.X), r=[z], w=[s6])
                    em.op("dve", lambda e, s6=s6: e.tensor_scalar(s6[:, 1:2], s6[:, 0:1], -1.0 / D, None, ALU.mult), r=[s6], w=[s6])
                    em.op("dve", lambda e, z=z, s6=s6: e.tensor_scalar(z[:], z[:], s6[:, 1:2], None, ALU.add), r=[z, s6], w=[z])
                    em.op("act", lambda e, z=z, zn=zn, s6=s6: e.activation(zn[:], z[:], AF.Square, accum_out=s6[:, 2:3]),
                          r=[z], w=[zn, s6])
                    em.op("dve", lambda e, s6=s6: e.tensor_scalar(s6[:, 3:4], s6[:, 2:3], 1.0 / D, EPS, ALU.mult, ALU.add), r=[s6], w=[s6])
                    em.op("act", lambda e, s6=s6: e.activation(s6[:, 4:5], s6[:, 3:4], AF.Sqrt), r=[s6], w=[s6])
                    em.op("dve", lambda e, s6=s6: e.reciprocal(s6[:, 5:6], s6[:, 4:5]), r=[s6], w=[s6])
                    em.op("dve", lambda e, z=z, zn=zn, s6=s6: e.scalar_tensor_tensor(zn[:], z[:], s6[:, 5:6], lngR[:], ALU.mult, ALU.mult),
                          r=[z, s6, lngR], w=[zn])
                    em.op("pool", lambda e, zn=zn: e.tensor_tensor(zn[:], zn[:], lnbR[:], ALU.add), r=[zn, lnbR], w=[zn])
                    pa, pb = pp.get(), pp.get()
                    for h in range(8):
                        ps = pa if h < 4 else pb
                        em.op("pe", lambda e, h=h, ps=ps, zn=zn: e.matmul(ps[:, (h % 4) * 128:(h % 4 + 1) * 128],
                                                                         zn[:, h * 128:(h + 1) * 128], wsT[:, h, :], start=True, stop=True),
                              r=[zn, wsT], w=[ps])
                    for half, ps in ((0, pa), (1, pb)):
                        for hh in range(4):
                            h = half * 4 + hh
                            tt = tmpr.get()
                            em.op("dve", lambda e, ps=ps, tt=tt, h=h, hh=hh: e.tensor_tensor(
                                tt[:, 0:128], ps[:, hh * 128:(hh + 1) * 128], bsR[:, h, :], ALU.add), r=[ps, bsR], w=[tt])
                            em.op("pool", lambda e, tt=tt, h=h, gate=gate, zu=zu, tl=tl: e.tensor_tensor(
                                gate[:, h, tl * 128:(tl + 1) * 128], zu[:, h, tl * 128:(tl + 1) * 128], tt[:, 0:128], ALU.mult),
                                r=[tt, zu], w=[gate])
                self.linear_fm(wout_ap, 0, D, gate, KC, TB, wrot, pp, resid)

            if not DBG.get('skip_cross'):
                seg = b // blk_per_seg
                if b % blk_per_seg == 0:
                    em.dma(memt[:], mT[seg].rearrange("(c p) t -> p c t", p=128), r=[mTr], w=[memt])

                    def epi_k(m, ps):
                        em.op("act", lambda e: e.copy(kt[:, m, :], ps[:, 0:MEM]), r=[ps], w=[kt])
                    self.linear_fm(wkv, 0, D, memt, KC, MEM, wrot, pp, epi_k)
                    wv = wkv.rearrange("(c p) n -> p c n", p=128)
                    for h2 in range(2):
                        wt = wrot.get()
                        wtv = wt[:].rearrange("p (c n) -> p c n", c=KC)
                        em.dma(wtv, wv[:, :, D + h2 * 512:D + (h2 + 1) * 512], w=[wt])
                        for mc in range(2):
                            ps = pp.get()
                            for k in range(KC):
                                em.op("pe", lambda e, k=k, ps=ps, wtv=wtv, mc=mc: e.matmul(
                                    ps[:], memt[:, k, mc * 128:(mc + 1) * 128], wtv[:, k, :], start=(k == 0), stop=(k == KC - 1)),
                                    r=[memt, wt], w=[ps])
                            em.op("act", lambda e, ps=ps, mc=mc, h2=h2: e.copy(vt[:, mc, h2 * 512:(h2 + 1) * 512], ps[:]), r=[ps], w=[vt])
                self.dump('d_kt', kt, [128, KC, MEM])
                self.dump('d_vt', vt, [128, 2, D])
                self.rmsnorm(x, xn_t, sq_t, gc, li, ones, pp, rstd, TB)
                self.dump('d_xn', xn_t, [128, KC, TB])
                q = a8.get()

                def epi_q(m, ps, q=q):
                    em.op("act", lambda e: e.copy(q[:, m, :], ps[:, 0:TB]), r=[ps], w=[q])
                self.linear_fm(wq, 0, D, xn_t, KC, TB, wrot, pp, epi_q)
                o = a8.get()

                def att_a(h, q=q):
                    pT = prot.get()
                    for mc in range(2):
                        ps = pp.get()
                        for dc in range(2):
                            em.op("pe", lambda e: e.matmul(
                                ps[:, 0:TB], kt[:, 2 * h + dc, mc * 128:(mc + 1) * 128], q[:, 2 * h + dc, :], start=(dc == 0), stop=(dc == 1)),
                                r=[kt, q], w=[ps])
                        em.op("act", lambda e: e.activation(pT[:, mc, :], ps[:, 0:TB], AF.Exp, scale=1.0 / 16.0),
                              r=[ps], w=[pT])
                    return pT

                def att_b(h, pT, o=o):
                    ps = pp.get()
                    for mc in range(2):
                        em.op("pe", lambda e: e.matmul(ps[:, 0:TB], ones[:], pT[:, mc, :], start=(mc == 0), stop=(mc == 1)),
                              r=[ones, pT], w=[ps])
                    rd = rden.get()
                    em.op("dve", lambda e: e.reciprocal(rd[:], ps[:, 0:TB]), r=[ps], w=[rd])
                    for dc in range(2):
                        ps2 = pp.get()
                        for mc in range(2):
                            em.op("pe", lambda e: e.matmul(
                                ps2[:, 0:TB], vt[:, mc, (2 * h + dc) * 128:(2 * h + dc + 1) * 128], pT[:, mc, :], start=(mc == 0), stop=(mc == 1)),
                                r=[vt, pT], w=[ps2])
                        em.op("dve", lambda e: e.tensor_tensor(o[:, 2 * h + dc, :], ps2[:, 0:TB], rd[:], ALU.mult),
                              r=[ps2, rd], w=[o], ns=(dc > 0))

                pTs = att_a(0)
                for h in range(4):
                    pTn = att_a(h + 1) if h < 3 else None
                    att_b(h, pTs)
                    pTs = pTn
                self.dump('d_q', q, [128, KC, TB])
                self.dump('d_o', o, [128, KC, TB])
                self.linear_fm(wo, 0, D, o, KC, TB, wrot, pp, resid)

            if not DBG.get('skip_ffn'):
                self.rmsnorm(x, xn_t, sq_t, gf, li, ones, pp, rstd, TB)
                for g0 in range(0, FC, 4):
                    gn = min(4, FC - g0)
                    wg = wrot.get()
                    wgv = wg[:, 0:KC * gn * 128].rearrange("p (c n) -> p c n", c=KC)
                    em.dma(wgv, wg_t[:, :, g0 * 128:(g0 + gn) * 128], w=[wg])
                    wu = wrot.get()
                    wuv = wu[:, 0:KC * gn * 128].rearrange("p (c n) -> p c n", c=KC)
                    em.dma(wuv, wu_t[:, :, g0 * 128:(g0 + gn) * 128], w=[wu])
                    for m in range(gn):
                        pg, pu = pp.get(), pp.get()
                        for k in range(KC):
                            em.op("pe", lambda e, k=k, m=m, pg=pg, wgv=wgv: e.matmul(
                                pg[:, 0:TB], wgv[:, k, m * 128:(m + 1) * 128], xn_t[:, k, :], start=(k == 0), stop=(k == KC - 1)),
                                r=[wg, xn_t], w=[pg])
                        for k in range(KC):
                            em.op("pe", lambda e, k=k, m=m, pu=pu, wuv=wuv: e.matmul(
                                pu[:, 0:TB], wuv[:, k, m * 128:(m + 1) * 128], xn_t[:, k, :], start=(k == 0), stop=(k == KC - 1)),
                                r=[wu, xn_t], w=[pu])
                        tt = tmpr.get()
                        em.op("act", lambda e, pg=pg, tt=tt: e.activation(tt[:], pg[:, 0:TB], AF.Silu), r=[pg], w=[tt])
                        em.op("dve", lambda e, pu=pu, tt=tt, mm=g0 + m: e.tensor_tensor(hbuf[:, mm, :], tt[:], pu[:, 0:TB], ALU.mult),
                              r=[tt, pu], w=[hbuf], ns=True)
                self.linear_fm(wdn, 0, D, hbuf, FC, TB, wrot, pp, resid, group=128)
            self.defer(lambda x=x, t0=t0: em.dma(xTv[:, :, t0:t0 + TB], x[:], r=[x], w=[xTr]))

    def pass_final(self):
        em = self.em
        self.reset()
        TB = 256
        xT, xTr = self.scratch("xT", [D, self.Tn])
        xTv = xT.rearrange("(c p) t -> p c t", p=128)
        ident = self.load_const("ident")
        ones = self.ones_tile()
        gfin = self.load_cols(self.W["g_final"], KC, "gfin")
        xrot = self.rot([128, KC, TB], 2, "x")
        xnr = self.rot([128, KC, TB], 2, "xnf")
        sq_t = self.alloc([128, KC, TB], "sq", mm=True)
        rstd = self.alloc([128, TB], "rstd")
        yo = self.rot([128, D], 3, "yo")
        pp = self.psum(8)
        for b in range(self.Tn // TB):
            t0 = b * TB
            x = xrot.get()
            em.dma(x[:], xTv[:, :, t0:t0 + TB], r=[xTr], w=[x])
            xn = xnr.get()
            self.rmsnorm(x, xn, sq_t, gfin, 0, ones, pp, rstd, TB)
            for j in range(TB // 128):
                pa, pb = pp.get(), pp.get()
                y = yo.get()
                for c in range(KC):
                    ps = pa if c < 4 else pb
                    em.op("pe", lambda e, c=c, ps=ps, xn=xn, j=j: e.transpose(ps[:, (c % 4) * 128:(c % 4 + 1) * 128],
                                                                             xn[:, c, j * 128:(j + 1) * 128], ident[:]),
                          r=[xn, ident], w=[ps])
                em.op("act", lambda e, y=y, pa=pa: e.copy(y[:, 0:512], pa[:]), r=[pa], w=[y])
                em.op("dve", lambda e, y=y, pb=pb: e.tensor_copy(y[:, 512:1024], pb[:]), r=[pb], w=[y])
                yr = Res("y")
                em.dma(self.y_out[t0 + j * 128:t0 + (j + 1) * 128, :], y[:], r=[y], w=[yr])

    def mixer_a(self, li, j):
        self.pass_a1(li, j)
        self.pass_a2(li, j)

    def pass_a1(self, li, j):
        em = self.em
        self.reset()
        TB = 512
        Tn = self.Tn
        xT, xTr = self.scratch("xT", [D, Tn])
        xTv = xT.rearrange("(c p) t -> p c t", p=128)
        w_in = self.Wb["a_w_in"][j]
        pads = [64 * d for (_, d) in A_GROUPS]
        QT, KT, VV = [], [], []
        for g in range(3):
            QT.append(self.scratch("QT%d" % g, [D, Tn + 2 * pads[g]], BF16))
            KT.append(self.scratch("KT%d" % g, [D, Tn + 2 * pads[g]], BF16))
            VV.append(self.scratch("VV%d" % g, [Tn + 2 * pads[g], D], BF16))
        ones = self.ones_tile()
        rblk = self.load_const("rblk", mm=True)
        gm = self.load_cols(self.W["g_mix"].rearrange("l d -> (l d)"), 4 * KC, "gm")
        zt = self.alloc([128, 1024], "zero", mm=True)
        em.op("pool", lambda e: e.memset(zt[:], 0.0), w=[zt])
        for g in range(3):
            pad = pads[g]
            for side in range(2):
                c0 = 0 if side == 0 else pad + Tn
                for c in range(KC):
                    em.dma(KT[g][0][c * 128:(c + 1) * 128, c0:c0 + pad], zt[:, 0:pad], r=[zt], w=[KT[g][1]])
                for r0 in range(0, pad, 128):
                    rn = min(128, pad - r0)
                    em.dma(VV[g][0][c0 + r0:c0 + r0 + rn, :], zt[0:rn, :], r=[zt], w=[VV[g][1]])
        xrot = self.rot([128, KC, TB], 2, "x")
        xn_t = self.alloc([128, KC, TB], "xn", mm=True)
        sq_t = self.alloc([128, KC, TB], "sq", mm=True)
        rstd = self.alloc([128, TB], "rstd")
        wrot = self.rot([128, 4096], 9, "w", mm=True)
        cbr = self.rot([128, TB], 2, "cb")
        sbr = self.rot([128, TB], 2, "sb")
        qraw = self.rot([128, TB], 3, "qraw", mm=True)
        t1r = self.rot([128, TB], 3, "t1")
        t2r = self.rot([128, TB], 3, "t2")
        stg = self.rot([128, KC, TB], 3, "stg", mm=True)
        vst = self.rot([128, D], 6, "vst", mm=True)
        pp = self.psum(8)
        for b in range(Tn // TB):
            t0 = b * TB
            x = xrot.get()
            em.dma(x[:], xTv[:, :, t0:t0 + TB], r=[xTr], w=[x])
            self.rmsnorm(x, xn_t, sq_t, gm, li, ones, pp, rstd, TB)
            cb, sb = cbr.get(), sbr.get()
            em.dma(cb[:], self.C["ropec"][:, t0:t0 + TB], w=[cb])
            em.dma(sb[:], self.C["ropes"][:, t0:t0 + TB], w=[sb])
            for g in range(3):
                pad = pads[g]
                for jj in range(2):
                    st_ = stg.get()

                    pend = []

                    def rope(m, qr, st_=st_, cb=cb, sb=sb):
                        p2 = pp.get()
                        em.op("pe", lambda e: e.matmul(p2[:, 0:TB], rblk[:], qr[:], start=True, stop=True), r=[rblk, qr], w=[p2])
                        t1, t2 = t1r.get(), t2r.get()
                        em.op("pool", lambda e: e.tensor_tensor(t1[:], qr[:], cb[:], ALU.mult), r=[qr, cb], w=[t1])
                        em.op("dve", lambda e: e.tensor_tensor(t2[:], p2[:, 0:TB], sb[:], ALU.mult), r=[p2, sb], w=[t2])
                        em.op("pool", lambda e: e.tensor_tensor(st_[:, m, :], t1[:], t2[:], ALU.add), r=[t1, t2], w=[st_])

                    def epi(m, ps, pend=pend, rope=rope):
                        qr = qraw.get()
                        em.op("act", lambda e: e.copy(qr[:], ps[:, 0:TB]), r=[ps], w=[qr])
                        if pend:
                            pend.pop()()
                        pend.append(lambda: rope(m, qr))
                    self.linear_fm(w_in, g * 3072 + jj * 1024, 1024, xn_t, KC, TB, wrot, pp, epi)
                    while pend:
                        pend.pop()()
                    dst, dres = (QT[g] if jj == 0 else KT[g])
                    self.defer(lambda dst=dst, dres=dres, st_=st_, pad=pad, t0=t0: em.dma(
                        dst.rearrange("(c p) t -> p c t", p=128)[:, :, pad + t0:pad + t0 + TB], st_[:], r=[st_], w=[dres]))
                wv = w_in.rearrange("(c p) n -> p c n", p=128)
                wts = []
                for h2 in range(2):
                    wt = wrot.get()
                    wtv = wt[:].rearrange("p (c n) -> p c n", c=KC)
                    c0 = g * 3072 + 2048 + h2 * 512
                    em.dma(wtv, wv[:, :, c0:c0 + 512], w=[wt])
                    wts.append((wt, wtv))
                for tl in range(TB // 128):
                    vs = vst.get()
                    for h2 in range(2):
                        wt, wtv = wts[h2]
                        ps = pp.get()
                        for k in range(KC):
                            em.op("pe", lambda e, k=k: e.matmul(ps[:], xn_t[:, k, tl * 128:(tl + 1) * 128], wtv[:, k, :],
                                                                start=(k == 0), stop=(k == KC - 1)), r=[xn_t, wt], w=[ps])
                        em.op("act", lambda e: e.copy(vs[:, h2 * 512:(h2 + 1) * 512], ps[:]), r=[ps], w=[vs])
                    r0 = pad + t0 + tl * 128
                    self.defer(lambda g=g, r0=r0, vs=vs: em.dma(VV[g][0][r0:r0 + 128, :], vs[:], r=[vs], w=[VV[g][1]]))

    def pass_a2(self, li, j):
        em = self.em
        self.reset()
        Tn = self.Tn
        RG = 2048
        pads = [64 * d for (_, d) in A_GROUPS]
        QT, KT, VV = [], [], []
        for g in range(3):
            QT.append(self.scratch("QT%d" % g, [D, Tn + 2 * pads[g]], BF16))
            KT.append(self.scratch("KT%d" % g, [D, Tn + 2 * pads[g]], BF16))
            VV.append(self.scratch("VV%d" % g, [Tn + 2 * pads[g], D], BF16))
        OT, OTr = self.scratch("OT", [D, Tn], BF16)
        Ur = self.rot([128, RG], 2, "U")
        Lr = self.rot([128, RG], 2, "L")
        rl = self.alloc([128, RG], "rl")
        otr = self.rot([128, RG], 2, "ot", mm=True)
        identb = self.load_const("identb", mm=True)
        qz0 = self.rot([128, RG], 2, "qz0", mm=True)
        qz1 = self.rot([128, RG], 2, "qz1", mm=True)
        ktr = self.rot([128, 2 * RG], 2, "kt", mm=True)
        vzr = self.rot([128, 2, 128], 8, "vz", mm=True)
        mkr = self.rot([128, 2, 128], 3, "mk", mm=True)
        pMr = self.rot([128, 4, 128], 6, "pM", mm=True)
        ob = [self.alloc([128, 128], "ob%d" % e, mm=True) for e in range(2)]
        for e_ in range(2):
            em.dma(ob[e_][:], self.C["obm"][e_], w=[ob[e_]])
        for tl in qz0.tiles + qz1.tiles:
            self.zero_fill(tl, RG)
        pss = self.psum(3)
        psu = self.psum(3)
        psl = self.psum(2)
        nv = 0
        for R in range(Tn // RG):
            for hc in range(KC):
                U, L, ot = Ur.get(), Lr.get(), otr.get()
                em.op("pool", lambda e: e.memset(U[:], 0.0), w=[U])
                em.op("pool", lambda e: e.memset(L[:], 0.0), w=[L])
                pend = []
                for g, (window, d) in enumerate(A_GROUPS):
                    pad = pads[g]
                    span = 128 * d
                    ntr = RG // span
                    for tt in range(ntr):
                        t = R * ntr + tt
                        base = t * span
                        q0, q1, kt = qz0.get(), qz1.get(), ktr.get()
                        em.dma(q0[0:64, 0:span], QT[g][0][hc * 128:hc * 128 + 64, pad + base:pad + base + span], r=[QT[g][1]], w=[q0])
                        em.dma(q1[64:128, 0:span], QT[g][0][hc * 128 + 64:hc * 128 + 128, pad + base:pad + base + span], r=[QT[g][1]], w=[q1])
                        em.dma(kt[:, 0:2 * span], KT[g][0][hc * 128:(hc + 1) * 128, pad + base - 64 * d:pad + base + 192 * d],
                               r=[KT[g][1]], w=[kt])
                        mk = mkr.get()
                        em.dma(mk[:], self.C["abias%d" % g][t].rearrange("x k q -> k x q"), w=[mk])
                        qz = (q0, q1)
                        for r in range(d):
                            vz = vzr.get()
                            rs = pad + base - 64 * d + r
                            src = VV[g][0][rs:rs + 255 * d + 1:d, hc * 128:(hc + 1) * 128].rearrange("(x k) c -> k x c", x=2)
                            em.dma(vz[:], src, r=[VV[g][1]], w=[vz], eng=("sp", "act", "pool")[nv % 3])
                            nv += 1
                            ps = pss.get()
                            for e_ in range(2):
                                for X in range(2):
                                    k0 = 128 * d * X + r
                                    em.op("pe", lambda e: e.matmul(ps[:, (e_ * 2 + X) * 128:(e_ * 2 + X + 1) * 128],
                                                                   kt[:, k0:k0 + 127 * d + 1:d], qz[e_][:, r:r + 127 * d + 1:d], start=True, stop=False),
                                          r=[kt, qz[e_]], w=[ps])
                                    em.op("pe", lambda e: e.matmul(ps[:, (e_ * 2 + X) * 128:(e_ * 2 + X + 1) * 128],
                                                                   identb[:], mk[:, X, :], start=False, stop=True),
                                          r=[identb, mk], w=[ps])
                            pM = pMr.get()
                            em.op("act", lambda e: e.activation(pM[:].rearrange("p a b -> p (a b)"), ps[:], AF.Exp, scale=0.125),
                                  r=[ps], w=[pM])
                            off = tt * span + r
                            pend.append((vz, pM, ob, psu, psl, U, L, off, d))
                            if len(pend) > 2:
                                self._a2_pv(*pend.pop(0))
                while pend:
                    self._a2_pv(*pend.pop(0))
                em.op("dve", lambda e: e.reciprocal(rl[:], L[:]), r=[L], w=[rl])
                em.op("pool", lambda e: e.tensor_tensor(ot[:], U[:], rl[:], ALU.mult), r=[U, rl], w=[ot])
                em.dma(OT[hc * 128:(hc + 1) * 128, R * RG:(R + 1) * RG], ot[:], r=[ot], w=[OTr])

    def _a2_pv(self, vz, pM, ob, psu, psl, U, L, off, d):
        em = self.em
        pus = []
        for e_ in range(2):
            pu = psu.get()
            for X in range(2):
                em.op("pe", lambda e: e.matmul(pu[:, 0:128], vz[:, X, :], pM[:, e_ * 2 + X, :],
                                               start=(X == 0), stop=(X == 1)), r=[vz, pM], w=[pu])
            pus.append(pu)
        pl = psl.get()
        n = 0
        for e_ in range(2):
            for X in range(2):
                em.op("pe", lambda e: e.matmul(pl[:, 0:128], ob[e_][:], pM[:, e_ * 2 + X, :],
                                               start=(n == 0), stop=(n == 3)), r=[ob[e_], pM], w=[pl])
                n += 1
        for e_ in range(2):
            rows = slice(e_ * 64, (e_ + 1) * 64)
            em.op("dve", lambda e: e.tensor_tensor(U[rows, off:off + 127 * d + 1:d], U[rows, off:off + 127 * d + 1:d],
                                                   pus[e_][rows, 0:128], ALU.add), r=[U, pus[e_]], w=[U], ns=True)
        em.op("dve", lambda e: e.tensor_tensor(L[:, off:off + 127 * d + 1:d], L[:, off:off + 127 * d + 1:d],
                                               pl[:, 0:128], ALU.add), r=[L, pl], w=[L], ns=True)

    def mixer_b(self, li, j):
        stop = DBG.get("b_stop", 99)
        AD = [self.scratch("AD%d" % i, [128, 2, 128, D], BF16) for i in range(3)]
        VD = self.scratch("VD", [self.Tn, D], BF16)
        HD = self.scratch("HD", [self.Tn, 2, D], BF16)
        hv = HD[0].rearrange("(i b) r c -> b r i c", b=128)
        steps = [
            lambda: self.pass_b0(j),
            lambda: self.pass_b1(li, j),
            lambda: self.pass_b2(j),
            lambda: self.pass_f1(VD[0].rearrange("(i b) c -> b i c", b=128), VD[1], "wselc", "wsels", "twi", AD[0]),
            lambda: self.pass_f1(hv[:, 0], HD[1], "hselc", "hsels", "twi", AD[1]),
            lambda: self.pass_f1(hv[:, 1], HD[1], "hselc", "hselsn", "twin", AD[2]),
            lambda: self.pass_f2h(j),
            lambda: self.pass_mid(),
            lambda: self.pass_i2(j),
        ]
        for i, f in enumerate(steps):
            if i < stop:
                f()

    def sin_layer(self, ps, hid, freq, fb, tmpr, n):
        em = self.em
        a, c = tmpr.get(), tmpr.get()
        em.op("dve", lambda e: e.tensor_scalar(a[0:64, 0:n], ps[0:64, 0:n], freq[0:64, 0:1], fb[0:64, 0:1], ALU.mult, ALU.add),
              r=[ps, freq, fb], w=[a])
        ci = c[0:64, 0:n].bitcast(mybir.dt.int32)
        em.op("dve", lambda e: e.tensor_copy(ci, a[0:64, 0:n]), r=[a], w=[c])
        em.op("dve", lambda e: e.tensor_copy(c[0:64, 0:n], ci), r=[c], w=[c])
        em.op("dve", lambda e: e.tensor_tensor(a[0:64, 0:n], a[0:64, 0:n], c[0:64, 0:n], ALU.subtract), r=[a, c], w=[a])
        em.op("act", lambda e: e.activation(hid[0:64, 0:n], a[0:64, 0:n], AF.Sin, scale=6.283179), r=[a], w=[hid])

    def pass_b0(self, j):
        em = self.em
        self.reset()
        Tn, nt = self.Tn, self.nt
        TBf = 512
        HD, HDr = self.scratch("HD", [Tn, 2, D], BF16)
        RN, RNr = self.scratch("RN", [128, D])
        W = self.W
        ones = self.alloc([128, 128], "onesf")
        em.op("dve", lambda e: e.memset(ones[:], 1.0), w=[ones])
        w1 = self.alloc([128, 64], "w1")
        em.dma(w1[0:33, :], W["b_f_w1"][j], w=[w1])
        w2 = self.alloc([128, 64], "w2")
        em.dma(w2[0:64, :], W["b_f_w2"][j], w=[w2])
        w3 = self.alloc([128, 64], "w3")
        em.dma(w3[0:64, :], W["b_f_w3"][j], w=[w3])
        wout = self.alloc([128, 2 * D], "wout")
        em.dma(wout[0:64, :], W["b_f_wout"][j], w=[wout])
        freq = self.alloc([128, 1], "freq")
        em.dma(freq[0:64, :], W["b_f_freq"][j].rearrange("(p o) -> p o", o=1), w=[freq], slow=True)
        fbs = []
        for nm in ("b_f_b1", "b_f_b2", "b_f_b3"):
            fb = self.alloc([128, 1], nm)
            em.dma(fb[0:64, :], W[nm][j].rearrange("(p o) -> p o", o=1), w=[fb], slow=True)
            em.op("dve", lambda e: e.tensor_tensor(fb[0:64, :], fb[0:64, :], freq[0:64, :], ALU.mult), r=[fb, freq], w=[fb])
            em.op("dve", lambda e: e.tensor_scalar(fb[0:64, :], fb[0:64, :], 1.0 / TWO_PI, None, ALU.mult), r=[fb], w=[fb])
            fbs.append(fb)
        em.op("dve", lambda e: e.tensor_scalar(freq[0:64, :], freq[0:64, :], 1.0 / TWO_PI, None, ALU.mult), r=[freq] + fbs, w=[freq])
        absd = self.load_const("absdelta")
        negt = self.load_const("negt")
        vmask = self.load_const("vmask")
        zr = self.rot([128, TBf], 2, "z")
        hidr = self.rot([128, TBf], 3, "hidf")
        tmpr = self.rot([128, TBf], 3, "tmp")
        decr = self.rot([128, D], 1, "dec")
        hr = self.rot([128, 2, D], 2, "hflt")
        abr = self.rot([128, 2, D], 1, "abf")
        hbr = self.rot([128, 2, D], 2, "hb", mm=True)
        nps = self.psum(2).tiles
        pp = self.psum(6)
        first = True
        for b in range(Tn // TBf):
            t0 = b * TBf
            z = zr.get()
            em.dma(z[0:33, :], self.C["zT"][:, t0:t0 + TBf], w=[z])
            ps = pp.get()
            em.op("pe", lambda e: e.matmul(ps[0:64, :], w1[0:33, :], z[0:33, :], start=True, stop=True), r=[w1, z], w=[ps])
            h1 = hidr.get()
            self.sin_layer(ps, h1, freq, fbs[0], tmpr, TBf)
            ps = pp.get()
            em.op("pe", lambda e: e.matmul(ps[0:64, :], w2[0:64, :], h1[0:64, :], start=True, stop=True), r=[w2, h1], w=[ps])
            h2 = hidr.get()
            self.sin_layer(ps, h2, freq, fbs[1], tmpr, TBf)
            ps = pp.get()
            em.op("pe", lambda e: e.matmul(ps[0:64, :], w3[0:64, :], h2[0:64, :], start=True, stop=True), r=[w3, h2], w=[ps])
            h3 = hidr.get()
            self.sin_layer(ps, h3, freq, fbs[2], tmpr, TBf)
            for tl in range(TBf // 128):
                i = b * (TBf // 128) + tl
                dec = decr.get()
                em.op("act", lambda e: e.activation(dec[:], absd[:], AF.Exp, scale=negt[:, i:i + 1]), r=[absd, negt], w=[dec])
                h = hr.get()
                for cg in range(4):
                    ps = pp.get()
                    em.op("pe", lambda e: e.matmul(ps[:], h3[0:64, tl * 128:(tl + 1) * 128], wout[0:64, cg * 512:(cg + 1) * 512],
                                                   start=True, stop=True), r=[h3, wout], w=[ps])
                    em.op("dve", lambda e: e.tensor_tensor(h[:, cg // 2, (cg % 2) * 512:(cg % 2 + 1) * 512], ps[:],
                                                           dec[:, (cg % 2) * 512:(cg % 2 + 1) * 512], ALU.mult), r=[ps, dec], w=[h])
                if i == 0:
                    em.op("dve", lambda e: e.tensor_tensor(h[0:1, 0, :], h[0:1, 0, :], h[0:1, 1, :], ALU.add), r=[h], w=[h])
                    em.op("dve", lambda e: e.memset(h[0:1, 1, :], 0.0), w=[h])
                ab = abr.get()
                em.op("act", lambda e: e.activation(ab[:].rearrange("p a b -> p (a b)"), h[:].rearrange("p a b -> p (a b)"),
                                                    AF.Abs, scale=vmask[:, i:i + 1]), r=[h, vmask], w=[ab])
                last = (i == nt - 1)
                for dr in range(2):
                    for hf in range(2):
                        em.op("pe", lambda e: e.matmul(nps[hf][:], ones[:], ab[:, dr, hf * 512:(hf + 1) * 512],
                                                       start=(first and dr == 0), stop=(last and dr == 1)), r=[ones, ab], w=[nps[hf]])
                first = False
                hb = hbr.get()
                em.op("pool", lambda e: e.tensor_copy(hb[:], h[:]), r=[h], w=[hb])
                em.dma(HD[i * 128:(i + 1) * 128], hb[:], r=[hb], w=[HDr])
        rn = decr.get()
        for hf in range(2):
            em.op("dve", lambda e: e.reciprocal(rn[:, hf * 512:(hf + 1) * 512], nps[hf][:]), r=[nps[hf]], w=[rn])
        em.dma(RN, rn[:], r=[rn], w=[RNr])

    def pass_b1(self, li, j):
        em = self.em
        self.reset()
        TB = 512
        Tn = self.Tn
        xT, xTr = self.scratch("xT", [D, Tn])
        xTv = xT.rearrange("(c p) t -> p c t", p=128)
        U0, U0r = self.scratch("U0T", [D, Tn + 2])
        U12, U12r = self.scratch("U12", [Tn + 2, 2 * D], BF16)
        w_in = self.Wb["b_w_in"][j]
        ones = self.ones_tile()
        gm = self.load_cols(self.W["g_mix"].rearrange("l d -> (l d)"), 4 * KC, "gm")
        zt = self.alloc([128, 8], "zero")
        em.op("pool", lambda e: e.memset(zt[:], 0.0), w=[zt])
        ztb = self.alloc([128, 2 * D], "zerob", mm=True)
        em.op("pool", lambda e: e.memset(ztb[:], 0.0), w=[ztb])
        for c in range(KC):
            em.dma(U0[c * 128:(c + 1) * 128, 0:1], zt[:, 0:1], r=[zt], w=[U0r], slow=True)
            em.dma(U0[c * 128:(c + 1) * 128, Tn + 1:Tn + 2], zt[:, 0:1], r=[zt], w=[U0r], slow=True)
        em.dma(U12[0:1, :], ztb[0:1, :], r=[ztb], w=[U12r])
        em.dma(U12[Tn + 1:Tn + 2, :], ztb[0:1, :], r=[ztb], w=[U12r])
        xrot = self.rot([128, KC, TB], 2, "x")
        xn_t = self.alloc([128, KC, TB], "xn", mm=True)
        sq_t = self.alloc([128, KC, TB], "sq", mm=True)
        rstd = self.alloc([128, TB], "rstd")
        wrot = self.rot([128, 4096], 8, "w", mm=True)
        stg = self.rot([128, KC, TB], 1, "stg")
        ust = self.rot([128, 2 * D], 8, "ust", mm=True)
        pp = self.psum(8)
        wv = w_in.rearrange("(c p) n -> p c n", p=128)
        for b in range(Tn // TB):
            t0 = b * TB
            x = xrot.get()
            em.dma(x[:], xTv[:, :, t0:t0 + TB], r=[xTr], w=[x])
            self.rmsnorm(x, xn_t, sq_t, gm, li, ones, pp, rstd, TB)
            st_ = stg.get()

            def epi(m, ps, st_=st_):
                em.op("act", lambda e: e.copy(st_[:, m, :], ps[:, 0:TB]), r=[ps], w=[st_])
            self.linear_fm(w_in, 0, D, xn_t, KC, TB, wrot, pp, epi)
            self.defer(lambda st_=st_, t0=t0: em.dma(U0.rearrange("(c p) t -> p c t", p=128)[:, :, 1 + t0:1 + t0 + TB], st_[:], r=[st_], w=[U0r]))
            us = [ust.get() for _ in range(TB // 128)]
            for cg in range(4):
                wt = wrot.get()
                wtv = wt[:].rearrange("p (c n) -> p c n", c=KC)
                em.dma(wtv, wv[:, :, D + cg * 512:D + (cg + 1) * 512], w=[wt])
                for tl in range(TB // 128):
                    ps = pp.get()
                    for k in range(KC):
                        em.op("pe", lambda e: e.matmul(ps[:], xn_t[:, k, tl * 128:(tl + 1) * 128], wtv[:, k, :],
                                                       start=(k == 0), stop=(k == KC - 1)), r=[xn_t, wt], w=[ps])
                    if (cg + tl) % 2 == 0:
                        em.op("act", lambda e: e.copy(us[tl][:, cg * 512:(cg + 1) * 512], ps[:]), r=[ps], w=[us[tl]])
                    else:
                        em.op("dve", lambda e: e.tensor_copy(us[tl][:, cg * 512:(cg + 1) * 512], ps[:]), r=[ps], w=[us[tl]])
            for tl in range(TB // 128):
                r0 = 1 + t0 + tl * 128
                self.defer(lambda r0=r0, u=us[tl]: em.dma(U12[r0:r0 + 128, :], u[:], r=[u], w=[U12r]))

    def pass_b2(self, j):
        em = self.em
        self.reset()
        Tn, nt = self.Tn, self.nt
        U12, U12r = self.scratch("U12", [Tn + 2, 2 * D], BF16)
        VD, VDr = self.scratch("VD", [Tn, D], BF16)
        cw = self.W["b_conv_w"][j]
        w0R = self.load_rep(cw[0, D:3 * D], 2 * D, "w0R")
        w1R = self.load_rep(cw[1, D:3 * D], 2 * D, "w1R")
        w2R = self.load_rep(cw[2, D:3 * D], 2 * D, "w2R")
        bR = self.load_rep(self.W["b_conv_b"][j, D:3 * D], 2 * D, "bR")
        mp = self.load_const("mp")
        mn = self.load_const("mn")
        ucr = self.rot([128, 2 * D], 3, "uc", mm=True)
        upr = self.rot([128, 2 * D], 3, "up", mm=True)
        unr = self.rot([128, 2 * D], 3, "un", mm=True)
        cr = self.rot([128, 2 * D], 2, "c", mm=True)
        tr = self.rot([128, 2 * D], 2, "t", mm=True)
        vr = self.rot([128, D], 2, "v", mm=True)
        def b2_load(i):
            uc, up, un = ucr.get(), upr.get(), unr.get()
            em.dma(uc[:], U12[1 + i * 128:1 + (i + 1) * 128, :], r=[U12r], w=[uc])
            em.dma(up[:], U12[i * 128:(i + 1) * 128, :], r=[U12r], w=[up])
            em.dma(un[:], U12[2 + i * 128:2 + (i + 1) * 128, :], r=[U12r], w=[un])
            return uc, up, un

        def b2_body(i, tl):
            uc, up, un = tl
            c = cr.get()
            em.op("dve", lambda e: e.tensor_tensor(c[:], uc[:], w1R[:], ALU.mult), r=[uc, w1R], w=[c])
            em.op("pool", lambda e: e.tensor_tensor(c[:], c[:], bR[:], ALU.add), r=[c, bR], w=[c])
            t1 = tr.get()
            em.op("pool", lambda e: e.tensor_tensor(t1[:], up[:], w0R[:], ALU.mult), r=[up, w0R], w=[t1])
            em.op("dve", lambda e: e.scalar_tensor_tensor(c[:], t1[:], mp[:, i:i + 1], c[:], ALU.mult, ALU.add), r=[t1, mp, c], w=[c])
            t2 = tr.get()
            em.op("pool", lambda e: e.tensor_tensor(t2[:], un[:], w2R[:], ALU.mult), r=[un, w2R], w=[t2])
            em.op("dve", lambda e: e.scalar_tensor_tensor(c[:], t2[:], mn[:, i:i + 1], c[:], ALU.mult, ALU.add), r=[t2, mn, c], w=[c])
            v = vr.get()
            em.op("pool", lambda e: e.tensor_tensor(v[:], c[:, D:2 * D], c[:, 0:D], ALU.mult), r=[c], w=[v])
            em.dma(VD[i * 128:(i + 1) * 128, :], v[:], r=[v], w=[VDr])

        self.prefetch_loop(nt, b2_load, b2_body)

    def pass_f1(self, src_b, src_res, selc_n, sels_n, twi_n, AD):
        em = self.em
        self.reset()
        nt = self.nt
        ADa, ADr = AD
        selc = self.alloc([128, 128], "selc", mm=True)
        self.zero_fill(selc, 128)
        em.dma(selc[0:nt, :], self.C[selc_n], w=[selc])
        sels = self.alloc([128, 128], "sels", mm=True)
        self.zero_fill(sels, 128)
        em.dma(sels[0:nt, :], self.C[sels_n], w=[sels])
        twr = self.load_const("twr")
        twi = self.load_const(twi_n)
        sr = self.rot([128, D], 3, "s", mm=True)
        for tl in sr.tiles:
            self.zero_fill(tl, D)
        ar = self.rot([128, 2, D], 2, "a", mm=True)
        tmpr = self.rot([128, 512], 4, "tmp")
        pp = self.psum(8)
        mode = DBG.get("f1_mode", 9)

        def f1_load(b):
            s_ = sr.get()
            em.dma(s_[0:nt, :], src_b[b], r=[src_res], w=[s_])
            return s_

        def f1_body(b, s_):
            a = ar.get()
            for hf in range(2):
                if mode < 1:
                    continue
                pre, pim = pp.get(), pp.get()
                em.op("pe", lambda e: e.matmul(pre[:], selc[:], s_[:, hf * 512:(hf + 1) * 512], start=True, stop=True),
                      r=[selc, s_], w=[pre])
                em.op("pe", lambda e: e.matmul(pim[:], sels[:], s_[:, hf * 512:(hf + 1) * 512], start=True, stop=True),
                      r=[sels, s_], w=[pim])
                if mode < 2:
                    continue
                self.twiddle(pre, pim, twr[:, b:b + 1], twi[:, b:b + 1], [twr, twi], a, hf, tmpr, 128)
            if mode >= 3:
                em.dma(ADa[b].rearrange("r k c -> k r c"), a[:], r=[a], w=[ADr])

        self.prefetch_loop(128, f1_load, f1_body)

    def twiddle(self, pre, pim, cr, ci, tabs, out, hf, tmpr, np_):
        em = self.em
        t1, t2 = tmpr.get(), tmpr.get()
        sl = slice(hf * 512, (hf + 1) * 512)
        twm = DBG.get("tw_mode", 3)
        if twm & 1:
            em.op("act", lambda e: e.activation(t1[0:np_, :], pim[0:np_, :], AF.Identity, scale=ci), r=[pim] + tabs, w=[t1])
            em.op("act", lambda e: e.activation(t2[0:np_, :], pre[0:np_, :], AF.Identity, scale=ci), r=[pre] + tabs, w=[t2])
        if twm & 2:
            em.op("dve", lambda e: e.scalar_tensor_tensor(out[0:np_, 0, sl], pre[0:np_, :], cr, t1[0:np_, :], ALU.mult, ALU.subtract),
                  r=[pre, t1] + tabs, w=[out])
            em.op("dve", lambda e: e.scalar_tensor_tensor(out[0:np_, 1, sl], pim[0:np_, :], cr, t2[0:np_, :], ALU.mult, ALU.add),
                  r=[pim, t2] + tabs, w=[out])

    def pass_f2h(self, j):
        em = self.em
        self.reset()
        ADf, ADfr = self.scratch("AD1", [128, 2, 128, D], BF16)
        ADb, ADbr = self.scratch("AD2", [128, 2, 128, D], BF16)
        HH, HHr = self.scratch("HH", [128, 2, 65, D])
        RN, RNr = self.scratch("RN", [128, D])
        c2 = self.load_const("c2", mm=True)
        s2 = self.load_const("s2", mm=True)
        ns2 = self.load_const("ns2", mm=True)
        wk = self.alloc([128, 128], "wk")
        em.dma(wk[0:65, :], self.C["wk"], w=[wk])
        rn = self.alloc([128, D], "rn")
        em.dma(rn[:], RN, r=[RNr], w=[rn])
        bd = self.load_rep(self.W["b_bias_d"][j], D, "bd")
        afr = self.rot([128, 2, D], 3, "af", mm=True)
        abr = self.rot([128, 2, D], 3, "ab", mm=True)
        hhr = self.rot([128, 2, D], 2, "hh")
        tmpr = self.rot([128, 512], 4, "tmp")
        pp = self.psum(8)
        def f2_load(K1):
            af, ab = afr.get(), abr.get()
            em.dma(af[:], ADf[:, :, K1, :], r=[ADfr], w=[af])
            em.dma(ab[:], ADb[:, :, K1, :], r=[ADbr], w=[ab])
            return af, ab

        def f2_body(K1, tl):
            af, ab = tl
            hh = hhr.get()
            for hf in range(2):
                sl = slice(hf * 512, (hf + 1) * 512)
                pre, pim = pp.get(), pp.get()
                terms_re = [(c2, af, 0), (s2, af, 1), (c2, ab, 0), (ns2, ab, 1)]
                terms_im = [(c2, af, 1), (ns2, af, 0), (c2, ab, 1), (s2, ab, 0)]
                for n, (tb, src, ri) in enumerate(terms_re):
                    em.op("pe", lambda e: e.matmul(pre[0:65, :], tb[:, 0:65], src[:, ri, sl], start=(n == 0), stop=(n == 3)),
                          r=[tb, src], w=[pre])
                for n, (tb, src, ri) in enumerate(terms_im):
                    em.op("pe", lambda e: e.matmul(pim[0:65, :], tb[:, 0:65], src[:, ri, sl], start=(n == 0), stop=(n == 3)),
                          r=[tb, src], w=[pim])
                t1, t2 = tmpr.get(), tmpr.get()
                em.op("dve", lambda e: e.tensor_tensor(t1[0:65, :], pre[0:65, :], rn[0:65, sl], ALU.mult), r=[pre, rn], w=[t1])
                em.op("pool", lambda e: e.tensor_tensor(t1[0:65, :], t1[0:65, :], bd[0:65, sl], ALU.add), r=[t1, bd], w=[t1])
                em.op("act", lambda e: e.activation(hh[0:65, 0, sl], t1[0:65, :], AF.Identity, scale=wk[0:65, K1:K1 + 1]), r=[t1, wk], w=[hh])
                em.op("dve", lambda e: e.tensor_tensor(t2[0:65, :], pim[0:65, :], rn[0:65, sl], ALU.mult), r=[pim, rn], w=[t2])
                em.op("act", lambda e: e.activation(hh[0:65, 1, sl], t2[0:65, :], AF.Identity, scale=wk[0:65, K1:K1 + 1]), r=[t2, wk], w=[hh])
            em.dma(HH[K1].rearrange("r k c -> k r c"), hh[0:65, :, :], r=[hh], w=[HHr])

        self.prefetch_loop(128, f2_load, f2_body)

    def pass_mid(self):
        em = self.em
        self.reset()
        ADv, ADvr = self.scratch("AD0", [128, 2, 128, D], BF16)
        HH, HHr = self.scratch("HH", [128, 2, 65, D])
        ZD, ZDr = self.scratch("ZD", [128, 2, 128, D], BF16)
        c2 = self.load_const("c2", mm=True)
        s2 = self.load_const("s2", mm=True)
        ns2 = self.load_const("ns2", mm=True)
        ic = self.alloc([128, 128], "ic", mm=True)
        em.dma(ic[0:65, :], self.C["ic"], w=[ic])
        is_ = self.alloc([128, 128], "is", mm=True)
        em.dma(is_[0:65, :], self.C["is"], w=[is_])
        nis = self.alloc([128, 128], "nis", mm=True)
        em.dma(nis[0:65, :], self.C["nis"], w=[nis])
        tir = self.load_const("tir")
        tii = self.load_const("tii")
        avr = self.rot([128, 2, D], 3, "av", mm=True)
        hhr = self.rot([128, 2, D], 3, "hh")
        yr = self.rot([128, 2, 512], 3, "y", mm=True)
        zr = self.rot([128, 2, D], 3, "z", mm=True)
        tmpr = self.rot([128, 512], 12, "tmp")
        pp = self.psum(8)
        def mid_load(K1):
            av, hh = avr.get(), hhr.get()
            em.dma(av[:], ADv[:, :, K1, :], r=[ADvr], w=[av])
            em.dma(hh[0:65, :, :], HH[K1].rearrange("r k c -> k r c"), r=[HHr], w=[hh])
            return av, hh

        pend = []

        def stage2(K1, hf, y, z):
            pzr, pzi = pp.get(), pp.get()
            for n, (tb, ri) in enumerate([(ic, 0), (nis, 1)]):
                em.op("pe", lambda e: e.matmul(pzr[:], tb[0:65, :], y[0:65, ri, :], start=(n == 0), stop=(n == 1)), r=[tb, y], w=[pzr])
            for n, (tb, ri) in enumerate([(ic, 1), (is_, 0)]):
                em.op("pe", lambda e: e.matmul(pzi[:], tb[0:65, :], y[0:65, ri, :], start=(n == 0), stop=(n == 1)), r=[tb, y], w=[pzi])
            self.twiddle(pzr, pzi, tir[:, K1:K1 + 1], tii[:, K1:K1 + 1], [tir, tii], z, hf, tmpr, 128)
            if hf == 1:
                em.dma(ZD[K1].rearrange("r b c -> b r c"), z[:], r=[z], w=[ZDr])

        def mid_body(K1, tl):
            av, hh = tl
            z = zr.get()
            for hf in range(2):
                sl = slice(hf * 512, (hf + 1) * 512)
                pxr, pxi = pp.get(), pp.get()
                for n, (tb, ri) in enumerate([(c2, 0), (s2, 1)]):
                    em.op("pe", lambda e: e.matmul(pxr[0:65, :], tb[:, 0:65], av[:, ri, sl], start=(n == 0), stop=(n == 1)),
                          r=[tb, av], w=[pxr])
                for n, (tb, ri) in enumerate([(c2, 1), (ns2, 0)]):
                    em.op("pe", lambda e: e.matmul(pxi[0:65, :], tb[:, 0:65], av[:, ri, sl], start=(n == 0), stop=(n == 1)),
                          r=[tb, av], w=[pxi])
                if pend:
                    stage2(*pend.pop(0))
                ta, tb_, tc, td = tmpr.get(), tmpr.get(), tmpr.get(), tmpr.get()
                y = yr.get()
                em.op("dve", lambda e: e.tensor_tensor(ta[0:65, :], pxr[0:65, :], hh[0:65, 0, sl], ALU.mult), r=[pxr, hh], w=[ta])
                em.op("dve", lambda e: e.tensor_tensor(tb_[0:65, :], pxi[0:65, :], hh[0:65, 1, sl], ALU.mult), r=[pxi, hh], w=[tb_])
                em.op("pool", lambda e: e.tensor_tensor(y[0:65, 0, :], ta[0:65, :], tb_[0:65, :], ALU.subtract), r=[ta, tb_], w=[y])
                em.op("dve", lambda e: e.tensor_tensor(tc[0:65, :], pxr[0:65, :], hh[0:65, 1, sl], ALU.mult), r=[pxr, hh], w=[tc])
                em.op("dve", lambda e: e.tensor_tensor(td[0:65, :], pxi[0:65, :], hh[0:65, 0, sl], ALU.mult), r=[pxi, hh], w=[td])
                em.op("pool", lambda e: e.tensor_tensor(y[0:65, 1, :], tc[0:65, :], td[0:65, :], ALU.add), r=[tc, td], w=[y])
                pend.append((K1, hf, y, z))

        self.prefetch_loop(128, mid_load, mid_body)
        while pend:
            stage2(*pend.pop(0))

    def pass_i2(self, j):
        em = self.em
        self.reset()
        Tn, nt = self.Tn, self.nt
        ZD, ZDr = self.scratch("ZD", [128, 2, 128, D], BF16)
        U0, U0r = self.scratch("U0T", [D, Tn + 2])
        GT, GTr = self.scratch("GT", [D, Tn], BF16)
        iwc = self.load_const("iwc", mm=True)
        iwsn = self.load_const("iwsn", mm=True)
        cw = self.W["b_conv_w"][j]
        w0c = self.load_cols(cw[0, 0:D], KC, "w0c")
        w1c = self.load_cols(cw[1, 0:D], KC, "w1c")
        w2c = self.load_cols(cw[2, 0:D], KC, "w2c")
        bc = self.load_cols(self.W["b_conv_b"][j, 0:D], KC, "bc")
        yT = self.alloc([128, Tn], "yT")
        yTv = yT[:].rearrange("p (i b) -> p b i", b=128)
        NB = 512 // nt
        zzr = self.rot([128, 2, NB, 128], 3, "zz", mm=True)
        TB = 512
        u0r = self.rot([128, TB + 2], 3, "u0")
        mpr = self.rot([128, TB], 3, "mp")
        mnr = self.rot([128, TB], 3, "mn")
        x0r = self.rot([128, TB], 2, "x0")
        ttr = self.rot([128, TB], 2, "tt")
        gr = self.rot([128, TB], 2, "g", mm=True)
        pp = self.psum(8)
        for cc in range(KC):
            def zz_load(ib):
                b0 = ib * NB
                zz = zzr.get()
                em.dma(zz[:], ZD[:, :, b0:b0 + NB, cc * 128:(cc + 1) * 128], r=[ZDr], w=[zz])
                return zz

            def zz_body(ib, zz):
                b0 = ib * NB
                ps = pp.get()
                for bb in range(NB):
                    em.op("pe", lambda e: e.matmul(ps[:, bb * nt:(bb + 1) * nt], zz[:, 0, bb, :], iwc[:, 0:nt], start=True, stop=False),
                          r=[zz, iwc], w=[ps])
                    em.op("pe", lambda e: e.matmul(ps[:, bb * nt:(bb + 1) * nt], zz[:, 1, bb, :], iwsn[:, 0:nt], start=False, stop=True),
                          r=[zz, iwsn], w=[ps])
                if ib % 2 == 0:
                    em.op("act", lambda e: e.copy(yTv[:, b0:b0 + NB, :], ps[:, 0:NB * nt].rearrange("p (b i) -> p b i", b=NB)), r=[ps], w=[yT])
                else:
                    em.op("dve", lambda e: e.tensor_copy(yTv[:, b0:b0 + NB, :], ps[:, 0:NB * nt].rearrange("p (b i) -> p b i", b=NB)), r=[ps], w=[yT])

            self.prefetch_loop(128 // NB, zz_load, zz_body)

            def g_load(blk):
                t0 = blk * TB
                u0 = u0r.get()
                em.dma(u0[:], U0[cc * 128:(cc + 1) * 128, t0:t0 + TB + 2], r=[U0r], w=[u0])
                mpt, mnt = mpr.get(), mnr.get()
                em.dma(mpt[:], self.C["mpR"][:, t0:t0 + TB], w=[mpt])
                em.dma(mnt[:], self.C["mnR"][:, t0:t0 + TB], w=[mnt])
                return u0, mpt, mnt

            def g_body(blk, tl):
                u0, mpt, mnt = tl
                t0 = blk * TB
                x0 = x0r.get()
                em.op("dve", lambda e: e.tensor_scalar(x0[:], u0[:, 1:TB + 1], w1c[:, cc:cc + 1], bc[:, cc:cc + 1], ALU.mult, ALU.add),
                      r=[u0, w1c, bc], w=[x0])
                t1 = ttr.get()
                em.op("pool", lambda e: e.tensor_tensor(t1[:], u0[:, 0:TB], mpt[:], ALU.mult), r=[u0, mpt], w=[t1])
                em.op("dve", lambda e: e.scalar_tensor_tensor(x0[:], t1[:], w0c[:, cc:cc + 1], x0[:], ALU.mult, ALU.add), r=[t1, w0c, x0], w=[x0])
                t2 = ttr.get()
                em.op("pool", lambda e: e.tensor_tensor(t2[:], u0[:, 2:TB + 2], mnt[:], ALU.mult), r=[u0, mnt], w=[t2])
                em.op("dve", lambda e: e.scalar_tensor_tensor(x0[:], t2[:], w2c[:, cc:cc + 1], x0[:], ALU.mult, ALU.add), r=[t2, w2c, x0], w=[x0])
                g = gr.get()
                em.op("pool", lambda e: e.tensor_tensor(g[:], x0[:], yT[:, t0:t0 + TB], ALU.mult), r=[x0, yT], w=[g])
                em.dma(GT[cc * 128:(cc + 1) * 128, t0:t0 + TB], g[:], r=[g], w=[GTr])

            self.prefetch_loop(Tn // TB, g_load, g_body)


_CACHE = {}


def run_cores(xs, mems, lsegs, weights, kinds, debug_out=()):
    nt = xs[0].shape[0] // 128
    consts = [build_consts(nt, l) for l in lsegs]
    wshapes = {k: weights[k].shape for k in WEIGHT_NAMES}
    cshapes = {k: v.shape for k, v in consts[0].items()}
    key = (nt, tuple(kinds), tuple(debug_out))
    if key not in _CACHE:
        _CACHE[key] = Builder(nt, kinds, wshapes, cshapes, debug_out).build()
    nc = _CACHE[key]
    in_maps = []
    for ci in range(len(xs)):
        m = {k: np.ascontiguousarray(weights[k], dtype=np.float32) for k in WEIGHT_NAMES}
        for k, v in consts[ci].items():
            m["c_" + k] = v.astype(ml_dtypes.bfloat16) if k in BF_CONSTS else v
        m["x"] = np.ascontiguousarray(xs[ci], dtype=np.float32)
        m["mem"] = np.ascontiguousarray(mems[ci], dtype=np.float32)
        in_maps.append(m)
    res = run_bass_kernel_spmd(nc, in_maps, core_ids=list(range(len(xs))))
    return res.results


def kernel(**inputs):
    xp = np.asarray(inputs["x_prompt"], dtype=np.float32)
    xs_ = np.asarray(inputs["x_sample"], dtype=np.float32)
    mp = np.asarray(inputs["mem_prompt"], dtype=np.float32)
    ms = np.asarray(inputs["mem_sample"], dtype=np.float32)
    weights = {k: np.asarray(inputs[k], dtype=np.float32) for k in WEIGHT_NAMES}
    xs, mems, lsegs = [], [], []
    for c in range(4):
        xs.append(xp[4 * c:4 * c + 4].reshape(8192, D))
        mems.append(mp[4 * c:4 * c + 4])
        lsegs.append(2048)
    for c in range(4):
        xs.append(xs_[c])
        mems.append(np.broadcast_to(ms[c][None], (4, MEM, D)))
        lsegs.append(8192)
    res = run_cores(xs, mems, lsegs, weights, ["a", "b", "c", "a"])
    yp = np.stack([res[c]["y"] for c in range(4)]).reshape(16, 2048, D)
    ys = np.stack([res[4 + c]["y"] for c in range(4)]).reshape(4, 8192, D)
    return (yp.astype(np.float32), ys.astype(np.float32))
```

```python
import contextlib
import math
import numpy as np
import ml_dtypes
import concourse.bass as bass
import concourse.mybir as mybir
from concourse.bass_utils import run_bass_kernel_spmd

F32 = mybir.dt.float32
BF16 = mybir.dt.bfloat16
AF = mybir.ActivationFunctionType
ALU = mybir.AluOpType
AX = mybir.AxisListType

D = 1024
KC = 8
DFF = 2816
FC = 22
MEM = 256
EPS = 1e-6
NFFT = 16384
TWO_PI = 2.0 * math.pi

DBG = {}
EPOCH = 30000
NRING = 16
DEPOCH = 1800


class Res:
    __slots__ = ("lw", "rd", "name", "excl")

    def __init__(self, name="", excl=False):
        self.lw = None
        self.rd = {}
        self.name = name
        self.excl = excl


class T:
    def __init__(self, t, name="", mm=False):
        self.t = t
        self.r = Res(name)
        self.mm = mm

    def __getitem__(self, idx):
        return self.t[idx]


def _res(x):
    return x.r if isinstance(x, T) else x


class _Rec:
    def __init__(self):
        self.call = None

    def __getattr__(self, name):
        def f(*a, **k):
            self.call = (name, a, k)
            return self
        return f


class Emitter:
    ENGS = ("pe", "act", "dve", "pool", "sp")

    def __init__(self, nc, stack):
        self.nc = nc
        self.stack = stack
        self.ops = {e: [] for e in self.ENGS}
        self.ccount = {e: 0 for e in self.ENGS}
        self.dcount = {e: 0 for e in self.ENGS}
        self.seen = {e: {} for e in self.ENGS}
        self.sems = {}
        self.nsem = 0

    def sem(self, key):
        s = self.sems.get(key)
        if s is None:
            s = self.stack.enter_context(self.nc.semaphore("s%d" % self.nsem))
            self.nsem += 1
            self.sems[key] = s
        return s

    def _target(self, ident):
        k, e, i = ident
        if k == "c":
            return ("c", e, i // EPOCH), (i % EPOCH) + 1
        slot = i % NRING
        j = i // NRING
        return ("d", e, slot, j // DEPOCH), 16 * ((j % DEPOCH) + 1)

    def _need(self, eng, ident, waits):
        if ident is None:
            return
        key, val = self._target(ident)
        if self.seen[eng].get(key, 0) >= val:
            return
        self.seen[eng][key] = val
        waits.append((key, val))

    def op(self, eng, fn, r=(), w=(), dma=False, ns=False):
        waits = []
        deps = []
        for x in r:
            res = _res(x)
            deps.append(res.lw)
            if res.excl:
                deps.extend(v for k, v in res.rd.items() if k[0] != eng)
        for x in w:
            res = _res(x)
            deps.append(res.lw)
            deps.extend(res.rd.values())
        for d in deps:
            if d is None:
                continue
            if d[1] == eng and d[0] == "c" and not dma and (eng == "pe" or ns):
                continue
            self._need(eng, d, waits)
        if dma:
            i = self.dcount[eng]
            self.dcount[eng] += 1
            ident = ("d", eng, i)
            if i >= NRING:
                self._need(eng, ("d", eng, i - NRING), waits)
        else:
            i = self.ccount[eng]
            self.ccount[eng] += 1
            ident = ("c", eng, i)
        key, val = self._target(ident)
        rec = _Rec()
        fn(rec)
        assert rec.call is not None
        call = rec.call
        self.ops[eng].append((call, waits, key, 16 if dma else 1))
        for x in r:
            _res(x).rd[(eng, dma)] = ident
        for x in w:
            res = _res(x)
            res.lw = ident
            res.rd = {}
        return ident

    def dma(self, out, in_, r=(), w=(), slow=False, eng="sp"):
        if slow:
            return self.op(eng, lambda e: e.dma_start(out=out, in_=in_, allow_slow_non_contiguous=True),
                           r=r, w=w, dma=True)
        return self.op(eng, lambda e: e.dma_start(out=out, in_=in_), r=r, w=w, dma=True)

    def barrier(self, engs=None):
        for x in (engs or self.ENGS):
            waits = []
            for e in self.ENGS:
                if self.ccount[e] and e != x:
                    self._need(x, ("c", e, self.ccount[e] - 1), waits)
                n = self.dcount[e]
                for i in range(max(0, n - NRING), n):
                    self._need(x, ("d", e, i), waits)
            if self.ccount[x]:
                self._need(x, ("c", x, self.ccount[x] - 1), waits)
            if waits:
                self.ops[x].append((None, waits, None, 0))

    def replay(self):
        nc = self.nc
        for e in self.ENGS:
            for fn, waits, key, inc in self.ops[e]:
                for k, v in waits:
                    self.sem(k)
                if key is not None:
                    self.sem(key)
        engmap = {"pe": "tensor", "act": "scalar", "dve": "vector", "pool": "gpsimd", "sp": "sync"}
        with nc.Block() as block:
            for e in self.ENGS:
                if not self.ops[e]:
                    continue
                ops = self.ops[e]

                def body(engine, ops=ops):
                    for fn, waits, key, inc in ops:
                        for k, v in waits:
                            engine.wait_ge(self.sems[k], v)
                        if fn is not None:
                            name, a, k = fn
                            getattr(engine, name)(*a, **k).then_inc(self.sems[key], inc)

                getattr(block, engmap[e])(body)


class TiledW:
    def __init__(self, ap4, gcols):
        self.ap = ap4
        self.g = gcols

    def rearrange(self, *a, **k):
        return self

    def __getitem__(self, idx):
        sp, sc, sn = idx
        c0, n = sn.start, sn.stop - sn.start
        g, off = c0 // self.g, c0 % self.g
        assert off + n <= self.g, (c0, n, self.g)
        return self.ap[g][:, :, off:off + n]


TILE_SPECS = {
    "a_w_in": ("a_w_in", 8, 512, 0, 9216), "a_w_out": ("a_w_out", 8, 512, 0, 1024),
    "b_w_in": ("b_w_in", 8, 512, 0, 3072), "b_w_out": ("b_w_out", 8, 512, 0, 1024),
    "c_w_in": ("c_w_in", 8, 512, 0, 2048), "c_w_out": ("c_w_out", 8, 512, 0, 1024),
    "x_w_q": ("x_w_q", 8, 512, 0, 1024), "x_w_kv": ("x_w_kv", 8, 512, 0, 2048), "x_w_o": ("x_w_o", 8, 512, 0, 1024),
    "f_w_g": ("f_w_gu", 8, 512, 0, 2816), "f_w_u": ("f_w_gu", 8, 512, 2816, 2816),
    "f_w_down": ("f_w_down", 22, 128, 0, 1024),
}


class Rot:
    def __init__(self, tiles):
        self.tiles = tiles
        self.i = 0

    def get(self):
        t = self.tiles[self.i % len(self.tiles)]
        self.i += 1
        return t


A_GROUPS = ((128, 1), (512, 4), (2048, 16))


def _coarse_index(nt, lseg):
    tiles_per_seg = lseg // 128
    i = np.arange(nt)
    if lseg >= nt * 128:
        return i
    return (i // tiles_per_seg) * (2 * tiles_per_seg) + (i % tiles_per_seg)


def build_consts(nt, lseg):
    Tn = nt * 128
    f32 = np.float32
    c = {}
    c["ident"] = np.eye(128, dtype=f32)
    c["identb"] = np.eye(128, dtype=f32)
    obm = np.zeros((2, 128, 128), f32)
    obm[0, :, 0:64] = 1.0
    obm[1, :, 64:128] = 1.0
    c["obm"] = obm
    R = np.zeros((64, 64), f32)
    for j in range(32):
        R[j + 32, j] = -1.0
        R[j, j + 32] = 1.0
    rb = np.zeros((128, 128), f32)
    rb[:64, :64] = R
    rb[64:, 64:] = R
    c["rblk"] = rb
    pos = (np.arange(Tn) % lseg).astype(np.float64)
    inv = 10000.0 ** (-np.arange(0, 64, 2, dtype=np.float64) / 64)
    inv32 = inv.astype(f32).astype(np.float64)
    ang = (pos.astype(f32)[:, None] * inv32.astype(f32)[None, :]).astype(f32)
    cos = np.cos(ang.astype(np.float64)).astype(f32)
    sin = np.sin(ang.astype(np.float64)).astype(f32)
    rows = np.arange(128) % 32
    c["ropec"] = np.ascontiguousarray(cos[:, rows].T)
    c["ropes"] = np.ascontiguousarray(sin[:, rows].T)
    for gi, (window, d) in enumerate(A_GROUPS):
        nsub = Tn // d
        ntile = nsub // 128
        nseg_sub = lseg // d
        m = np.zeros((ntile, 2, 128, 128), f32)
        kk = np.arange(128)[:, None]
        qq = np.arange(128)[None, :]
        for t in range(ntile):
            nq = 128 * t + qq
            for X in range(2):
                nk = 128 * t - 64 + 128 * X + kk
                ok = (np.abs(nq - nk) <= 64) & (nk >= 0) & (nk < nsub) & ((nq // nseg_sub) == (nk // nseg_sub))
                m[t, X] = ok.astype(f32)
        c["abias%d" % gi] = ((m - 1.0) * 30000.0).astype(f32)
    L = lseg
    tl = np.linspace(0.0, 1.0, L, dtype=f32)
    w = (2.0 * math.pi * np.arange(L, dtype=f32) / L).astype(f32)
    bands = 16
    fb = np.linspace(1e-4, bands - 1, bands, dtype=f32)[None, :]
    fw = (fb * w[:, None]).astype(f32)
    z = np.concatenate([tl[:, None], np.cos(fw.astype(np.float64)).astype(f32),
                        -np.sin(fw.astype(np.float64)).astype(f32)], axis=-1)
    reps = Tn // L
    zfull = np.tile(z, (reps, 1))
    c["zT"] = np.ascontiguousarray(zfull.T)
    tfull = np.tile(tl, reps)
    c["negt"] = np.ascontiguousarray((-tfull).reshape(nt, 128).T)
    vm = np.zeros((128, nt), f32)
    vm[:, : L // 128] = 1.0
    c["vmask"] = vm
    max_decay = math.log(1e-2) / 0.3
    min_decay = math.log(1e-2) / 1.5
    deltas = np.linspace(min_decay, max_decay, D, dtype=f32)
    c["absdelta"] = np.ascontiguousarray(np.broadcast_to(np.abs(deltas)[None, :], (128, D)))
    a = _coarse_index(nt, lseg).astype(np.float64)
    ah = np.arange(nt, dtype=np.float64)
    K1 = np.arange(128, dtype=np.float64)
    hvalid = (np.arange(nt) < L // 128).astype(np.float64)

    def cs(x):
        xm = np.mod(x, 1.0)
        return np.cos(TWO_PI * xm).astype(f32), np.sin(TWO_PI * xm).astype(f32)

    cc, ss = cs(np.outer(a, K1) / 128.0)
    c["wselc"] = cc
    c["wsels"] = -ss
    cc, ss = cs(np.outer(ah, K1) / 128.0)
    c["hselc"] = (cc * hvalid[:, None]).astype(f32)
    c["hsels"] = (-ss * hvalid[:, None]).astype(f32)
    c["hselsn"] = (ss * hvalid[:, None]).astype(f32)
    b = np.arange(128, dtype=np.float64)
    cc, ss = cs(np.outer(K1, b) / NFFT)
    c["twr"] = cc
    c["twi"] = -ss
    c["twin"] = ss
    K2 = np.arange(65, dtype=np.float64)
    cc, ss = cs(np.outer(b, K2) / 128.0)
    c["c2"] = cc
    c["s2"] = ss
    c["ns2"] = -ss
    wk = np.zeros((65, 128), f32)
    for k2 in range(65):
        for k1 in range(128):
            k = k1 + 128 * k2
            if k == 0 or k == NFFT // 2:
                wk[k2, k1] = 1.0 / NFFT
            elif k < NFFT // 2:
                wk[k2, k1] = 2.0 / NFFT
    c["wk"] = wk
    cc, ss = cs(np.outer(K2, b) / 128.0)
    c["ic"] = cc
    c["is"] = ss
    c["nis"] = -ss
    cc, ss = cs(np.outer(b, K1) / NFFT)
    c["tir"] = cc
    c["tii"] = ss
    cc, ss = cs(np.outer(K1, a) / 128.0)
    c["iwc"] = cc
    c["iwsn"] = -ss
    tok = np.arange(Tn)
    mp = ((tok % L) != 0).astype(f32)
    mn = ((tok % L) != (L - 1)).astype(f32)
    c["mp"] = np.ascontiguousarray(mp.reshape(nt, 128).T)
    c["mn"] = np.ascontiguousarray(mn.reshape(nt, 128).T)
    c["mpR"] = np.ascontiguousarray(np.broadcast_to(mp[None, :], (128, Tn)))
    c["mnR"] = np.ascontiguousarray(np.broadcast_to(mn[None, :], (128, Tn)))
    return c


BF_CONSTS = {"rblk", "obm", "identb", "abias0", "abias1", "abias2", "c2", "s2", "ns2", "ic", "is", "nis", "iwc", "iwsn", "wselc", "wsels", "hselc", "hsels", "hselsn"}
BIG_WEIGHTS = ["a_w_in", "a_w_out", "b_w_in", "b_w_out", "c_w_in", "c_w_out", "c_w_s", "x_w_q", "x_w_kv", "x_w_o", "f_w_gu", "f_w_down"]

WEIGHT_NAMES = ["g_mix", "g_cross", "g_ffn", "g_final", "a_w_in", "a_w_out", "b_w_in", "b_conv_w", "b_conv_b",
                "b_f_w1", "b_f_b1", "b_f_w2", "b_f_b2", "b_f_w3", "b_f_b3", "b_f_wout", "b_f_freq", "b_bias_d",
                "b_w_out", "c_w_in", "c_ln_g", "c_ln_b", "c_w_s", "c_b_s", "c_w_out", "x_w_q", "x_w_kv", "x_w_o",
                "f_w_gu", "f_w_down"]


class Builder:
    def __init__(self, nt, kinds, wshapes, cshapes, debug_out=()):
        self.nt = nt
        self.Tn = nt * 128
        self.kinds = kinds
        self.debug_out = set(debug_out)
        nc = bass.Bass("TRN2", target_bir_lowering=False)
        self.nc = nc
        self.W = {}
        for name in WEIGHT_NAMES:
            self.W[name] = nc.dram_tensor(name, list(wshapes[name]), F32, kind="ExternalInput").ap()
        self.C = {}
        for name, shp in cshapes.items():
            self.C[name] = nc.dram_tensor("c_" + name, list(shp), BF16 if name in BF_CONSTS else F32, kind="ExternalInput").ap()
        self.x_in = nc.dram_tensor("x", [self.Tn, D], F32, kind="ExternalInput").ap()
        self.mem_in = nc.dram_tensor("mem", [4, MEM, D], F32, kind="ExternalInput").ap()
        self.y_out = nc.dram_tensor("y", [self.Tn, D], F32, kind="ExternalOutput").ap()
        self.scr = {}
        self.scr_res = {}

    def scratch(self, name, shape, dt=F32):
        if name not in self.scr:
            kind = "ExternalOutput" if name in self.debug_out else "Internal"
            self.scr[name] = self.nc.dram_tensor("scr_" + name, list(shape), dt, kind=kind).ap()
            self.scr_res[name] = Res(name)
        return self.scr[name], self.scr_res[name]

    def dump(self, name, tile, shape):
        if name not in self.debug_out or name in self.scr:
            return
        d, dr = self.scratch(name, shape)
        self.em.dma(d, tile[:], r=[tile], w=[dr])

    def defer(self, fn):
        self.deferred.append(fn)

    def run_deferred(self):
        d, self.deferred = self.deferred, []
        for fn in d:
            fn()

    def reset(self):
        self.run_deferred()
        self.em.barrier()
        self.off = 0
        self.off_mm = 0
        self.psi = 0

    def alloc(self, shape, name="", mm=False):
        n = int(np.prod(shape[1:]))
        if mm:
            assert self.off_mm + n <= self.mm_words, ("mm arena overflow", name, self.off_mm, n)
            ap = self.arena_mm[0:shape[0], self.off_mm:self.off_mm + n]
            self.off_mm += n
        else:
            assert self.off + n <= self.arena_words, ("arena overflow", name, self.off, n)
            ap = self.arena[0:shape[0], self.off:self.off + n]
            self.off += n
        if len(shape) == 3:
            ap = ap.rearrange("p (a b) -> p a b", a=shape[1])
        elif len(shape) == 4:
            ap = ap.rearrange("p (a b c) -> p a b c", a=shape[1], b=shape[2])
        return T(ap, name, mm)

    def rot(self, shape, n, name="", mm=False):
        return Rot([self.alloc(shape, "%s%d" % (name, i), mm) for i in range(n)])

    def psum(self, n):
        assert self.psi + n <= 8
        r = Rot(self.pbanks[self.psi:self.psi + n])
        self.psi += n
        return r

    def build(self):
        nc = self.nc
        with contextlib.ExitStack() as st:
            self.em = Emitter(nc, st)
            em = self.em
            self.arena_words = 16000
            self.mm_words = 74000
            self.arena = st.enter_context(nc.sbuf_tensor("arena", [128, self.arena_words], F32))
            self.arena_mm = st.enter_context(nc.sbuf_tensor("arena_mm", [128, self.mm_words], BF16))
            self.off_mm = 0
            self.pbanks = [T(st.enter_context(nc.psum_tensor("pb%d" % i, [128, 512], F32)), "pb%d" % i)
                           for i in range(8)]
            for pb in self.pbanks:
                pb.r.excl = True
            self.off = 0
            self.psi = 0
            self.deferred = []
            self.program()
            self.run_deferred()
            em.barrier()
            em.replay()
        return nc

    def load_const(self, name, shape=None, slow=False, mm=False):
        ap = self.C[name]
        t = self.alloc(list(ap.shape) if shape is None else shape, name, mm)
        self.em.dma(t[:], ap, w=[t], slow=slow)
        return t

    @staticmethod
    def prefetch_loop(n, load_fn, body_fn):
        nxt = load_fn(0) if n > 0 else None
        for i in range(n):
            cur = nxt
            nxt = load_fn(i + 1) if i + 1 < n else None
            body_fn(i, cur)

    def ones_tile(self):
        t = self.alloc([128, 128], "ones", mm=True)
        self.em.op("dve", lambda e: e.memset(t[:], 1.0), w=[t])
        return t

    def zero_fill(self, t, n):
        self.em.op("pool", lambda e: e.memset(t[:], 0.0), w=[t])

    def load_cols(self, vec_ap, nchunk, name):
        t = self.alloc([128, nchunk], name)
        self.em.dma(t[:], vec_ap.rearrange("(c p) -> p c", p=128), w=[t], slow=True)
        return t

    def load_rep(self, vec_ap, n, name):
        t = self.alloc([128, n], name)
        self.em.dma(t[:], vec_ap.partition_broadcast(128), w=[t], slow=True)
        return t

    def rmsnorm(self, x, xn, sq, gcols, gi, ones, pp, rstd, TB):
        em = self.em
        em.op("act", lambda e: e.activation(sq[:], x[:], AF.Square), r=[x], w=[sq])
        ps = pp.get()
        for c in range(KC):
            em.op("pe", lambda e, c=c: e.matmul(ps[:, 0:TB], ones[:], sq[:, c, :], start=(c == 0), stop=(c == KC - 1)),
                  r=[ones, sq], w=[ps])
        em.op("dve", lambda e: e.tensor_scalar(rstd[:], ps[:, 0:TB], 1.0 / D, EPS, ALU.mult, ALU.add), r=[ps], w=[rstd])
        em.op("act", lambda e: e.activation(rstd[:], rstd[:], AF.Sqrt), r=[rstd], w=[rstd])
        em.op("dve", lambda e: e.reciprocal(rstd[:], rstd[:]), r=[rstd], w=[rstd])
        for c in range(KC):
            em.op("dve", lambda e, c=c: e.scalar_tensor_tensor(xn[:, c, :], x[:, c, :], gcols[:, gi * KC + c:gi * KC + c + 1],
                                                               rstd[:], ALU.mult, ALU.mult),
                  r=[x, gcols, rstd], w=[xn], ns=(c > 0))

    def linear_fm(self, w_ap, col0, ncols, src, kch, TB, wrot, pp, epi, group=512):
        em = self.em
        wv = w_ap.rearrange("(c p) n -> p c n", p=128)
        for g0 in range(0, ncols, group):
            gn = min(group, ncols - g0)
            wt = wrot.get()
            wtv = wt[:, 0:kch * gn].rearrange("p (c n) -> p c n", c=kch)
            em.dma(wtv, wv[:, :, col0 + g0:col0 + g0 + gn], w=[wt])
            self.run_deferred()
            for m in range(gn // 128):
                ps = pp.get()
                for k in range(kch):
                    em.op("pe", lambda e, k=k, m=m, ps=ps, wtv=wtv: e.matmul(
                        ps[:, 0:TB], wtv[:, k, m * 128:(m + 1) * 128], src[:, k, :], start=(k == 0), stop=(k == kch - 1)),
                        r=[wt, src], w=[ps])
                epi((g0 // 128) + m, ps)

    def pass_cast_weights(self):
        em = self.em
        self.reset()
        self.Wb = {}
        ldr = self.rot([128, 4096], 3, "wld")
        cvr = self.rot([128, 4096], 4, "wcv", mm=True)
        n = 0

        def cast(ld, cv, sz):
            nonlocal n
            eng = ("act", "dve")[n % 2]
            n += 1
            if eng == "act":
                em.op("act", lambda e: e.copy(cv[:, 0:sz], ld[:, 0:sz]), r=[ld], w=[cv])
            else:
                em.op(eng, lambda e: e.tensor_copy(cv[:, 0:sz], ld[:, 0:sz]), r=[ld], w=[cv])

        w = self.W["c_w_s"]
        wb, wbr = self.scratch("wb_c_w_s", list(w.shape), BF16)
        self.Wb["c_w_s"] = wb
        wf = w.rearrange("l h p q -> (l h p q)").rearrange("(p f) -> p f", p=128)
        wbf = wb.rearrange("l h p q -> (l h p q)").rearrange("(p f) -> p f", p=128)
        F = wf.shape[1]
        for f0 in range(0, F, 4096):
            fn = min(4096, F - f0)
            ld, cv = ldr.get(), cvr.get()
            em.dma(ld[:, 0:fn], wf[:, f0:f0 + fn], w=[ld])
            cast(ld, cv, fn)
            em.dma(wbf[:, f0:f0 + fn], cv[:, 0:fn], r=[cv], w=[wbr], eng="pool")
        for oname, (src, kch, gcols, base, ncols) in TILE_SPECS.items():
            w = self.W[src]
            Lw = w.shape[0]
            ng = -(-ncols // gcols)
            wt, wtr = self.scratch("wt_" + oname, [Lw, ng, 128, kch, gcols], BF16)
            self.Wb[oname] = [TiledW(wt[l], gcols) for l in range(Lw)]
            for l in range(Lw):
                wv = w[l].rearrange("(c p) n -> p c n", p=128)
                for g in range(ng):
                    gn = min(gcols, ncols - g * gcols)
                    ld, cv = ldr.get(), cvr.get()
                    ldv = ld[:, 0:kch * gn].rearrange("p (c n) -> p c n", c=kch)
                    cvv = cv[:, 0:kch * gn].rearrange("p (c n) -> p c n", c=kch)
                    em.dma(ldv, wv[:, :, base + g * gcols:base + g * gcols + gn], w=[ld])
                    cast(ld, cv, kch * gn)
                    em.dma(wt[l, g][:, :, 0:gn], cvv, r=[cv], w=[wtr], eng="pool")

    def program(self):
        self.pass_cast_weights()
        self.pass_transpose_in()
        self.pass_mem()
        nA = nB = nC = 0
        for li, kind in enumerate(self.kinds):
            if kind == "a":
                self.mixer_a(li, nA)
                self.pass_post(li, "OT", self.Wb["a_w_out"][nA])
                nA += 1
            elif kind == "b":
                self.mixer_b(li, nB)
                self.pass_post(li, "GT", self.Wb["b_w_out"][nB])
                nB += 1
            elif kind == "c":
                self.pass_post(li, None, self.Wb["c_w_out"][nC], sgu=nC)
                nC += 1
            else:
                self.pass_post(li, None, None)
        self.pass_final()

    def pass_transpose_in(self):
        em = self.em
        self.reset()
        xT, xTr = self.scratch("xT", [D, self.Tn])
        ident = self.load_const("ident")
        xin = self.rot([128, D], 3, "xin")
        stg = self.rot([128, KC, 512], 2, "stg")
        pp = self.psum(8)
        xTv = xT.rearrange("(c p) t -> p c t", p=128)
        for b4 in range(0, self.nt, 4):
            nb = min(4, self.nt - b4)
            s = stg.get()
            for j in range(nb):
                i = b4 + j
                xt = xin.get()
                em.dma(xt[:], self.x_in[i * 128:(i + 1) * 128, :], w=[xt])
                pa, pb = pp.get(), pp.get()
                for c in range(KC):
                    ps = pa if c < 4 else pb
                    em.op("pe", lambda e, c=c, ps=ps, xt=xt: e.transpose(ps[:, (c % 4) * 128:(c % 4 + 1) * 128],
                                                                          xt[:, c * 128:(c + 1) * 128], ident[:]),
                          r=[xt, ident], w=[ps])
                em.op("act", lambda e, s=s, j=j, pa=pa: e.copy(s[:, 0:4, j * 128:(j + 1) * 128],
                                                               pa[:].rearrange("p (c t) -> p c t", c=4)), r=[pa], w=[s])
                em.op("dve", lambda e, s=s, j=j, pb=pb: e.tensor_copy(s[:, 4:8, j * 128:(j + 1) * 128],
                                                                      pb[:].rearrange("p (c t) -> p c t", c=4)), r=[pb], w=[s])
            em.dma(xTv[:, :, b4 * 128:(b4 + nb) * 128], s[:, :, 0:nb * 128], r=[s], w=[xTr])

    def pass_mem(self):
        em = self.em
        self.reset()
        mT, mTr = self.scratch("memT", [4, D, MEM], BF16)
        ident = self.load_const("ident")
        xin = self.rot([128, D], 3, "min")
        stg = self.rot([128, KC, 128], 2, "mstg", mm=True)
        pp = self.psum(8)
        for s4 in range(4):
            for j in range(2):
                xt = xin.get()
                em.dma(xt[:], self.mem_in[s4, j * 128:(j + 1) * 128, :], w=[xt])
                pa, pb = pp.get(), pp.get()
                s = stg.get()
                for c in range(KC):
                    ps = pa if c < 4 else pb
                    em.op("pe", lambda e, c=c, ps=ps, xt=xt: e.transpose(ps[:, (c % 4) * 128:(c % 4 + 1) * 128],
                                                                          xt[:, c * 128:(c + 1) * 128], ident[:]),
                          r=[xt, ident], w=[ps])
                em.op("act", lambda e, s=s, pa=pa: e.copy(s[:, 0:4, :], pa[:].rearrange("p (c t) -> p c t", c=4)), r=[pa], w=[s])
                em.op("dve", lambda e, s=s, pb=pb: e.tensor_copy(s[:, 4:8, :], pb[:].rearrange("p (c t) -> p c t", c=4)), r=[pb], w=[s])
                em.dma(mT[s4].rearrange("(c p) t -> p c t", p=128)[:, :, j * 128:(j + 1) * 128], s[:], r=[s], w=[mTr])

    def pass_post(self, li, mixname, wout_ap, sgu=None):
        em = self.em
        self.reset()
        TB = 512
        nblk = self.Tn // TB
        blk_per_seg = self.Tn // 4 // TB
        xT, xTr = self.scratch("xT", [D, self.Tn])
        xTv = xT.rearrange("(c p) t -> p c t", p=128)
        mT, mTr = self.scratch("memT", [4, D, MEM], BF16)
        if mixname is not None:
            mx, mxr = self.scratch(mixname, [D, self.Tn], BF16)
            mxv = mx.rearrange("(c p) t -> p c t", p=128)
        ones = self.ones_tile()
        gm = self.load_cols(self.W["g_mix"].rearrange("l d -> (l d)"), 4 * KC, "gm")
        gc = self.load_cols(self.W["g_cross"].rearrange("l d -> (l d)"), 4 * KC, "gc")
        gf = self.load_cols(self.W["g_ffn"].rearrange("l d -> (l d)"), 4 * KC, "gf")
        xrot = self.rot([128, KC, TB], 2 if sgu is not None else 3, "x")
        a8 = self.rot([128, KC, TB], 3, "a8", mm=True)
        xn_t = self.alloc([128, KC, TB], "xn", mm=True)
        sq_t = self.alloc([128, KC, TB], "sq", mm=True)
        rstd = self.alloc([128, TB], "rstd")
        wrot = self.rot([128, 4096], 5, "w", mm=True)
        kt = self.alloc([128, KC, MEM], "kt", mm=True)
        vt = self.alloc([128, 2, D], "vt", mm=True)
        memt = self.alloc([128, KC, MEM], "memt", mm=True)
        prot = self.rot([128, 2, TB], 3, "pT", mm=True)
        rden = self.rot([128, TB], 2, "rden")
        hbuf = self.alloc([128, FC, TB], "h", mm=True)
        tmpr = self.rot([128, TB], 3, "tmp")
        pp = self.psum(8)
        if sgu is not None:
            j = sgu
            lngR = self.load_rep(self.W["c_ln_g"][j], D, "lng")
            lnbR = self.load_rep(self.W["c_ln_b"][j], D, "lnb")
            wsT = self.alloc([128, 8, 128], "wsT", mm=True)
            em.dma(wsT[:], self.Wb["c_w_s"][j].rearrange("h p q -> q h p"), w=[wsT], slow=True)
            bsR = self.alloc([128, 8, 128], "bsR")
            em.dma(bsR[:].rearrange("p h q -> p (h q)"),
                   self.W["c_b_s"][j].rearrange("h p -> (h p)").partition_broadcast(128), w=[bsR], slow=True)
            zv = self.rot([128, D], 1, "zv")
            zvn = self.rot([128, D], 1, "zvn", mm=True)
            st6 = self.rot([128, 8], 2, "st6")
        wq = self.Wb["x_w_q"][li]
        wkv = self.Wb["x_w_kv"][li]
        wo = self.Wb["x_w_o"][li]
        wg_t = self.Wb["f_w_g"][li]
        wu_t = self.Wb["f_w_u"][li]
        wdn = self.Wb["f_w_down"][li]
        mtr = self.rot([128, KC, TB], 2, "mt", mm=True) if mixname is not None else None

        def post_load(b):
            t0 = b * TB
            x = xrot.get()
            em.dma(x[:], xTv[:, :, t0:t0 + TB], r=[xTr], w=[x])
            mt = None
            if mixname is not None:
                mt = mtr.get()
                em.dma(mt[:], mxv[:, :, t0:t0 + TB], r=[mxr], w=[mt])
            return x, mt

        nxt = post_load(0)
        for b in range(nblk):
            t0 = b * TB
            x, mt = nxt
            if sgu is None:
                nxt = post_load(b + 1) if b + 1 < nblk else None

            def resid(m, ps, x=x):
                em.op("dve", lambda e: e.tensor_tensor(x[:, m, :], x[:, m, :], ps[:, 0:TB], ALU.add), r=[x, ps], w=[x], ns=True)

            if mixname is not None:
                self.linear_fm(wout_ap, 0, D, mt, KC, TB, wrot, pp, resid)
            elif sgu is not None:
                j = sgu
                self.rmsnorm(x, xn_t, sq_t, gm, li, ones, pp, rstd, TB)
                zu = a8.get()

                def epi_zu(m, ps, zu=zu):
                    em.op("act", lambda e: e.activation(zu[:, m, :], ps[:, 0:TB], AF.Gelu), r=[ps], w=[zu])
                self.linear_fm(self.Wb["c_w_in"][j], 0, D, xn_t, KC, TB, wrot, pp, epi_zu)
                nxt = post_load(b + 1) if b + 1 < nblk else None
                gate = a8.get()
                wv = self.Wb["c_w_in"][j].rearrange("(c p) n -> p c n", p=128)
                wts = []
                for h2 in range(2):
                    wt = wrot.get()
                    wtv = wt[:].rearrange("p (c n) -> p c n", c=KC)
                    em.dma(wtv, wv[:, :, D + h2 * 512:D + (h2 + 1) * 512], w=[wt])
                    wts.append((wt, wtv))
                for tl in range(TB // 128):
                    z = zv.get()
                    for h2 in range(2):
                        wt, wtv = wts[h2]
                        ps = pp.get()
                        for k in range(KC):
                            em.op("pe", lambda e, k=k, ps=ps, wtv=wtv, tl=tl: e.matmul(
                                ps[:], xn_t[:, k, tl * 128:(tl + 1) * 128], wtv[:, k, :], start=(k == 0), stop=(k == KC - 1)),
                                r=[xn_t, wt], w=[ps])
                        em.op("act", lambda e, ps=ps, z=z, h2=h2: e.activation(z[:, h2 * 512:(h2 + 1) * 512], ps[:], AF.Gelu),
                              r=[ps], w=[z])
                    s6 = st6.get()
                    zn = zvn.get()
                    em.op("dve", lambda e, z=z, s6=s6: e.reduce_sum(s6[:, 0:1], z[:], axis=AX.X), r=[z], w=[s6])
                    em.op("dve", lambda e, s6=s6: e.tensor_scalar(s6[:, 1:2], s6[:, 0:1], -1.0 / D, None, ALU.mult), r=[s6], w=[s6])
                    em.op("dve", lambda e, z=z, s6=s6: e.tensor_scalar(z[:], z[:], s6[:, 1:2], None, ALU.add), r=[z, s6], w=[z])
                    em.op("act", lambda e, z=z, zn=zn, s6=s6: e.activation(zn[:], z[:], AF.Square, accum_out=s6[:, 2:3]),
                          r=[z], w=[zn, s6])
                    em.op("dve", lambda e, s6=s6: e.tensor_scalar(s6[:, 3:4], s6[:, 2:3], 1.0 / D, EPS, ALU.mult, ALU.add), r=[s6], w=[s6])
                    em.op("act", lambda e, s6=s6: e.activation(s6[:, 4:5], s6[:, 3:4], AF.Sqrt), r=[s6], w=[s6])
                    em.op("dve", lambda e, s6=s6: e.reciprocal(s6[:, 5:6], s6[:, 4:5]), r=[s6], w=[s6])
                    em.op("dve", lambda e, z=z, zn=zn, s6=s6: e.scalar_tensor_tensor(zn[:], z[:], s6[:, 5:6], lngR[:], ALU.mult, ALU.mult),
                          r=[z, s6, lngR], w=[zn])
                    em.op("pool", lambda e, zn=zn: e.tensor_tensor(zn[:], zn[:], lnbR[:], ALU.add), r=[zn, lnbR], w=[zn])
                    pa, pb = pp.get(), pp.get()
                    for h in range(8):
                        ps = pa if h < 4 else pb
                        em.op("pe", lambda e, h=h, ps=ps, zn=zn: e.matmul(ps[:, (h % 4) * 128:(h % 4 + 1) * 128],
                                                                         zn[:, h * 128:(h + 1) * 128], wsT[:, h, :], start=True, stop=True),
                              r=[zn, wsT], w=[ps])
                    for half, ps in ((0, pa), (1, pb)):
                        for hh in range(4):
                            h = half * 4 + hh
                            tt = tmpr.get()
                            em.op("dve", lambda e, ps=ps, tt=tt, h=h, hh=hh: e.tensor_tensor(
                                tt[:, 0:128], ps[:, hh * 128:(hh + 1) * 128], bsR[:, h, :], ALU.add), r=[ps, bsR], w=[tt])
                            em.op("pool", lambda e, tt=tt, h=h, gate=gate, zu=zu, tl=tl: e.tensor_tensor(
                                gate[:, h, tl * 128:(tl + 1) * 128], zu[:, h, tl * 128:(tl + 1) * 128], tt[:, 0:128], ALU.mult),
                                r=[tt, zu], w=[gate])
                self.linear_fm(wout_ap, 0, D, gate, KC, TB, wrot, pp, resid)

            if not DBG.get('skip_cross'):
                seg = b // blk_per_seg
                if b % blk_per_seg == 0:
                    em.dma(memt[:], mT[seg].rearrange("(c p) t -> p c t", p=128), r=[mTr], w=[memt])

                    def epi_k(m, ps):
                        em.op("act", lambda e: e.copy(kt[:, m, :], ps[:, 0:MEM]), r=[ps], w=[kt])
                    self.linear_fm(wkv, 0, D, memt, KC, MEM, wrot, pp, epi_k)
                    wv = wkv.rearrange("(c p) n -> p c n", p=128)
                    for h2 in range(2):
                        wt = wrot.get()
                        wtv = wt[:].rearrange("p (c n) -> p c n", c=KC)
                        em.dma(wtv, wv[:, :, D + h2 * 512:D + (h2 + 1) * 512], w=[wt])
                        for mc in range(2):
                            ps = pp.get()
                            for k in range(KC):
                                em.op("pe", lambda e, k=k, ps=ps, wtv=wtv, mc=mc: e.matmul(
                                    ps[:], memt[:, k, mc * 128:(mc + 1) * 128], wtv[:, k, :], start=(k == 0), stop=(k == KC - 1)),
                                    r=[memt, wt], w=[ps])
                            em.op("act", lambda e, ps=ps, mc=mc, h2=h2: e.copy(vt[:, mc, h2 * 512:(h2 + 1) * 512], ps[:]), r=[ps], w=[vt])
                self.dump('d_kt', kt, [128, KC, MEM])
                self.dump('d_vt', vt, [128, 2, D])
                self.rmsnorm(x, xn_t, sq_t, gc, li, ones, pp, rstd, TB)
                self.dump('d_xn', xn_t, [128, KC, TB])
                q = a8.get()

                def epi_q(m, ps, q=q):
                    em.op("act", lambda e: e.copy(q[:, m, :], ps[:, 0:TB]), r=[ps], w=[q])
                self.linear_fm(wq, 0, D, xn_t, KC, TB, wrot, pp, epi_q)
                o = a8.get()

                def att_a(h, q=q):
                    pT = prot.get()
                    for mc in range(2):
                        ps = pp.get()
                        for dc in range(2):
                            em.op("pe", lambda e: e.matmul(
                                ps[:, 0:TB], kt[:, 2 * h + dc, mc * 128:(mc + 1) * 128], q[:, 2 * h + dc, :], start=(dc == 0), stop=(dc == 1)),
                                r=[kt, q], w=[ps])
                        em.op("act", lambda e: e.activation(pT[:, mc, :], ps[:, 0:TB], AF.Exp, scale=1.0 / 16.0),
                              r=[ps], w=[pT])
                    return pT

                def att_b(h, pT, o=o):
                    ps = pp.get()
                    for mc in range(2):
                        em.op("pe", lambda e: e.matmul(ps[:, 0:TB], ones[:], pT[:, mc, :], start=(mc == 0), stop=(mc == 1)),
                              r=[ones, pT], w=[ps])
                    rd = rden.get()
                    em.op("dve", lambda e: e.reciprocal(rd[:], ps[:, 0:TB]), r=[ps], w=[rd])
                    for dc in range(2):
                        ps2 = pp.get()
                        for mc in range(2):
                            em.op("pe", lambda e: e.matmul(
                                ps2[:, 0:TB], vt[:, mc, (2 * h + dc) * 128:(2 * h + dc + 1) * 128], pT[:, mc, :], start=(mc == 0), stop=(mc == 1)),
                                r=[vt, pT], w=[ps2])
                        em.op("dve", lambda e: e.tensor_tensor(o[:, 2 * h + dc, :], ps2[:, 0:TB], rd[:], ALU.mult),
                              r=[ps2, rd], w=[o], ns=(dc > 0))

                pTs = att_a(0)
                for h in range(4):
                    pTn = att_a(h + 1) if h < 3 else None
                    att_b(h, pTs)
                    pTs = pTn
                self.dump('d_q', q, [128, KC, TB])
                self.dump('d_o', o, [128, KC, TB])
                self.linear_fm(wo, 0, D, o, KC, TB, wrot, pp, resid)

            if not DBG.get('skip_ffn'):
                self.rmsnorm(x, xn_t, sq_t, gf, li, ones, pp, rstd, TB)
                for g0 in range(0, FC, 4):
                    gn = min(4, FC - g0)
                    wg = wrot.get()
                    wgv = wg[:, 0:KC * gn * 128].rearrange("p (c n) -> p c n", c=KC)
                    em.dma(wgv, wg_t[:, :, g0 * 128:(g0 + gn) * 128], w=[wg])
                    wu = wrot.get()
                    wuv = wu[:, 0:KC * gn * 128].rearrange("p (c n) -> p c n", c=KC)
                    em.dma(wuv, wu_t[:, :, g0 * 128:(g0 + gn) * 128], w=[wu])
                    for m in range(gn):
                        pg, pu = pp.get(), pp.get()
                        for k in range(KC):
                            em.op("pe", lambda e, k=k, m=m, pg=pg, wgv=wgv: e.matmul(
                                pg[:, 0:TB], wgv[:, k, m * 128:(m + 1) * 128], xn_t[:, k, :], start=(k == 0), stop=(k == KC - 1)),
                                r=[wg, xn_t], w=[pg])
                        for k in range(KC):
                            em.op("pe", lambda e, k=k, m=m, pu=pu, wuv=wuv: e.matmul(
                                pu[:, 0:TB], wuv[:, k, m * 128:(m + 1) * 128], xn_t[:, k, :], start=(k == 0), stop=(k == KC - 1)),
                                r=[wu, xn_t], w=[pu])
                        tt = tmpr.get()
                        em.op("act", lambda e, pg=pg, tt=tt: e.activation(tt[:], pg[:, 0:TB], AF.Silu), r=[pg], w=[tt])
                        em.op("dve", lambda e, pu=pu, tt=tt, mm=g0 + m: e.tensor_tensor(hbuf[:, mm, :], tt[:], pu[:, 0:TB], ALU.mult),
                              r=[tt, pu], w=[hbuf], ns=True)
                self.linear_fm(wdn, 0, D, hbuf, FC, TB, wrot, pp, resid, group=128)
            self.defer(lambda x=x, t0=t0: em.dma(xTv[:, :, t0:t0 + TB], x[:], r=[x], w=[xTr]))

    def pass_final(self):
        em = self.em
        self.reset()
        TB = 256
        xT, xTr = self.scratch("xT", [D, self.Tn])
        xTv = xT.rearrange("(c p) t -> p c t", p=128)
        ident = self.load_const("ident")
        ones = self.ones_tile()
        gfin = self.load_cols(self.W["g_final"], KC, "gfin")
        xrot = self.rot([128, KC, TB], 2, "x")
        xnr = self.rot([128, KC, TB], 2, "xnf")
        sq_t = self.alloc([128, KC, TB], "sq", mm=True)
        rstd = self.alloc([128, TB], "rstd")
        yo = self.rot([128, D], 3, "yo")
        pp = self.psum(8)
        for b in range(self.Tn // TB):
            t0 = b * TB
            x = xrot.get()
            em.dma(x[:], xTv[:, :, t0:t0 + TB], r=[xTr], w=[x])
            xn = xnr.get()
            self.rmsnorm(x, xn, sq_t, gfin, 0, ones, pp, rstd, TB)
            for j in range(TB // 128):
                pa, pb = pp.get(), pp.get()
                y = yo.get()
                for c in range(KC):
                    ps = pa if c < 4 else pb
                    em.op("pe", lambda e, c=c, ps=ps, xn=xn, j=j: e.transpose(ps[:, (c % 4) * 128:(c % 4 + 1) * 128],
                                                                             xn[:, c, j * 128:(j + 1) * 128], ident[:]),
                          r=[xn, ident], w=[ps])
                em.op("act", lambda e, y=y, pa=pa: e.copy(y[:, 0:512], pa[:]), r=[pa], w=[y])
                em.op("dve", lambda e, y=y, pb=pb: e.tensor_copy(y[:, 512:1024], pb[:]), r=[pb], w=[y])
                yr = Res("y")
                em.dma(self.y_out[t0 + j * 128:t0 + (j + 1) * 128, :], y[:], r=[y], w=[yr])

    def mixer_a(self, li, j):
        self.pass_a1(li, j)
        self.pass_a2(li, j)

    def pass_a1(self, li, j):
        em = self.em
        self.reset()
        TB = 512
        Tn = self.Tn
        xT, xTr = self.scratch("xT", [D, Tn])
        xTv = xT.rearrange("(c p) t -> p c t", p=128)
        w_in = self.Wb["a_w_in"][j]
        pads = [64 * d for (_, d) in A_GROUPS]
        QT, KT, VV = [], [], []
        for g in range(3):
            QT.append(self.scratch("QT%d" % g, [D, Tn + 2 * pads[g]], BF16))
            KT.append(self.scratch("KT%d" % g, [D, Tn + 2 * pads[g]], BF16))
            VV.append(self.scratch("VV%d" % g, [Tn + 2 * pads[g], D], BF16))
        ones = self.ones_tile()
        rblk = self.load_const("rblk", mm=True)
        gm = self.load_cols(self.W["g_mix"].rearrange("l d -> (l d)"), 4 * KC, "gm")
        zt = self.alloc([128, 1024], "zero", mm=True)
        em.op("pool", lambda e: e.memset(zt[:], 0.0), w=[zt])
        for g in range(3):
            pad = pads[g]
            for side in range(2):
                c0 = 0 if side == 0 else pad + Tn
                for c in range(KC):
                    em.dma(KT[g][0][c * 128:(c + 1) * 128, c0:c0 + pad], zt[:, 0:pad], r=[zt], w=[KT[g][1]])
                for r0 in range(0, pad, 128):
                    rn = min(128, pad - r0)
                    em.dma(VV[g][0][c0 + r0:c0 + r0 + rn, :], zt[0:rn, :], r=[zt], w=[VV[g][1]])
        xrot = self.rot([128, KC, TB], 2, "x")
        xn_t = self.alloc([128, KC, TB], "xn", mm=True)
        sq_t = self.alloc([128, KC, TB], "sq", mm=True)
        rstd = self.alloc([128, TB], "rstd")
        wrot = self.rot([128, 4096], 9, "w", mm=True)
        cbr = self.rot([128, TB], 2, "cb")
        sbr = self.rot([128, TB], 2, "sb")
        qraw = self.rot([128, TB], 3, "qraw", mm=True)
        t1r = self.rot([128, TB], 3, "t1")
        t2r = self.rot([128, TB], 3, "t2")
        stg = self.rot([128, KC, TB], 3, "stg", mm=True)
        vst = self.rot([128, D], 6, "vst", mm=True)
        pp = self.psum(8)
        for b in range(Tn // TB):
            t0 = b * TB
            x = xrot.get()
            em.dma(x[:], xTv[:, :, t0:t0 + TB], r=[xTr], w=[x])
            self.rmsnorm(x, xn_t, sq_t, gm, li, ones, pp, rstd, TB)
            cb, sb = cbr.get(), sbr.get()
            em.dma(cb[:], self.C["ropec"][:, t0:t0 + TB], w=[cb])
            em.dma(sb[:], self.C["ropes"][:, t0:t0 + TB], w=[sb])
            for g in range(3):
                pad = pads[g]
                for jj in range(2):
                    st_ = stg.get()

                    pend = []

                    def rope(m, qr, st_=st_, cb=cb, sb=sb):
                        p2 = pp.get()
                        em.op("pe", lambda e: e.matmul(p2[:, 0:TB], rblk[:], qr[:], start=True, stop=True), r=[rblk, qr], w=[p2])
                        t1, t2 = t1r.get(), t2r.get()
                        em.op("pool", lambda e: e.tensor_tensor(t1[:], qr[:], cb[:], ALU.mult), r=[qr, cb], w=[t1])
                        em.op("dve", lambda e: e.tensor_tensor(t2[:], p2[:, 0:TB], sb[:], ALU.mult), r=[p2, sb], w=[t2])
                        em.op("pool", lambda e: e.tensor_tensor(st_[:, m, :], t1[:], t2[:], ALU.add), r=[t1, t2], w=[st_])

                    def epi(m, ps, pend=pend, rope=rope):
                        qr = qraw.get()
                        em.op("act", lambda e: e.copy(qr[:], ps[:, 0:TB]), r=[ps], w=[qr])
                        if pend:
                            pend.pop()()
                        pend.append(lambda: rope(m, qr))
                    self.linear_fm(w_in, g * 3072 + jj * 1024, 1024, xn_t, KC, TB, wrot, pp, epi)
                    while pend:
                        pend.pop()()
                    dst, dres = (QT[g] if jj == 0 else KT[g])
                    self.defer(lambda dst=dst, dres=dres, st_=st_, pad=pad, t0=t0: em.dma(
                        dst.rearrange("(c p) t -> p c t", p=128)[:, :, pad + t0:pad + t0 + TB], st_[:], r=[st_], w=[dres]))
                wv = w_in.rearrange("(c p) n -> p c n", p=128)
                wts = []
                for h2 in range(2):
                    wt = wrot.get()
                    wtv = wt[:].rearrange("p (c n) -> p c n", c=KC)
                    c0 = g * 3072 + 2048 + h2 * 512
                    em.dma(wtv, wv[:, :, c0:c0 + 512], w=[wt])
                    wts.append((wt, wtv))
                for tl in range(TB // 128):
                    vs = vst.get()
                    for h2 in range(2):
                        wt, wtv = wts[h2]
                        ps = pp.get()
                        for k in range(KC):
                            em.op("pe", lambda e, k=k: e.matmul(ps[:], xn_t[:, k, tl * 128:(tl + 1) * 128], wtv[:, k, :],
                                                                start=(k == 0), stop=(k == KC - 1)), r=[xn_t, wt], w=[ps])
                        em.op("act", lambda e: e.copy(vs[:, h2 * 512:(h2 + 1) * 512], ps[:]), r=[ps], w=[vs])
                    r0 = pad + t0 + tl * 128
                    self.defer(lambda g=g, r0=r0, vs=vs: em.dma(VV[g][0][r0:r0 + 128, :], vs[:], r=[vs], w=[VV[g][1]]))

    def pass_a2(self, li, j):
        em = self.em
        self.reset()
        Tn = self.Tn
        RG = 2048
        pads = [64 * d for (_, d) in A_GROUPS]
        QT, KT, VV = [], [], []
        for g in range(3):
            QT.append(self.scratch("QT%d" % g, [D, Tn + 2 * pads[g]], BF16))
            KT.append(self.scratch("KT%d" % g, [D, Tn + 2 * pads[g]], BF16))
            VV.append(self.scratch("VV%d" % g, [Tn + 2 * pads[g], D], BF16))
        OT, OTr = self.scratch("OT", [D, Tn], BF16)
        Ur = self.rot([128, RG], 2, "U")
        Lr = self.rot([128, RG], 2, "L")
        rl = self.alloc([128, RG], "rl")
        otr = self.rot([128, RG], 2, "ot", mm=True)
        identb = self.load_const("identb", mm=True)
        qz0 = self.rot([128, RG], 2, "qz0", mm=True)
        qz1 = self.rot([128, RG], 2, "qz1", mm=True)
        ktr = self.rot([128, 2 * RG], 2, "kt", mm=True)
        vzr = self.rot([128, 2, 128], 8, "vz", mm=True)
        mkr = self.rot([128, 2, 128], 3, "mk", mm=True)
        pMr = self.rot([128, 4, 128], 6, "pM", mm=True)
        ob = [self.alloc([128, 128], "ob%d" % e, mm=True) for e in range(2)]
        for e_ in range(2):
            em.dma(ob[e_][:], self.C["obm"][e_], w=[ob[e_]])
        for tl in qz0.tiles + qz1.tiles:
            self.zero_fill(tl, RG)
        pss = self.psum(3)
        psu = self.psum(3)
        psl = self.psum(2)
        nv = 0
        for R in range(Tn // RG):
            for hc in range(KC):
                U, L, ot = Ur.get(), Lr.get(), otr.get()
                em.op("pool", lambda e: e.memset(U[:], 0.0), w=[U])
                em.op("pool", lambda e: e.memset(L[:], 0.0), w=[L])
                pend = []
                for g, (window, d) in enumerate(A_GROUPS):
                    pad = pads[g]
                    span = 128 * d
                    ntr = RG // span
                    for tt in range(ntr):
                        t = R * ntr + tt
                        base = t * span
                        q0, q1, kt = qz0.get(), qz1.get(), ktr.get()
                        em.dma(q0[0:64, 0:span], QT[g][0][hc * 128:hc * 128 + 64, pad + base:pad + base + span], r=[QT[g][1]], w=[q0])
                        em.dma(q1[64:128, 0:span], QT[g][0][hc * 128 + 64:hc * 128 + 128, pad + base:pad + base + span], r=[QT[g][1]], w=[q1])
                        em.dma(kt[:, 0:2 * span], KT[g][0][hc * 128:(hc + 1) * 128, pad + base - 64 * d:pad + base + 192 * d],
                               r=[KT[g][1]], w=[kt])
                        mk = mkr.get()
                        em.dma(mk[:], self.C["abias%d" % g][t].rearrange("x k q -> k x q"), w=[mk])
                        qz = (q0, q1)
                        for r in range(d):
                            vz = vzr.get()
                            rs = pad + base - 64 * d + r
                            src = VV[g][0][rs:rs + 255 * d + 1:d, hc * 128:(hc + 1) * 128].rearrange("(x k) c -> k x c", x=2)
                            em.dma(vz[:], src, r=[VV[g][1]], w=[vz], eng=("sp", "act", "pool")[nv % 3])
                            nv += 1
                            ps = pss.get()
                            for e_ in range(2):
                                for X in range(2):
                                    k0 = 128 * d * X + r
                                    em.op("pe", lambda e: e.matmul(ps[:, (e_ * 2 + X) * 128:(e_ * 2 + X + 1) * 128],
                                                                   kt[:, k0:k0 + 127 * d + 1:d], qz[e_][:, r:r + 127 * d + 1:d], start=True, stop=False),
                                          r=[kt, qz[e_]], w=[ps])
                                    em.op("pe", lambda e: e.matmul(ps[:, (e_ * 2 + X) * 128:(e_ * 2 + X + 1) * 128],
                                                                   identb[:], mk[:, X, :], start=False, stop=True),
                                          r=[identb, mk], w=[ps])
                            pM = pMr.get()
                            em.op("act", lambda e: e.activation(pM[:].rearrange("p a b -> p (a b)"), ps[:], AF.Exp, scale=0.125),
                                  r=[ps], w=[pM])
                            off = tt * span + r
                            pend.append((vz, pM, ob, psu, psl, U, L, off, d))
                            if len(pend) > 2:
                                self._a2_pv(*pend.pop(0))
                while pend:
                    self._a2_pv(*pend.pop(0))
                em.op("dve", lambda e: e.reciprocal(rl[:], L[:]), r=[L], w=[rl])
                em.op("pool", lambda e: e.tensor_tensor(ot[:], U[:], rl[:], ALU.mult), r=[U, rl], w=[ot])
                em.dma(OT[hc * 128:(hc + 1) * 128, R * RG:(R + 1) * RG], ot[:], r=[ot], w=[OTr])

    def _a2_pv(self, vz, pM, ob, psu, psl, U, L, off, d):
        em = self.em
        pus = []
        for e_ in range(2):
            pu = psu.get()
            for X in range(2):
                em.op("pe", lambda e: e.matmul(pu[:, 0:128], vz[:, X, :], pM[:, e_ * 2 + X, :],
                                               start=(X == 0), stop=(X == 1)), r=[vz, pM], w=[pu])
            pus.append(pu)
        pl = psl.get()
        n = 0
        for e_ in range(2):
            for X in range(2):
                em.op("pe", lambda e: e.matmul(pl[:, 0:128], ob[e_][:], pM[:, e_ * 2 + X, :],
                                               start=(n == 0), stop=(n == 3)), r=[ob[e_], pM], w=[pl])
                n += 1
        for e_ in range(2):
            rows = slice(e_ * 64, (e_ + 1) * 64)
            em.op("dve", lambda e: e.tensor_tensor(U[rows, off:off + 127 * d + 1:d], U[rows, off:off + 127 * d + 1:d],
                                                   pus[e_][rows, 0:128], ALU.add), r=[U, pus[e_]], w=[U], ns=True)
        em.op("dve", lambda e: e.tensor_tensor(L[:, off:off + 127 * d + 1:d], L[:, off:off + 127 * d + 1:d],
                                               pl[:, 0:128], ALU.add), r=[L, pl], w=[L], ns=True)

    def mixer_b(self, li, j):
        stop = DBG.get("b_stop", 99)
        AD = [self.scratch("AD%d" % i, [128, 2, 128, D], BF16) for i in range(3)]
        VD = self.scratch("VD", [self.Tn, D], BF16)
        HD = self.scratch("HD", [self.Tn, 2, D], BF16)
        hv = HD[0].rearrange("(i b) r c -> b r i c", b=128)
        steps = [
            lambda: self.pass_b0(j),
            lambda: self.pass_b1(li, j),
            lambda: self.pass_b2(j),
            lambda: self.pass_f1(VD[0].rearrange("(i b) c -> b i c", b=128), VD[1], "wselc", "wsels", "twi", AD[0]),
            lambda: self.pass_f1(hv[:, 0], HD[1], "hselc", "hsels", "twi", AD[1]),
            lambda: self.pass_f1(hv[:, 1], HD[1], "hselc", "hselsn", "twin", AD[2]),
            lambda: self.pass_f2h(j),
            lambda: self.pass_mid(),
            lambda: self.pass_i2(j),
        ]
        for i, f in enumerate(steps):
            if i < stop:
                f()

    def sin_layer(self, ps, hid, freq, fb, tmpr, n):
        em = self.em
        a, c = tmpr.get(), tmpr.get()
        em.op("dve", lambda e: e.tensor_scalar(a[0:64, 0:n], ps[0:64, 0:n], freq[0:64, 0:1], fb[0:64, 0:1], ALU.mult, ALU.add),
              r=[ps, freq, fb], w=[a])
        ci = c[0:64, 0:n].bitcast(mybir.dt.int32)
        em.op("dve", lambda e: e.tensor_copy(ci, a[0:64, 0:n]), r=[a], w=[c])
        em.op("dve", lambda e: e.tensor_copy(c[0:64, 0:n], ci), r=[c], w=[c])
        em.op("dve", lambda e: e.tensor_tensor(a[0:64, 0:n], a[0:64, 0:n], c[0:64, 0:n], ALU.subtract), r=[a, c], w=[a])
        em.op("act", lambda e: e.activation(hid[0:64, 0:n], a[0:64, 0:n], AF.Sin, scale=6.283179), r=[a], w=[hid])

    def pass_b0(self, j):
        em = self.em
        self.reset()
        Tn, nt = self.Tn, self.nt
        TBf = 512
        HD, HDr = self.scratch("HD", [Tn, 2, D], BF16)
        RN, RNr = self.scratch("RN", [128, D])
        W = self.W
        ones = self.alloc([128, 128], "onesf")
        em.op("dve", lambda e: e.memset(ones[:], 1.0), w=[ones])
        w1 = self.alloc([128, 64], "w1")
        em.dma(w1[0:33, :], W["b_f_w1"][j], w=[w1])
        w2 = self.alloc([128, 64], "w2")
        em.dma(w2[0:64, :], W["b_f_w2"][j], w=[w2])
        w3 = self.alloc([128, 64], "w3")
        em.dma(w3[0:64, :], W["b_f_w3"][j], w=[w3])
        wout = self.alloc([128, 2 * D], "wout")
        em.dma(wout[0:64, :], W["b_f_wout"][j], w=[wout])
        freq = self.alloc([128, 1], "freq")
        em.dma(freq[0:64, :], W["b_f_freq"][j].rearrange("(p o) -> p o", o=1), w=[freq], slow=True)
        fbs = []
        for nm in ("b_f_b1", "b_f_b2", "b_f_b3"):
            fb = self.alloc([128, 1], nm)
            em.dma(fb[0:64, :], W[nm][j].rearrange("(p o) -> p o", o=1), w=[fb], slow=True)
            em.op("dve", lambda e: e.tensor_tensor(fb[0:64, :], fb[0:64, :], freq[0:64, :], ALU.mult), r=[fb, freq], w=[fb])
            em.op("dve", lambda e: e.tensor_scalar(fb[0:64, :], fb[0:64, :], 1.0 / TWO_PI, None, ALU.mult), r=[fb], w=[fb])
            fbs.append(fb)
        em.op("dve", lambda e: e.tensor_scalar(freq[0:64, :], freq[0:64, :], 1.0 / TWO_PI, None, ALU.mult), r=[freq] + fbs, w=[freq])
        absd = self.load_const("absdelta")
        negt = self.load_const("negt")
        vmask = self.load_const("vmask")
        zr = self.rot([128, TBf], 2, "z")
        hidr = self.rot([128, TBf], 3, "hidf")
        tmpr = self.rot([128, TBf], 3, "tmp")
        decr = self.rot([128, D], 1, "dec")
        hr = self.rot([128, 2, D], 2, "hflt")
        abr = self.rot([128, 2, D], 1, "abf")
        hbr = self.rot([128, 2, D], 2, "hb", mm=True)
        nps = self.psum(2).tiles
        pp = self.psum(6)
        first = True
        for b in range(Tn // TBf):
            t0 = b * TBf
            z = zr.get()
            em.dma(z[0:33, :], self.C["zT"][:, t0:t0 + TBf], w=[z])
            ps = pp.get()
            em.op("pe", lambda e: e.matmul(ps[0:64, :], w1[0:33, :], z[0:33, :], start=True, stop=True), r=[w1, z], w=[ps])
            h1 = hidr.get()
            self.sin_layer(ps, h1, freq, fbs[0], tmpr, TBf)
            ps = pp.get()
            em.op("pe", lambda e: e.matmul(ps[0:64, :], w2[0:64, :], h1[0:64, :], start=True, stop=True), r=[w2, h1], w=[ps])
            h2 = hidr.get()
            self.sin_layer(ps, h2, freq, fbs[1], tmpr, TBf)
            ps = pp.get()
            em.op("pe", lambda e: e.matmul(ps[0:64, :], w3[0:64, :], h2[0:64, :], start=True, stop=True), r=[w3, h2], w=[ps])
            h3 = hidr.get()
            self.sin_layer(ps, h3, freq, fbs[2], tmpr, TBf)
            for tl in range(TBf // 128):
                i = b * (TBf // 128) + tl
                dec = decr.get()
                em.op("act", lambda e: e.activation(dec[:], absd[:], AF.Exp, scale=negt[:, i:i + 1]), r=[absd, negt], w=[dec])
                h = hr.get()
                for cg in range(4):
                    ps = pp.get()
                    em.op("pe", lambda e: e.matmul(ps[:], h3[0:64, tl * 128:(tl + 1) * 128], wout[0:64, cg * 512:(cg + 1) * 512],
                                                   start=True, stop=True), r=[h3, wout], w=[ps])
                    em.op("dve", lambda e: e.tensor_tensor(h[:, cg // 2, (cg % 2) * 512:(cg % 2 + 1) * 512], ps[:],
                                                           dec[:, (cg % 2) * 512:(cg % 2 + 1) * 512], ALU.mult), r=[ps, dec], w=[h])
                if i == 0:
                    em.op("dve", lambda e: e.tensor_tensor(h[0:1, 0, :], h[0:1, 0, :], h[0:1, 1, :], ALU.add), r=[h], w=[h])
                    em.op("dve", lambda e: e.memset(h[0:1, 1, :], 0.0), w=[h])
                ab = abr.get()
                em.op("act", lambda e: e.activation(ab[:].rearrange("p a b -> p (a b)"), h[:].rearrange("p a b -> p (a b)"),
                                                    AF.Abs, scale=vmask[:, i:i + 1]), r=[h, vmask], w=[ab])
                last = (i == nt - 1)
                for dr in range(2):
                    for hf in range(2):
                        em.op("pe", lambda e: e.matmul(nps[hf][:], ones[:], ab[:, dr, hf * 512:(hf + 1) * 512],
                                                       start=(first and dr == 0), stop=(last and dr == 1)), r=[ones, ab], w=[nps[hf]])
                first = False
                hb = hbr.get()
                em.op("pool", lambda e: e.tensor_copy(hb[:], h[:]), r=[h], w=[hb])
                em.dma(HD[i * 128:(i + 1) * 128], hb[:], r=[hb], w=[HDr])
        rn = decr.get()
        for hf in range(2):
            em.op("dve", lambda e: e.reciprocal(rn[:, hf * 512:(hf + 1) * 512], nps[hf][:]), r=[nps[hf]], w=[rn])
        em.dma(RN, rn[:], r=[rn], w=[RNr])

    def pass_b1(self, li, j):
        em = self.em
        self.reset()
        TB = 512
        Tn = self.Tn
        xT, xTr = self.scratch("xT", [D, Tn])
        xTv = xT.rearrange("(c p) t -> p c t", p=128)
        U0, U0r = self.scratch("U0T", [D, Tn + 2])
        U12, U12r = self.scratch("U12", [Tn + 2, 2 * D], BF16)
        w_in = self.Wb["b_w_in"][j]
        ones = self.ones_tile()
        gm = self.load_cols(self.W["g_mix"].rearrange("l d -> (l d)"), 4 * KC, "gm")
        zt = self.alloc([128, 8], "zero")
        em.op("pool", lambda e: e.memset(zt[:], 0.0), w=[zt])
        ztb = self.alloc([128, 2 * D], "zerob", mm=True)
        em.op("pool", lambda e: e.memset(ztb[:], 0.0), w=[ztb])
        for c in range(KC):
            em.dma(U0[c * 128:(c + 1) * 128, 0:1], zt[:, 0:1], r=[zt], w=[U0r], slow=True)
            em.dma(U0[c * 128:(c + 1) * 128, Tn + 1:Tn + 2], zt[:, 0:1], r=[zt], w=[U0r], slow=True)
        em.dma(U12[0:1, :], ztb[0:1, :], r=[ztb], w=[U12r])
        em.dma(U12[Tn + 1:Tn + 2, :], ztb[0:1, :], r=[ztb], w=[U12r])
        xrot = self.rot([128, KC, TB], 2, "x")
        xn_t = self.alloc([128, KC, TB], "xn", mm=True)
        sq_t = self.alloc([128, KC, TB], "sq", mm=True)
        rstd = self.alloc([128, TB], "rstd")
        wrot = self.rot([128, 4096], 8, "w", mm=True)
        stg = self.rot([128, KC, TB], 1, "stg")
        ust = self.rot([128, 2 * D], 8, "ust", mm=True)
        pp = self.psum(8)
        wv = w_in.rearrange("(c p) n -> p c n", p=128)
        for b in range(Tn // TB):
            t0 = b * TB
            x = xrot.get()
            em.dma(x[:], xTv[:, :, t0:t0 + TB], r=[xTr], w=[x])
            self.rmsnorm(x, xn_t, sq_t, gm, li, ones, pp, rstd, TB)
            st_ = stg.get()

            def epi(m, ps, st_=st_):
                em.op("act", lambda e: e.copy(st_[:, m, :], ps[:, 0:TB]), r=[ps], w=[st_])
            self.linear_fm(w_in, 0, D, xn_t, KC, TB, wrot, pp, epi)
            self.defer(lambda st_=st_, t0=t0: em.dma(U0.rearrange("(c p) t -> p c t", p=128)[:, :, 1 + t0:1 + t0 + TB], st_[:], r=[st_], w=[U0r]))
            us = [ust.get() for _ in range(TB // 128)]
            for cg in range(4):
                wt = wrot.get()
                wtv = wt[:].rearrange("p (c n) -> p c n", c=KC)
                em.dma(wtv, wv[:, :, D + cg * 512:D + (cg + 1) * 512], w=[wt])
                for tl in range(TB // 128):
                    ps = pp.get()
                    for k in range(KC):
                        em.op("pe", lambda e: e.matmul(ps[:], xn_t[:, k, tl * 128:(tl + 1) * 128], wtv[:, k, :],
                                                       start=(k == 0), stop=(k == KC - 1)), r=[xn_t, wt], w=[ps])
                    if (cg + tl) % 2 == 0:
                        em.op("act", lambda e: e.copy(us[tl][:, cg * 512:(cg + 1) * 512], ps[:]), r=[ps], w=[us[tl]])
                    else:
                        em.op("dve", lambda e: e.tensor_copy(us[tl][:, cg * 512:(cg + 1) * 512], ps[:]), r=[ps], w=[us[tl]])
            for tl in range(TB // 128):
                r0 = 1 + t0 + tl * 128
                self.defer(lambda r0=r0, u=us[tl]: em.dma(U12[r0:r0 + 128, :], u[:], r=[u], w=[U12r]))

    def pass_b2(self, j):
        em = self.em
        self.reset()
        Tn, nt = self.Tn, self.nt
        U12, U12r = self.scratch("U12", [Tn + 2, 2 * D], BF16)
        VD, VDr = self.scratch("VD", [Tn, D], BF16)
        cw = self.W["b_conv_w"][j]
        w0R = self.load_rep(cw[0, D:3 * D], 2 * D, "w0R")
        w1R = self.load_rep(cw[1, D:3 * D], 2 * D, "w1R")
        w2R = self.load_rep(cw[2, D:3 * D], 2 * D, "w2R")
        bR = self.load_rep(self.W["b_conv_b"][j, D:3 * D], 2 * D, "bR")
        mp = self.load_const("mp")
        mn = self.load_const("mn")
        ucr = self.rot([128, 2 * D], 3, "uc", mm=True)
        upr = self.rot([128, 2 * D], 3, "up", mm=True)
        unr = self.rot([128, 2 * D], 3, "un", mm=True)
        cr = self.rot([128, 2 * D], 2, "c", mm=True)
        tr = self.rot([128, 2 * D], 2, "t", mm=True)
        vr = self.rot([128, D], 2, "v", mm=True)
        def b2_load(i):
            uc, up, un = ucr.get(), upr.get(), unr.get()
            em.dma(uc[:], U12[1 + i * 128:1 + (i + 1) * 128, :], r=[U12r], w=[uc])
            em.dma(up[:], U12[i * 128:(i + 1) * 128, :], r=[U12r], w=[up])
            em.dma(un[:], U12[2 + i * 128:2 + (i + 1) * 128, :], r=[U12r], w=[un])
            return uc, up, un

        def b2_body(i, tl):
            uc, up, un = tl
            c = cr.get()
            em.op("dve", lambda e: e.tensor_tensor(c[:], uc[:], w1R[:], ALU.mult), r=[uc, w1R], w=[c])
            em.op("pool", lambda e: e.tensor_tensor(c[:], c[:], bR[:], ALU.add), r=[c, bR], w=[c])
            t1 = tr.get()
            em.op("pool", lambda e: e.tensor_tensor(t1[:], up[:], w0R[:], ALU.mult), r=[up, w0R], w=[t1])
            em.op("dve", lambda e: e.scalar_tensor_tensor(c[:], t1[:], mp[:, i:i + 1], c[:], ALU.mult, ALU.add), r=[t1, mp, c], w=[c])
            t2 = tr.get()
            em.op("pool", lambda e: e.tensor_tensor(t2[:], un[:], w2R[:], ALU.mult), r=[un, w2R], w=[t2])
            em.op("dve", lambda e: e.scalar_tensor_tensor(c[:], t2[:], mn[:, i:i + 1], c[:], ALU.mult, ALU.add), r=[t2, mn, c], w=[c])
            v = vr.get()
            em.op("pool", lambda e: e.tensor_tensor(v[:], c[:, D:2 * D], c[:, 0:D], ALU.mult), r=[c], w=[v])
            em.dma(VD[i * 128:(i + 1) * 128, :], v[:], r=[v], w=[VDr])

        self.prefetch_loop(nt, b2_load, b2_body)

    def pass_f1(self, src_b, src_res, selc_n, sels_n, twi_n, AD):
        em = self.em
        self.reset()
        nt = self.nt
        ADa, ADr = AD
        selc = self.alloc([128, 128], "selc", mm=True)
        self.zero_fill(selc, 128)
        em.dma(selc[0:nt, :], self.C[selc_n], w=[selc])
        sels = self.alloc([128, 128], "sels", mm=True)
        self.zero_fill(sels, 128)
        em.dma(sels[0:nt, :], self.C[sels_n], w=[sels])
        twr = self.load_const("twr")
        twi = self.load_const(twi_n)
        sr = self.rot([128, D], 3, "s", mm=True)
        for tl in sr.tiles:
            self.zero_fill(tl, D)
        ar = self.rot([128, 2, D], 2, "a", mm=True)
        tmpr = self.rot([128, 512], 4, "tmp")
        pp = self.psum(8)
        mode = DBG.get("f1_mode", 9)

        def f1_load(b):
            s_ = sr.get()
            em.dma(s_[0:nt, :], src_b[b], r=[src_res], w=[s_])
            return s_

        def f1_body(b, s_):
            a = ar.get()
            for hf in range(2):
                if mode < 1:
                    continue
                pre, pim = pp.get(), pp.get()
                em.op("pe", lambda e: e.matmul(pre[:], selc[:], s_[:, hf * 512:(hf + 1) * 512], start=True, stop=True),
                      r=[selc, s_], w=[pre])
                em.op("pe", lambda e: e.matmul(pim[:], sels[:], s_[:, hf * 512:(hf + 1) * 512], start=True, stop=True),
                      r=[sels, s_], w=[pim])
                if mode < 2:
                    continue
                self.twiddle(pre, pim, twr[:, b:b + 1], twi[:, b:b + 1], [twr, twi], a, hf, tmpr, 128)
            if mode >= 3:
                em.dma(ADa[b].rearrange("r k c -> k r c"), a[:], r=[a], w=[ADr], eng="pool")

        self.prefetch_loop(128, f1_load, f1_body)

    def twiddle(self, pre, pim, cr, ci, tabs, out, hf, tmpr, np_):
        em = self.em
        t1, t2 = tmpr.get(), tmpr.get()
        sl = slice(hf * 512, (hf + 1) * 512)
        twm = DBG.get("tw_mode", 3)
        if twm & 1:
            em.op("act", lambda e: e.activation(t1[0:np_, :], pim[0:np_, :], AF.Identity, scale=ci), r=[pim] + tabs, w=[t1])
            em.op("act", lambda e: e.activation(t2[0:np_, :], pre[0:np_, :], AF.Identity, scale=ci), r=[pre] + tabs, w=[t2])
        if twm & 2:
            em.op("dve", lambda e: e.scalar_tensor_tensor(out[0:np_, 0, sl], pre[0:np_, :], cr, t1[0:np_, :], ALU.mult, ALU.subtract),
                  r=[pre, t1] + tabs, w=[out])
            em.op("dve", lambda e: e.scalar_tensor_tensor(out[0:np_, 1, sl], pim[0:np_, :], cr, t2[0:np_, :], ALU.mult, ALU.add),
                  r=[pim, t2] + tabs, w=[out])

    def pass_f2h(self, j):
        em = self.em
        self.reset()
        ADf, ADfr = self.scratch("AD1", [128, 2, 128, D], BF16)
        ADb, ADbr = self.scratch("AD2", [128, 2, 128, D], BF16)
        HH, HHr = self.scratch("HH", [128, 2, 65, D])
        RN, RNr = self.scratch("RN", [128, D])
        c2 = self.load_const("c2", mm=True)
        s2 = self.load_const("s2", mm=True)
        ns2 = self.load_const("ns2", mm=True)
        wk = self.alloc([128, 128], "wk")
        em.dma(wk[0:65, :], self.C["wk"], w=[wk])
        rn = self.alloc([128, D], "rn")
        em.dma(rn[:], RN, r=[RNr], w=[rn])
        bd = self.load_rep(self.W["b_bias_d"][j], D, "bd")
        afr = self.rot([128, 2, D], 3, "af", mm=True)
        abr = self.rot([128, 2, D], 3, "ab", mm=True)
        hhr = self.rot([128, 2, D], 2, "hh")
        tmpr = self.rot([128, 512], 4, "tmp")
        pp = self.psum(8)
        def f2_load(K1):
            af, ab = afr.get(), abr.get()
            em.dma(af[:], ADf[:, :, K1, :], r=[ADfr], w=[af])
            em.dma(ab[:], ADb[:, :, K1, :], r=[ADbr], w=[ab])
            return af, ab

        def f2_body(K1, tl):
            af, ab = tl
            hh = hhr.get()
            for hf in range(2):
                sl = slice(hf * 512, (hf + 1) * 512)
                pre, pim = pp.get(), pp.get()
                terms_re = [(c2, af, 0), (s2, af, 1), (c2, ab, 0), (ns2, ab, 1)]
                terms_im = [(c2, af, 1), (ns2, af, 0), (c2, ab, 1), (s2, ab, 0)]
                for n, (tb, src, ri) in enumerate(terms_re):
                    em.op("pe", lambda e: e.matmul(pre[0:65, :], tb[:, 0:65], src[:, ri, sl], start=(n == 0), stop=(n == 3)),
                          r=[tb, src], w=[pre])
                for n, (tb, src, ri) in enumerate(terms_im):
                    em.op("pe", lambda e: e.matmul(pim[0:65, :], tb[:, 0:65], src[:, ri, sl], start=(n == 0), stop=(n == 3)),
                          r=[tb, src], w=[pim])
                t1, t2 = tmpr.get(), tmpr.get()
                em.op("dve", lambda e: e.tensor_tensor(t1[0:65, :], pre[0:65, :], rn[0:65, sl], ALU.mult), r=[pre, rn], w=[t1])
                em.op("pool", lambda e: e.tensor_tensor(t1[0:65, :], t1[0:65, :], bd[0:65, sl], ALU.add), r=[t1, bd], w=[t1])
                em.op("act", lambda e: e.activation(hh[0:65, 0, sl], t1[0:65, :], AF.Identity, scale=wk[0:65, K1:K1 + 1]), r=[t1, wk], w=[hh])
                em.op("dve", lambda e: e.tensor_tensor(t2[0:65, :], pim[0:65, :], rn[0:65, sl], ALU.mult), r=[pim, rn], w=[t2])
                em.op("act", lambda e: e.activation(hh[0:65, 1, sl], t2[0:65, :], AF.Identity, scale=wk[0:65, K1:K1 + 1]), r=[t2, wk], w=[hh])
            em.dma(HH[K1].rearrange("r k c -> k r c"), hh[0:65, :, :], r=[hh], w=[HHr], eng="pool")

        self.prefetch_loop(128, f2_load, f2_body)

    def pass_mid(self):
        em = self.em
        self.reset()
        ADv, ADvr = self.scratch("AD0", [128, 2, 128, D], BF16)
        HH, HHr = self.scratch("HH", [128, 2, 65, D])
        ZD, ZDr = self.scratch("ZD", [128, 2, 128, D], BF16)
        c2 = self.load_const("c2", mm=True)
        s2 = self.load_const("s2", mm=True)
        ns2 = self.load_const("ns2", mm=True)
        ic = self.alloc([128, 128], "ic", mm=True)
        em.dma(ic[0:65, :], self.C["ic"], w=[ic])
        is_ = self.alloc([128, 128], "is", mm=True)
        em.dma(is_[0:65, :], self.C["is"], w=[is_])
        nis = self.alloc([128, 128], "nis", mm=True)
        em.dma(nis[0:65, :], self.C["nis"], w=[nis])
        tir = self.load_const("tir")
        tii = self.load_const("tii")
        avr = self.rot([128, 2, D], 3, "av", mm=True)
        hhr = self.rot([128, 2, D], 3, "hh")
        yr = self.rot([128, 2, 512], 3, "y", mm=True)
        zr = self.rot([128, 2, D], 3, "z", mm=True)
        tmpr = self.rot([128, 512], 12, "tmp")
        pp = self.psum(8)
        def mid_load(K1):
            av, hh = avr.get(), hhr.get()
            em.dma(av[:], ADv[:, :, K1, :], r=[ADvr], w=[av])
            em.dma(hh[0:65, :, :], HH[K1].rearrange("r k c -> k r c"), r=[HHr], w=[hh])
            return av, hh

        pend = []

        def stage2(K1, hf, y, z):
            pzr, pzi = pp.get(), pp.get()
            for n, (tb, ri) in enumerate([(ic, 0), (nis, 1)]):
                em.op("pe", lambda e: e.matmul(pzr[:], tb[0:65, :], y[0:65, ri, :], start=(n == 0), stop=(n == 1)), r=[tb, y], w=[pzr])
            for n, (tb, ri) in enumerate([(ic, 1), (is_, 0)]):
                em.op("pe", lambda e: e.matmul(pzi[:], tb[0:65, :], y[0:65, ri, :], start=(n == 0), stop=(n == 1)), r=[tb, y], w=[pzi])
            self.twiddle(pzr, pzi, tir[:, K1:K1 + 1], tii[:, K1:K1 + 1], [tir, tii], z, hf, tmpr, 128)
            if hf == 1:
                em.dma(ZD[K1].rearrange("r b c -> b r c"), z[:], r=[z], w=[ZDr], eng="pool")

        def mid_body(K1, tl):
            av, hh = tl
            z = zr.get()
            for hf in range(2):
                sl = slice(hf * 512, (hf + 1) * 512)
                pxr, pxi = pp.get(), pp.get()
                for n, (tb, ri) in enumerate([(c2, 0), (s2, 1)]):
                    em.op("pe", lambda e: e.matmul(pxr[0:65, :], tb[:, 0:65], av[:, ri, sl], start=(n == 0), stop=(n == 1)),
                          r=[tb, av], w=[pxr])
                for n, (tb, ri) in enumerate([(c2, 1), (ns2, 0)]):
                    em.op("pe", lambda e: e.matmul(pxi[0:65, :], tb[:, 0:65], av[:, ri, sl], start=(n == 0), stop=(n == 1)),
                          r=[tb, av], w=[pxi])
                if pend:
                    stage2(*pend.pop(0))
                ta, tb_, tc, td = tmpr.get(), tmpr.get(), tmpr.get(), tmpr.get()
                y = yr.get()
                em.op("dve", lambda e: e.tensor_tensor(ta[0:65, :], pxr[0:65, :], hh[0:65, 0, sl], ALU.mult), r=[pxr, hh], w=[ta])
                em.op("dve", lambda e: e.tensor_tensor(tb_[0:65, :], pxi[0:65, :], hh[0:65, 1, sl], ALU.mult), r=[pxi, hh], w=[tb_])
                em.op("pool", lambda e: e.tensor_tensor(y[0:65, 0, :], ta[0:65, :], tb_[0:65, :], ALU.subtract), r=[ta, tb_], w=[y])
                em.op("dve", lambda e: e.tensor_tensor(tc[0:65, :], pxr[0:65, :], hh[0:65, 1, sl], ALU.mult), r=[pxr, hh], w=[tc])
                em.op("dve", lambda e: e.tensor_tensor(td[0:65, :], pxi[0:65, :], hh[0:65, 0, sl], ALU.mult), r=[pxi, hh], w=[td])
                em.op("pool", lambda e: e.tensor_tensor(y[0:65, 1, :], tc[0:65, :], td[0:65, :], ALU.add), r=[tc, td], w=[y])
                pend.append((K1, hf, y, z))

        self.prefetch_loop(128, mid_load, mid_body)
        while pend:
            stage2(*pend.pop(0))

    def pass_i2(self, j):
        em = self.em
        self.reset()
        Tn, nt = self.Tn, self.nt
        ZD, ZDr = self.scratch("ZD", [128, 2, 128, D], BF16)
        U0, U0r = self.scratch("U0T", [D, Tn + 2])
        GT, GTr = self.scratch("GT", [D, Tn], BF16)
        iwc = self.load_const("iwc", mm=True)
        iwsn = self.load_const("iwsn", mm=True)
        cw = self.W["b_conv_w"][j]
        w0c = self.load_cols(cw[0, 0:D], KC, "w0c")
        w1c = self.load_cols(cw[1, 0:D], KC, "w1c")
        w2c = self.load_cols(cw[2, 0:D], KC, "w2c")
        bc = self.load_cols(self.W["b_conv_b"][j, 0:D], KC, "bc")
        yT = self.alloc([128, Tn], "yT")
        yTv = yT[:].rearrange("p (i b) -> p b i", b=128)
        NB = 512 // nt
        zzr = self.rot([128, 2, NB, 128], 3, "zz", mm=True)
        TB = 512
        u0r = self.rot([128, TB + 2], 3, "u0")
        mpr = self.rot([128, TB], 3, "mp")
        mnr = self.rot([128, TB], 3, "mn")
        x0r = self.rot([128, TB], 2, "x0")
        ttr = self.rot([128, TB], 2, "tt")
        gr = self.rot([128, TB], 2, "g", mm=True)
        pp = self.psum(8)
        for cc in range(KC):
            def zz_load(ib):
                b0 = ib * NB
                zz = zzr.get()
                em.dma(zz[:], ZD[:, :, b0:b0 + NB, cc * 128:(cc + 1) * 128], r=[ZDr], w=[zz])
                return zz

            def zz_body(ib, zz):
                b0 = ib * NB
                ps = pp.get()
                for bb in range(NB):
                    em.op("pe", lambda e: e.matmul(ps[:, bb * nt:(bb + 1) * nt], zz[:, 0, bb, :], iwc[:, 0:nt], start=True, stop=False),
                          r=[zz, iwc], w=[ps])
                    em.op("pe", lambda e: e.matmul(ps[:, bb * nt:(bb + 1) * nt], zz[:, 1, bb, :], iwsn[:, 0:nt], start=False, stop=True),
                          r=[zz, iwsn], w=[ps])
                if ib % 2 == 0:
                    em.op("act", lambda e: e.copy(yTv[:, b0:b0 + NB, :], ps[:, 0:NB * nt].rearrange("p (b i) -> p b i", b=NB)), r=[ps], w=[yT])
                else:
                    em.op("dve", lambda e: e.tensor_copy(yTv[:, b0:b0 + NB, :], ps[:, 0:NB * nt].rearrange("p (b i) -> p b i", b=NB)), r=[ps], w=[yT])

            self.prefetch_loop(128 // NB, zz_load, zz_body)

            def g_load(blk):
                t0 = blk * TB
                u0 = u0r.get()
                em.dma(u0[:], U0[cc * 128:(cc + 1) * 128, t0:t0 + TB + 2], r=[U0r], w=[u0])
                mpt, mnt = mpr.get(), mnr.get()
                em.dma(mpt[:], self.C["mpR"][:, t0:t0 + TB], w=[mpt])
                em.dma(mnt[:], self.C["mnR"][:, t0:t0 + TB], w=[mnt])
                return u0, mpt, mnt

            def g_body(blk, tl):
                u0, mpt, mnt = tl
                t0 = blk * TB
                x0 = x0r.get()
                em.op("dve", lambda e: e.tensor_scalar(x0[:], u0[:, 1:TB + 1], w1c[:, cc:cc + 1], bc[:, cc:cc + 1], ALU.mult, ALU.add),
                      r=[u0, w1c, bc], w=[x0])
                t1 = ttr.get()
                em.op("pool", lambda e: e.tensor_tensor(t1[:], u0[:, 0:TB], mpt[:], ALU.mult), r=[u0, mpt], w=[t1])
                em.op("dve", lambda e: e.scalar_tensor_tensor(x0[:], t1[:], w0c[:, cc:cc + 1], x0[:], ALU.mult, ALU.add), r=[t1, w0c, x0], w=[x0])
                t2 = ttr.get()
                em.op("pool", lambda e: e.tensor_tensor(t2[:], u0[:, 2:TB + 2], mnt[:], ALU.mult), r=[u0, mnt], w=[t2])
                em.op("dve", lambda e: e.scalar_tensor_tensor(x0[:], t2[:], w2c[:, cc:cc + 1], x0[:], ALU.mult, ALU.add), r=[t2, w2c, x0], w=[x0])
                g = gr.get()
                em.op("pool", lambda e: e.tensor_tensor(g[:], x0[:], yT[:, t0:t0 + TB], ALU.mult), r=[x0, yT], w=[g])
                em.dma(GT[cc * 128:(cc + 1) * 128, t0:t0 + TB], g[:], r=[g], w=[GTr])

            self.prefetch_loop(Tn // TB, g_load, g_body)


_CACHE = {}


def run_cores(xs, mems, lsegs, weights, kinds, debug_out=()):
    nt = xs[0].shape[0] // 128
    consts = [build_consts(nt, l) for l in lsegs]
    wshapes = {k: weights[k].shape for k in WEIGHT_NAMES}
    cshapes = {k: v.shape for k, v in consts[0].items()}
    key = (nt, tuple(kinds), tuple(debug_out))
    if key not in _CACHE:
        _CACHE[key] = Builder(nt, kinds, wshapes, cshapes, debug_out).build()
    nc = _CACHE[key]
    in_maps = []
    for ci in range(len(xs)):
        m = {k: np.ascontiguousarray(weights[k], dtype=np.float32) for k in WEIGHT_NAMES}
        for k, v in consts[ci].items():
            m["c_" + k] = v.astype(ml_dtypes.bfloat16) if k in BF_CONSTS else v
        m["x"] = np.ascontiguousarray(xs[ci], dtype=np.float32)
        m["mem"] = np.ascontiguousarray(mems[ci], dtype=np.float32)
        in_maps.append(m)
    res = run_bass_kernel_spmd(nc, in_maps, core_ids=list(range(len(xs))))
    return res.results


def kernel(**inputs):
    xp = np.asarray(inputs["x_prompt"], dtype=np.float32)
    xs_ = np.asarray(inputs["x_sample"], dtype=np.float32)
    mp = np.asarray(inputs["mem_prompt"], dtype=np.float32)
    ms = np.asarray(inputs["mem_sample"], dtype=np.float32)
    weights = {k: np.asarray(inputs[k], dtype=np.float32) for k in WEIGHT_NAMES}
    xs, mems, lsegs = [], [], []
    for c in range(4):
        xs.append(xp[4 * c:4 * c + 4].reshape(8192, D))
        mems.append(mp[4 * c:4 * c + 4])
        lsegs.append(2048)
    for c in range(4):
        xs.append(xs_[c])
        mems.append(np.broadcast_to(ms[c][None], (4, MEM, D)))
        lsegs.append(8192)
    res = run_cores(xs, mems, lsegs, weights, ["a", "b", "c", "a"])
    yp = np.stack([res[c]["y"] for c in range(4)]).reshape(16, 2048, D)
    ys = np.stack([res[4 + c]["y"] for c in range(4)]).reshape(4, 8192, D)
    return (yp.astype(np.float32), ys.astype(np.float32))
```

```python
import contextlib
import math
import numpy as np
import ml_dtypes
import concourse.bass as bass
import concourse.mybir as mybir
from concourse.bass_utils import run_bass_kernel_spmd

F32 = mybir.dt.float32
BF16 = mybir.dt.bfloat16
AF = mybir.ActivationFunctionType
ALU = mybir.AluOpType
AX = mybir.AxisListType

D = 1024
KC = 8
DFF = 2816
FC = 22
MEM = 256
EPS = 1e-6
NFFT = 16384
TWO_PI = 2.0 * math.pi

DBG = {}
EPOCH = 30000
NRING = 16
DEPOCH = 1800


class Res:
    __slots__ = ("lw", "rd", "name", "excl")

    def __init__(self, name="", excl=False):
        self.lw = None
        self.rd = {}
        self.name = name
        self.excl = excl


class T:
    def __init__(self, t, name="", mm=False):
        self.t = t
        self.r = Res(name)
        self.mm = mm

    def __getitem__(self, idx):
        return self.t[idx]


def _res(x):
    return x.r if isinstance(x, T) else x


class _Rec:
    def __init__(self):
        self.call = None

    def __getattr__(self, name):
        def f(*a, **k):
            self.call = (name, a, k)
            return self
        return f


class Emitter:
    ENGS = ("pe", "act", "dve", "pool", "sp")

    def __init__(self, nc, stack):
        self.nc = nc
        self.stack = stack
        self.ops = {e: [] for e in self.ENGS}
        self.ccount = {e: 0 for e in self.ENGS}
        self.dcount = {e: 0 for e in self.ENGS}
        self.seen = {e: {} for e in self.ENGS}
        self.sems = {}
        self.nsem = 0

    def sem(self, key):
        s = self.sems.get(key)
        if s is None:
            s = self.stack.enter_context(self.nc.semaphore("s%d" % self.nsem))
            self.nsem += 1
            self.sems[key] = s
        return s

    def _target(self, ident):
        k, e, i = ident
        if k == "c":
            return ("c", e, i // EPOCH), (i % EPOCH) + 1
        slot = i % NRING
        j = i // NRING
        return ("d", e, slot, j // DEPOCH), 16 * ((j % DEPOCH) + 1)

    def _need(self, eng, ident, waits):
        if ident is None:
            return
        key, val = self._target(ident)
        if self.seen[eng].get(key, 0) >= val:
            return
        self.seen[eng][key] = val
        waits.append((key, val))

    def op(self, eng, fn, r=(), w=(), dma=False, ns=False):
        waits = []
        deps = []
        for x in r:
            res = _res(x)
            deps.append(res.lw)
            if res.excl:
                deps.extend(v for k, v in res.rd.items() if k[0] != eng)
        for x in w:
            res = _res(x)
            deps.append(res.lw)
            deps.extend(res.rd.values())
        for d in deps:
            if d is None:
                continue
            if d[1] == eng and d[0] == "c" and not dma and (eng == "pe" or ns):
                continue
            self._need(eng, d, waits)
        if dma:
            i = self.dcount[eng]
            self.dcount[eng] += 1
            ident = ("d", eng, i)
            if i >= NRING:
                self._need(eng, ("d", eng, i - NRING), waits)
        else:
            i = self.ccount[eng]
            self.ccount[eng] += 1
            ident = ("c", eng, i)
        key, val = self._target(ident)
        rec = _Rec()
        fn(rec)
        assert rec.call is not None
        call = rec.call
        self.ops[eng].append((call, waits, key, 16 if dma else 1))
        for x in r:
            _res(x).rd[(eng, dma)] = ident
        for x in w:
            res = _res(x)
            res.lw = ident
            res.rd = {}
        return ident

    def dma(self, out, in_, r=(), w=(), slow=False, eng="sp"):
        if slow:
            return self.op(eng, lambda e: e.dma_start(out=out, in_=in_, allow_slow_non_contiguous=True),
                           r=r, w=w, dma=True)
        return self.op(eng, lambda e: e.dma_start(out=out, in_=in_), r=r, w=w, dma=True)

    def barrier(self, engs=None):
        for x in (engs or self.ENGS):
            waits = []
            for e in self.ENGS:
                if self.ccount[e] and e != x:
                    self._need(x, ("c", e, self.ccount[e] - 1), waits)
                n = self.dcount[e]
                for i in range(max(0, n - NRING), n):
                    self._need(x, ("d", e, i), waits)
            if self.ccount[x]:
                self._need(x, ("c", x, self.ccount[x] - 1), waits)
            if waits:
                self.ops[x].append((None, waits, None, 0))

    def replay(self):
        nc = self.nc
        for e in self.ENGS:
            for fn, waits, key, inc in self.ops[e]:
                for k, v in waits:
                    self.sem(k)
                if key is not None:
                    self.sem(key)
        engmap = {"pe": "tensor", "act": "scalar", "dve": "vector", "pool": "gpsimd", "sp": "sync"}
        with nc.Block() as block:
            for e in self.ENGS:
                if not self.ops[e]:
                    continue
                ops = self.ops[e]

                def body(engine, ops=ops):
                    for fn, waits, key, inc in ops:
                        for k, v in waits:
                            engine.wait_ge(self.sems[k], v)
                        if fn is not None:
                            name, a, k = fn
                            getattr(engine, name)(*a, **k).then_inc(self.sems[key], inc)

                getattr(block, engmap[e])(body)


class TiledW:
    def __init__(self, ap4, gcols):
        self.ap = ap4
        self.g = gcols

    def rearrange(self, *a, **k):
        return self

    def __getitem__(self, idx):
        sp, sc, sn = idx
        c0, n = sn.start, sn.stop - sn.start
        g, off = c0 // self.g, c0 % self.g
        assert off + n <= self.g, (c0, n, self.g)
        return self.ap[g][:, :, off:off + n]


TILE_SPECS = {
    "a_w_in": ("a_w_in", 8, 512, 0, 9216), "a_w_out": ("a_w_out", 8, 512, 0, 1024),
    "b_w_in": ("b_w_in", 8, 512, 0, 3072), "b_w_out": ("b_w_out", 8, 512, 0, 1024),
    "c_w_in": ("c_w_in", 8, 512, 0, 2048), "c_w_out": ("c_w_out", 8, 512, 0, 1024),
    "x_w_q": ("x_w_q", 8, 512, 0, 1024), "x_w_kv": ("x_w_kv", 8, 512, 0, 2048), "x_w_o": ("x_w_o", 8, 512, 0, 1024),
    "f_w_g": ("f_w_gu", 8, 512, 0, 2816), "f_w_u": ("f_w_gu", 8, 512, 2816, 2816),
    "f_w_down": ("f_w_down", 22, 128, 0, 1024),
}


class Rot:
    def __init__(self, tiles):
        self.tiles = tiles
        self.i = 0

    def get(self):
        t = self.tiles[self.i % len(self.tiles)]
        self.i += 1
        return t


A_GROUPS = ((128, 1), (512, 4), (2048, 16))


def _coarse_index(nt, lseg):
    tiles_per_seg = lseg // 128
    i = np.arange(nt)
    if lseg >= nt * 128:
        return i
    return (i // tiles_per_seg) * (2 * tiles_per_seg) + (i % tiles_per_seg)


def build_consts(nt, lseg):
    Tn = nt * 128
    f32 = np.float32
    c = {}
    c["ident"] = np.eye(128, dtype=f32)
    c["identb"] = np.eye(128, dtype=f32)
    obm = np.zeros((2, 128, 128), f32)
    obm[0, :, 0:64] = 1.0
    obm[1, :, 64:128] = 1.0
    c["obm"] = obm
    R = np.zeros((64, 64), f32)
    for j in range(32):
        R[j + 32, j] = -1.0
        R[j, j + 32] = 1.0
    rb = np.zeros((128, 128), f32)
    rb[:64, :64] = R
    rb[64:, 64:] = R
    c["rblk"] = rb
    pos = (np.arange(Tn) % lseg).astype(np.float64)
    inv = 10000.0 ** (-np.arange(0, 64, 2, dtype=np.float64) / 64)
    inv32 = inv.astype(f32).astype(np.float64)
    ang = (pos.astype(f32)[:, None] * inv32.astype(f32)[None, :]).astype(f32)
    cos = np.cos(ang.astype(np.float64)).astype(f32)
    sin = np.sin(ang.astype(np.float64)).astype(f32)
    rows = np.arange(128) % 32
    c["ropec"] = np.ascontiguousarray(cos[:, rows].T)
    c["ropes"] = np.ascontiguousarray(sin[:, rows].T)
    for gi, (window, d) in enumerate(A_GROUPS):
        nsub = Tn // d
        ntile = nsub // 128
        nseg_sub = lseg // d
        m = np.zeros((ntile, 2, 128, 128), f32)
        kk = np.arange(128)[:, None]
        qq = np.arange(128)[None, :]
        for t in range(ntile):
            nq = 128 * t + qq
            for X in range(2):
                nk = 128 * t - 64 + 128 * X + kk
                ok = (np.abs(nq - nk) <= 64) & (nk >= 0) & (nk < nsub) & ((nq // nseg_sub) == (nk // nseg_sub))
                m[t, X] = ok.astype(f32)
        c["abias%d" % gi] = ((m - 1.0) * 30000.0).astype(f32)
    L = lseg
    tl = np.linspace(0.0, 1.0, L, dtype=f32)
    w = (2.0 * math.pi * np.arange(L, dtype=f32) / L).astype(f32)
    bands = 16
    fb = np.linspace(1e-4, bands - 1, bands, dtype=f32)[None, :]
    fw = (fb * w[:, None]).astype(f32)
    z = np.concatenate([tl[:, None], np.cos(fw.astype(np.float64)).astype(f32),
                        -np.sin(fw.astype(np.float64)).astype(f32)], axis=-1)
    reps = Tn // L
    zfull = np.tile(z, (reps, 1))
    c["zT"] = np.ascontiguousarray(zfull.T)
    tfull = np.tile(tl, reps)
    c["negt"] = np.ascontiguousarray((-tfull).reshape(nt, 128).T)
    vm = np.zeros((128, nt), f32)
    vm[:, : L // 128] = 1.0
    c["vmask"] = vm
    max_decay = math.log(1e-2) / 0.3
    min_decay = math.log(1e-2) / 1.5
    deltas = np.linspace(min_decay, max_decay, D, dtype=f32)
    c["absdelta"] = np.ascontiguousarray(np.broadcast_to(np.abs(deltas)[None, :], (128, D)))
    a = _coarse_index(nt, lseg).astype(np.float64)
    ah = np.arange(nt, dtype=np.float64)
    K1 = np.arange(128, dtype=np.float64)
    hvalid = (np.arange(nt) < L // 128).astype(np.float64)

    def cs(x):
        xm = np.mod(x, 1.0)
        return np.cos(TWO_PI * xm).astype(f32), np.sin(TWO_PI * xm).astype(f32)

    cc, ss = cs(np.outer(a, K1) / 128.0)
    c["wselc"] = cc
    c["wsels"] = -ss
    cc, ss = cs(np.outer(ah, K1) / 128.0)
    c["hselc"] = (cc * hvalid[:, None]).astype(f32)
    c["hsels"] = (-ss * hvalid[:, None]).astype(f32)
    c["hselsn"] = (ss * hvalid[:, None]).astype(f32)
    b = np.arange(128, dtype=np.float64)
    cc, ss = cs(np.outer(K1, b) / NFFT)
    c["twr"] = cc
    c["twi"] = -ss
    c["twin"] = ss
    K2 = np.arange(65, dtype=np.float64)
    cc, ss = cs(np.outer(b, K2) / 128.0)
    c["c2"] = cc
    c["s2"] = ss
    c["ns2"] = -ss
    wk = np.zeros((65, 128), f32)
    for k2 in range(65):
        for k1 in range(128):
            k = k1 + 128 * k2
            if k == 0 or k == NFFT // 2:
                wk[k2, k1] = 1.0 / NFFT
            elif k < NFFT // 2:
                wk[k2, k1] = 2.0 / NFFT
    c["wk"] = wk
    cc, ss = cs(np.outer(K2, b) / 128.0)
    c["ic"] = cc
    c["is"] = ss
    c["nis"] = -ss
    cc, ss = cs(np.outer(b, K1) / NFFT)
    c["tir"] = cc
    c["tii"] = ss
    cc, ss = cs(np.outer(K1, a) / 128.0)
    c["iwc"] = cc
    c["iwsn"] = -ss
    tok = np.arange(Tn)
    mp = ((tok % L) != 0).astype(f32)
    mn = ((tok % L) != (L - 1)).astype(f32)
    c["mp"] = np.ascontiguousarray(mp.reshape(nt, 128).T)
    c["mn"] = np.ascontiguousarray(mn.reshape(nt, 128).T)
    c["mpR"] = np.ascontiguousarray(np.broadcast_to(mp[None, :], (128, Tn)))
    c["mnR"] = np.ascontiguousarray(np.broadcast_to(mn[None, :], (128, Tn)))
    return c


BF_CONSTS = {"rblk", "obm", "identb", "abias0", "abias1", "abias2", "c2", "s2", "ns2", "ic", "is", "nis", "iwc", "iwsn", "wselc", "wsels", "hselc", "hsels", "hselsn"}
BIG_WEIGHTS = ["a_w_in", "a_w_out", "b_w_in", "b_w_out", "c_w_in", "c_w_out", "c_w_s", "x_w_q", "x_w_kv", "x_w_o", "f_w_gu", "f_w_down"]

WEIGHT_NAMES = ["g_mix", "g_cross", "g_ffn", "g_final", "a_w_in", "a_w_out", "b_w_in", "b_conv_w", "b_conv_b",
                "b_f_w1", "b_f_b1", "b_f_w2", "b_f_b2", "b_f_w3", "b_f_b3", "b_f_wout", "b_f_freq", "b_bias_d",
                "b_w_out", "c_w_in", "c_ln_g", "c_ln_b", "c_w_s", "c_b_s", "c_w_out", "x_w_q", "x_w_kv", "x_w_o",
                "f_w_gu", "f_w_down"]


class Builder:
    def __init__(self, nt, kinds, wshapes, cshapes, debug_out=()):
        self.nt = nt
        self.Tn = nt * 128
        self.kinds = kinds
        self.debug_out = set(debug_out)
        nc = bass.Bass("TRN2", target_bir_lowering=False)
        self.nc = nc
        self.W = {}
        for name in WEIGHT_NAMES:
            self.W[name] = nc.dram_tensor(name, list(wshapes[name]), F32, kind="ExternalInput").ap()
        self.C = {}
        for name, shp in cshapes.items():
            self.C[name] = nc.dram_tensor("c_" + name, list(shp), BF16 if name in BF_CONSTS else F32, kind="ExternalInput").ap()
        self.x_in = nc.dram_tensor("x", [self.Tn, D], F32, kind="ExternalInput").ap()
        self.mem_in = nc.dram_tensor("mem", [4, MEM, D], F32, kind="ExternalInput").ap()
        self.y_out = nc.dram_tensor("y", [self.Tn, D], F32, kind="ExternalOutput").ap()
        self.scr = {}
        self.scr_res = {}

    def scratch(self, name, shape, dt=F32):
        if name not in self.scr:
            kind = "ExternalOutput" if name in self.debug_out else "Internal"
            self.scr[name] = self.nc.dram_tensor("scr_" + name, list(shape), dt, kind=kind).ap()
            self.scr_res[name] = Res(name)
        return self.scr[name], self.scr_res[name]

    def dump(self, name, tile, shape):
        if name not in self.debug_out or name in self.scr:
            return
        d, dr = self.scratch(name, shape)
        self.em.dma(d, tile[:], r=[tile], w=[dr])

    def defer(self, fn):
        self.deferred.append(fn)

    def run_deferred(self):
        d, self.deferred = self.deferred, []
        for fn in d:
            fn()

    def reset(self):
        self.run_deferred()
        self.em.barrier()
        self.off = 0
        self.off_mm = 0
        self.psi = 0

    def alloc(self, shape, name="", mm=False):
        n = int(np.prod(shape[1:]))
        if mm:
            assert self.off_mm + n <= self.mm_words, ("mm arena overflow", name, self.off_mm, n)
            ap = self.arena_mm[0:shape[0], self.off_mm:self.off_mm + n]
            self.off_mm += n
        else:
            assert self.off + n <= self.arena_words, ("arena overflow", name, self.off, n)
            ap = self.arena[0:shape[0], self.off:self.off + n]
            self.off += n
        if len(shape) == 3:
            ap = ap.rearrange("p (a b) -> p a b", a=shape[1])
        elif len(shape) == 4:
            ap = ap.rearrange("p (a b c) -> p a b c", a=shape[1], b=shape[2])
        return T(ap, name, mm)

    def rot(self, shape, n, name="", mm=False):
        return Rot([self.alloc(shape, "%s%d" % (name, i), mm) for i in range(n)])

    def psum(self, n):
        assert self.psi + n <= 8
        r = Rot(self.pbanks[self.psi:self.psi + n])
        self.psi += n
        return r

    def build(self):
        nc = self.nc
        with contextlib.ExitStack() as st:
            self.em = Emitter(nc, st)
            em = self.em
            self.arena_words = 16600
            self.mm_words = 72800
            self.arena = st.enter_context(nc.sbuf_tensor("arena", [128, self.arena_words], F32))
            self.arena_mm = st.enter_context(nc.sbuf_tensor("arena_mm", [128, self.mm_words], BF16))
            self.off_mm = 0
            self.pbanks = [T(st.enter_context(nc.psum_tensor("pb%d" % i, [128, 512], F32)), "pb%d" % i)
                           for i in range(8)]
            for pb in self.pbanks:
                pb.r.excl = True
            self.off = 0
            self.psi = 0
            self.deferred = []
            self.program()
            self.run_deferred()
            em.barrier()
            em.replay()
        return nc

    def load_const(self, name, shape=None, slow=False, mm=False):
        ap = self.C[name]
        t = self.alloc(list(ap.shape) if shape is None else shape, name, mm)
        self.em.dma(t[:], ap, w=[t], slow=slow)
        return t

    @staticmethod
    def prefetch_loop(n, load_fn, body_fn):
        nxt = load_fn(0) if n > 0 else None
        for i in range(n):
            cur = nxt
            nxt = load_fn(i + 1) if i + 1 < n else None
            body_fn(i, cur)

    def ones_tile(self):
        t = self.alloc([128, 128], "ones", mm=True)
        self.em.op("dve", lambda e: e.memset(t[:], 1.0), w=[t])
        return t

    def zero_fill(self, t, n):
        self.em.op("pool", lambda e: e.memset(t[:], 0.0), w=[t])

    def load_cols(self, vec_ap, nchunk, name):
        t = self.alloc([128, nchunk], name)
        self.em.dma(t[:], vec_ap.rearrange("(c p) -> p c", p=128), w=[t], slow=True)
        return t

    def load_rep(self, vec_ap, n, name):
        t = self.alloc([128, n], name)
        self.em.dma(t[:], vec_ap.partition_broadcast(128), w=[t], slow=True)
        return t

    def rmsnorm(self, x, xn, sq, gcols, gi, ones, pp, rstd, TB):
        em = self.em
        em.op("act", lambda e: e.activation(sq[:], x[:], AF.Square), r=[x], w=[sq])
        ps = pp.get()
        for c in range(KC):
            em.op("pe", lambda e, c=c: e.matmul(ps[:, 0:TB], ones[:], sq[:, c, :], start=(c == 0), stop=(c == KC - 1)),
                  r=[ones, sq], w=[ps])
        em.op("dve", lambda e: e.tensor_scalar(rstd[:], ps[:, 0:TB], 1.0 / D, EPS, ALU.mult, ALU.add), r=[ps], w=[rstd])
        em.op("act", lambda e: e.activation(rstd[:], rstd[:], AF.Sqrt), r=[rstd], w=[rstd])
        em.op("dve", lambda e: e.reciprocal(rstd[:], rstd[:]), r=[rstd], w=[rstd])
        for c in range(KC):
            em.op("dve", lambda e, c=c: e.scalar_tensor_tensor(xn[:, c, :], x[:, c, :], gcols[:, gi * KC + c:gi * KC + c + 1],
                                                               rstd[:], ALU.mult, ALU.mult),
                  r=[x, gcols, rstd], w=[xn], ns=(c > 0))

    def linear_fm(self, w_ap, col0, ncols, src, kch, TB, wrot, pp, epi, group=512):
        em = self.em
        wv = w_ap.rearrange("(c p) n -> p c n", p=128)
        for g0 in range(0, ncols, group):
            gn = min(group, ncols - g0)
            wt = wrot.get()
            wtv = wt[:, 0:kch * gn].rearrange("p (c n) -> p c n", c=kch)
            em.dma(wtv, wv[:, :, col0 + g0:col0 + g0 + gn], w=[wt])
            self.run_deferred()
            for m in range(gn // 128):
                ps = pp.get()
                for k in range(kch):
                    em.op("pe", lambda e, k=k, m=m, ps=ps, wtv=wtv: e.matmul(
                        ps[:, 0:TB], wtv[:, k, m * 128:(m + 1) * 128], src[:, k, :], start=(k == 0), stop=(k == kch - 1)),
                        r=[wt, src], w=[ps])
                epi((g0 // 128) + m, ps)

    def pass_cast_weights(self):
        em = self.em
        self.reset()
        self.Wb = {}
        ldr = self.rot([128, 4096], 3, "wld")
        cvr = self.rot([128, 4096], 4, "wcv", mm=True)
        n = 0

        def cast(ld, cv, sz):
            nonlocal n
            eng = ("act", "dve")[n % 2]
            n += 1
            if eng == "act":
                em.op("act", lambda e: e.copy(cv[:, 0:sz], ld[:, 0:sz]), r=[ld], w=[cv])
            else:
                em.op(eng, lambda e: e.tensor_copy(cv[:, 0:sz], ld[:, 0:sz]), r=[ld], w=[cv])

        w = self.W["c_w_s"]
        wb, wbr = self.scratch("wb_c_w_s", list(w.shape), BF16)
        self.Wb["c_w_s"] = wb
        wf = w.rearrange("l h p q -> (l h p q)").rearrange("(p f) -> p f", p=128)
        wbf = wb.rearrange("l h p q -> (l h p q)").rearrange("(p f) -> p f", p=128)
        F = wf.shape[1]
        for f0 in range(0, F, 4096):
            fn = min(4096, F - f0)
            ld, cv = ldr.get(), cvr.get()
            em.dma(ld[:, 0:fn], wf[:, f0:f0 + fn], w=[ld])
            cast(ld, cv, fn)
            em.dma(wbf[:, f0:f0 + fn], cv[:, 0:fn], r=[cv], w=[wbr], eng="pool")
        for oname, (src, kch, gcols, base, ncols) in TILE_SPECS.items():
            w = self.W[src]
            Lw = w.shape[0]
            ng = -(-ncols // gcols)
            wt, wtr = self.scratch("wt_" + oname, [Lw, ng, 128, kch, gcols], BF16)
            self.Wb[oname] = [TiledW(wt[l], gcols) for l in range(Lw)]
            for l in range(Lw):
                wv = w[l].rearrange("(c p) n -> p c n", p=128)
                for g in range(ng):
                    gn = min(gcols, ncols - g * gcols)
                    ld, cv = ldr.get(), cvr.get()
                    ldv = ld[:, 0:kch * gn].rearrange("p (c n) -> p c n", c=kch)
                    cvv = cv[:, 0:kch * gn].rearrange("p (c n) -> p c n", c=kch)
                    em.dma(ldv, wv[:, :, base + g * gcols:base + g * gcols + gn], w=[ld])
                    cast(ld, cv, kch * gn)
                    em.dma(wt[l, g][:, :, 0:gn], cvv, r=[cv], w=[wtr], eng="pool")

    def program(self):
        self.pass_cast_weights()
        self.pass_transpose_in()
        self.pass_mem()
        nA = nB = nC = 0
        for li, kind in enumerate(self.kinds):
            if kind == "a":
                self.mixer_a(li, nA)
                self.pass_post(li, "OT", self.Wb["a_w_out"][nA])
                nA += 1
            elif kind == "b":
                self.mixer_b(li, nB)
                self.pass_post(li, "GT", self.Wb["b_w_out"][nB])
                nB += 1
            elif kind == "c":
                self.pass_post(li, None, self.Wb["c_w_out"][nC], sgu=nC)
                nC += 1
            else:
                self.pass_post(li, None, None)
        self.pass_final()

    def pass_transpose_in(self):
        em = self.em
        self.reset()
        xT, xTr = self.scratch("xT", [D, self.Tn])
        ident = self.load_const("ident")
        xin = self.rot([128, D], 3, "xin")
        stg = self.rot([128, KC, 512], 2, "stg")
        pp = self.psum(8)
        xTv = xT.rearrange("(c p) t -> p c t", p=128)
        for b4 in range(0, self.nt, 4):
            nb = min(4, self.nt - b4)
            s = stg.get()
            for j in range(nb):
                i = b4 + j
                xt = xin.get()
                em.dma(xt[:], self.x_in[i * 128:(i + 1) * 128, :], w=[xt])
                pa, pb = pp.get(), pp.get()
                for c in range(KC):
                    ps = pa if c < 4 else pb
                    em.op("pe", lambda e, c=c, ps=ps, xt=xt: e.transpose(ps[:, (c % 4) * 128:(c % 4 + 1) * 128],
                                                                          xt[:, c * 128:(c + 1) * 128], ident[:]),
                          r=[xt, ident], w=[ps])
                em.op("act", lambda e, s=s, j=j, pa=pa: e.copy(s[:, 0:4, j * 128:(j + 1) * 128],
                                                               pa[:].rearrange("p (c t) -> p c t", c=4)), r=[pa], w=[s])
                em.op("dve", lambda e, s=s, j=j, pb=pb: e.tensor_copy(s[:, 4:8, j * 128:(j + 1) * 128],
                                                                      pb[:].rearrange("p (c t) -> p c t", c=4)), r=[pb], w=[s])
            em.dma(xTv[:, :, b4 * 128:(b4 + nb) * 128], s[:, :, 0:nb * 128], r=[s], w=[xTr])

    def pass_mem(self):
        em = self.em
        self.reset()
        mT, mTr = self.scratch("memT", [4, D, MEM], BF16)
        ident = self.load_const("ident")
        xin = self.rot([128, D], 3, "min")
        stg = self.rot([128, KC, 128], 2, "mstg", mm=True)
        pp = self.psum(8)
        for s4 in range(4):
            for j in range(2):
                xt = xin.get()
                em.dma(xt[:], self.mem_in[s4, j * 128:(j + 1) * 128, :], w=[xt])
                pa, pb = pp.get(), pp.get()
                s = stg.get()
                for c in range(KC):
                    ps = pa if c < 4 else pb
                    em.op("pe", lambda e, c=c, ps=ps, xt=xt: e.transpose(ps[:, (c % 4) * 128:(c % 4 + 1) * 128],
                                                                          xt[:, c * 128:(c + 1) * 128], ident[:]),
                          r=[xt, ident], w=[ps])
                em.op("act", lambda e, s=s, pa=pa: e.copy(s[:, 0:4, :], pa[:].rearrange("p (c t) -> p c t", c=4)), r=[pa], w=[s])
                em.op("dve", lambda e, s=s, pb=pb: e.tensor_copy(s[:, 4:8, :], pb[:].rearrange("p (c t) -> p c t", c=4)), r=[pb], w=[s])
                em.dma(mT[s4].rearrange("(c p) t -> p c t", p=128)[:, :, j * 128:(j + 1) * 128], s[:], r=[s], w=[mTr])

    def pass_post(self, li, mixname, wout_ap, sgu=None):
        em = self.em
        self.reset()
        TB = 512
        nblk = self.Tn // TB
        blk_per_seg = self.Tn // 4 // TB
        xT, xTr = self.scratch("xT", [D, self.Tn])
        xTv = xT.rearrange("(c p) t -> p c t", p=128)
        mT, mTr = self.scratch("memT", [4, D, MEM], BF16)
        if mixname is not None:
            mx, mxr = self.scratch(mixname, [D, self.Tn], BF16)
            mxv = mx.rearrange("(c p) t -> p c t", p=128)
        ones = self.ones_tile()
        gm = self.load_cols(self.W["g_mix"].rearrange("l d -> (l d)"), 4 * KC, "gm")
        gc = self.load_cols(self.W["g_cross"].rearrange("l d -> (l d)"), 4 * KC, "gc")
        gf = self.load_cols(self.W["g_ffn"].rearrange("l d -> (l d)"), 4 * KC, "gf")
        xrot = self.rot([128, KC, TB], 2 if sgu is not None else 3, "x")
        a8 = self.rot([128, KC, TB], 3, "a8", mm=True)
        xn_t = self.alloc([128, KC, TB], "xn", mm=True)
        sq_t = self.alloc([128, KC, TB], "sq", mm=True)
        rstd = self.alloc([128, TB], "rstd")
        wrot = self.rot([128, 4096], 5, "w", mm=True)
        kt = self.alloc([128, KC, MEM], "kt", mm=True)
        vt = self.alloc([128, 2, D], "vt", mm=True)
        memt = self.alloc([128, KC, MEM], "memt", mm=True)
        prot = self.rot([128, 2, TB], 3, "pT", mm=True)
        rden = self.rot([128, TB], 2, "rden")
        hbuf = self.alloc([128, FC, TB], "h", mm=True)
        tmpr = self.rot([128, TB], 3, "tmp")
        pp = self.psum(8)
        if sgu is not None:
            j = sgu
            lngR = self.load_rep(self.W["c_ln_g"][j], D, "lng")
            lnbR = self.load_rep(self.W["c_ln_b"][j], D, "lnb")
            wsT = self.alloc([128, 8, 128], "wsT", mm=True)
            em.dma(wsT[:], self.Wb["c_w_s"][j].rearrange("h p q -> q h p"), w=[wsT], slow=True)
            bsR = self.alloc([128, 8, 128], "bsR")
            em.dma(bsR[:].rearrange("p h q -> p (h q)"),
                   self.W["c_b_s"][j].rearrange("h p -> (h p)").partition_broadcast(128), w=[bsR], slow=True)
            zv = self.rot([128, D], 2, "zv")
            zvn = self.rot([128, D], 2, "zvn", mm=True)
            st6 = self.rot([128, 8], 3, "st6")
        wq = self.Wb["x_w_q"][li]
        wkv = self.Wb["x_w_kv"][li]
        wo = self.Wb["x_w_o"][li]
        wg_t = self.Wb["f_w_g"][li]
        wu_t = self.Wb["f_w_u"][li]
        wdn = self.Wb["f_w_down"][li]
        mtr = self.rot([128, KC, TB], 2, "mt", mm=True) if mixname is not None else None

        def post_load(b):
            t0 = b * TB
            x = xrot.get()
            em.dma(x[:], xTv[:, :, t0:t0 + TB], r=[xTr], w=[x])
            mt = None
            if mixname is not None:
                mt = mtr.get()
                em.dma(mt[:], mxv[:, :, t0:t0 + TB], r=[mxr], w=[mt])
            return x, mt

        nxt = post_load(0)
        for b in range(nblk):
            t0 = b * TB
            x, mt = nxt
            if sgu is None:
                nxt = post_load(b + 1) if b + 1 < nblk else None

            def resid(m, ps, x=x):
                em.op("dve", lambda e: e.tensor_tensor(x[:, m, :], x[:, m, :], ps[:, 0:TB], ALU.add), r=[x, ps], w=[x], ns=True)

            if mixname is not None:
                self.linear_fm(wout_ap, 0, D, mt, KC, TB, wrot, pp, resid)
            elif sgu is not None:
                j = sgu
                self.rmsnorm(x, xn_t, sq_t, gm, li, ones, pp, rstd, TB)
                zu = a8.get()

                def epi_zu(m, ps, zu=zu):
                    em.op("act", lambda e: e.activation(zu[:, m, :], ps[:, 0:TB], AF.Gelu), r=[ps], w=[zu])
                self.linear_fm(self.Wb["c_w_in"][j], 0, D, xn_t, KC, TB, wrot, pp, epi_zu)
                nxt = post_load(b + 1) if b + 1 < nblk else None
                gate = a8.get()
                wv = self.Wb["c_w_in"][j].rearrange("(c p) n -> p c n", p=128)
                wts = []
                for h2 in range(2):
                    wt = wrot.get()
                    wtv = wt[:].rearrange("p (c n) -> p c n", c=KC)
                    em.dma(wtv, wv[:, :, D + h2 * 512:D + (h2 + 1) * 512], w=[wt])
                    wts.append((wt, wtv))
                sg_pend = []

                def sgu_b(tl, zn, gate=gate, zu=zu):
                    pa, pb = pp.get(), pp.get()
                    for h in range(8):
                        ps = pa if h < 4 else pb
                        em.op("pe", lambda e: e.matmul(ps[:, (h % 4) * 128:(h % 4 + 1) * 128],
                                                       zn[:, h * 128:(h + 1) * 128], wsT[:, h, :], start=True, stop=True),
                              r=[zn, wsT], w=[ps])
                    for half, ps in ((0, pa), (1, pb)):
                        for hh in range(4):
                            h = half * 4 + hh
                            tt = tmpr.get()
                            em.op("dve", lambda e: e.tensor_tensor(
                                tt[:, 0:128], ps[:, hh * 128:(hh + 1) * 128], bsR[:, h, :], ALU.add), r=[ps, bsR], w=[tt])
                            em.op("pool", lambda e: e.tensor_tensor(
                                gate[:, h, tl * 128:(tl + 1) * 128], zu[:, h, tl * 128:(tl + 1) * 128], tt[:, 0:128], ALU.mult),
                                r=[tt, zu], w=[gate], ns=True)

                for tl in range(TB // 128):
                    z = zv.get()
                    for h2 in range(2):
                        wt, wtv = wts[h2]
                        ps = pp.get()
                        for k in range(KC):
                            em.op("pe", lambda e, k=k, ps=ps, wtv=wtv, tl=tl: e.matmul(
                                ps[:], xn_t[:, k, tl * 128:(tl + 1) * 128], wtv[:, k, :], start=(k == 0), stop=(k == KC - 1)),
                                r=[xn_t, wt], w=[ps])
                        em.op("act", lambda e, ps=ps, z=z, h2=h2: e.activation(z[:, h2 * 512:(h2 + 1) * 512], ps[:], AF.Gelu),
                              r=[ps], w=[z])
                    s6 = st6.get()
                    zn = zvn.get()
                    em.op("dve", lambda e, z=z, s6=s6: e.reduce_sum(s6[:, 0:1], z[:], axis=AX.X), r=[z], w=[s6])
                    em.op("dve", lambda e, s6=s6: e.tensor_scalar(s6[:, 1:2], s6[:, 0:1], -1.0 / D, None, ALU.mult), r=[s6], w=[s6])
                    em.op("dve", lambda e, z=z, s6=s6: e.tensor_scalar(z[:], z[:], s6[:, 1:2], None, ALU.add), r=[z, s6], w=[z])
                    em.op("act", lambda e, z=z, zn=zn, s6=s6: e.activation(zn[:], z[:], AF.Square, accum_out=s6[:, 2:3]),
                          r=[z], w=[zn, s6])
                    em.op("dve", lambda e, s6=s6: e.tensor_scalar(s6[:, 3:4], s6[:, 2:3], 1.0 / D, EPS, ALU.mult, ALU.add), r=[s6], w=[s6])
                    em.op("act", lambda e, s6=s6: e.activation(s6[:, 4:5], s6[:, 3:4], AF.Sqrt), r=[s6], w=[s6])
                    em.op("dve", lambda e, s6=s6: e.reciprocal(s6[:, 5:6], s6[:, 4:5]), r=[s6], w=[s6])
                    em.op("dve", lambda e, z=z, zn=zn, s6=s6: e.scalar_tensor_tensor(zn[:], z[:], s6[:, 5:6], lngR[:], ALU.mult, ALU.mult),
                          r=[z, s6, lngR], w=[zn])
                    em.op("pool", lambda e, zn=zn: e.tensor_tensor(zn[:], zn[:], lnbR[:], ALU.add), r=[zn, lnbR], w=[zn])
                    if sg_pend:
                        sgu_b(*sg_pend.pop(0))
                    sg_pend.append((tl, zn))
                while sg_pend:
                    sgu_b(*sg_pend.pop(0))
                self.linear_fm(wout_ap, 0, D, gate, KC, TB, wrot, pp, resid)

            if not DBG.get('skip_cross'):
                seg = b // blk_per_seg
                if b % blk_per_seg == 0:
                    em.dma(memt[:], mT[seg].rearrange("(c p) t -> p c t", p=128), r=[mTr], w=[memt])

                    def epi_k(m, ps):
                        em.op("act", lambda e: e.copy(kt[:, m, :], ps[:, 0:MEM]), r=[ps], w=[kt])
                    self.linear_fm(wkv, 0, D, memt, KC, MEM, wrot, pp, epi_k)
                    wv = wkv.rearrange("(c p) n -> p c n", p=128)
                    for h2 in range(2):
                        wt = wrot.get()
                        wtv = wt[:].rearrange("p (c n) -> p c n", c=KC)
                        em.dma(wtv, wv[:, :, D + h2 * 512:D + (h2 + 1) * 512], w=[wt])
                        for mc in range(2):
                            ps = pp.get()
                            for k in range(KC):
                                em.op("pe", lambda e, k=k, ps=ps, wtv=wtv, mc=mc: e.matmul(
                                    ps[:], memt[:, k, mc * 128:(mc + 1) * 128], wtv[:, k, :], start=(k == 0), stop=(k == KC - 1)),
                                    r=[memt, wt], w=[ps])
                            em.op("act", lambda e, ps=ps, mc=mc, h2=h2: e.copy(vt[:, mc, h2 * 512:(h2 + 1) * 512], ps[:]), r=[ps], w=[vt])
                self.dump('d_kt', kt, [128, KC, MEM])
                self.dump('d_vt', vt, [128, 2, D])
                self.rmsnorm(x, xn_t, sq_t, gc, li, ones, pp, rstd, TB)
                self.dump('d_xn', xn_t, [128, KC, TB])
                q = a8.get()

                def epi_q(m, ps, q=q):
                    em.op("act", lambda e: e.copy(q[:, m, :], ps[:, 0:TB]), r=[ps], w=[q])
                self.linear_fm(wq, 0, D, xn_t, KC, TB, wrot, pp, epi_q)
                o = a8.get()

                def att_a(h, q=q):
                    pT = prot.get()
                    for mc in range(2):
                        ps = pp.get()
                        for dc in range(2):
                            em.op("pe", lambda e: e.matmul(
                                ps[:, 0:TB], kt[:, 2 * h + dc, mc * 128:(mc + 1) * 128], q[:, 2 * h + dc, :], start=(dc == 0), stop=(dc == 1)),
                                r=[kt, q], w=[ps])
                        em.op("act", lambda e: e.activation(pT[:, mc, :], ps[:, 0:TB], AF.Exp, scale=1.0 / 16.0),
                              r=[ps], w=[pT])
                    return pT

                def att_b(h, pT, o=o):
                    ps = pp.get()
                    for mc in range(2):
                        em.op("pe", lambda e: e.matmul(ps[:, 0:TB], ones[:], pT[:, mc, :], start=(mc == 0), stop=(mc == 1)),
                              r=[ones, pT], w=[ps])
                    rd = rden.get()
                    em.op("dve", lambda e: e.reciprocal(rd[:], ps[:, 0:TB]), r=[ps], w=[rd])
                    for dc in range(2):
                        ps2 = pp.get()
                        for mc in range(2):
                            em.op("pe", lambda e: e.matmul(
                                ps2[:, 0:TB], vt[:, mc, (2 * h + dc) * 128:(2 * h + dc + 1) * 128], pT[:, mc, :], start=(mc == 0), stop=(mc == 1)),
                                r=[vt, pT], w=[ps2])
                        em.op("dve", lambda e: e.tensor_tensor(o[:, 2 * h + dc, :], ps2[:, 0:TB], rd[:], ALU.mult),
                              r=[ps2, rd], w=[o], ns=(dc > 0))

                pTs = att_a(0)
                for h in range(4):
                    pTn = att_a(h + 1) if h < 3 else None
                    att_b(h, pTs)
                    pTs = pTn
                self.dump('d_q', q, [128, KC, TB])
                self.dump('d_o', o, [128, KC, TB])
                self.linear_fm(wo, 0, D, o, KC, TB, wrot, pp, resid)

            if not DBG.get('skip_ffn'):
                self.rmsnorm(x, xn_t, sq_t, gf, li, ones, pp, rstd, TB)
                for g0 in range(0, FC, 4):
                    gn = min(4, FC - g0)
                    wg = wrot.get()
                    wgv = wg[:, 0:KC * gn * 128].rearrange("p (c n) -> p c n", c=KC)
                    em.dma(wgv, wg_t[:, :, g0 * 128:(g0 + gn) * 128], w=[wg])
                    wu = wrot.get()
                    wuv = wu[:, 0:KC * gn * 128].rearrange("p (c n) -> p c n", c=KC)
                    em.dma(wuv, wu_t[:, :, g0 * 128:(g0 + gn) * 128], w=[wu])
                    for m in range(gn):
                        pg, pu = pp.get(), pp.get()
                        for k in range(KC):
                            em.op("pe", lambda e, k=k, m=m, pg=pg, wgv=wgv: e.matmul(
                                pg[:, 0:TB], wgv[:, k, m * 128:(m + 1) * 128], xn_t[:, k, :], start=(k == 0), stop=(k == KC - 1)),
                                r=[wg, xn_t], w=[pg])
                        for k in range(KC):
                            em.op("pe", lambda e, k=k, m=m, pu=pu, wuv=wuv: e.matmul(
                                pu[:, 0:TB], wuv[:, k, m * 128:(m + 1) * 128], xn_t[:, k, :], start=(k == 0), stop=(k == KC - 1)),
                                r=[wu, xn_t], w=[pu])
                        tt = tmpr.get()
                        em.op("act", lambda e, pg=pg, tt=tt: e.activation(tt[:], pg[:, 0:TB], AF.Silu), r=[pg], w=[tt])
                        em.op("dve", lambda e, pu=pu, tt=tt, mm=g0 + m: e.tensor_tensor(hbuf[:, mm, :], tt[:], pu[:, 0:TB], ALU.mult),
                              r=[tt, pu], w=[hbuf], ns=True)
                self.linear_fm(wdn, 0, D, hbuf, FC, TB, wrot, pp, resid, group=128)
            self.defer(lambda x=x, t0=t0: em.dma(xTv[:, :, t0:t0 + TB], x[:], r=[x], w=[xTr]))

    def pass_final(self):
        em = self.em
        self.reset()
        TB = 256
        xT, xTr = self.scratch("xT", [D, self.Tn])
        xTv = xT.rearrange("(c p) t -> p c t", p=128)
        ident = self.load_const("ident")
        ones = self.ones_tile()
        gfin = self.load_cols(self.W["g_final"], KC, "gfin")
        xrot = self.rot([128, KC, TB], 2, "x")
        xnr = self.rot([128, KC, TB], 2, "xnf")
        sq_t = self.alloc([128, KC, TB], "sq", mm=True)
        rstd = self.alloc([128, TB], "rstd")
        yo = self.rot([128, D], 3, "yo")
        pp = self.psum(8)
        for b in range(self.Tn // TB):
            t0 = b * TB
            x = xrot.get()
            em.dma(x[:], xTv[:, :, t0:t0 + TB], r=[xTr], w=[x])
            xn = xnr.get()
            self.rmsnorm(x, xn, sq_t, gfin, 0, ones, pp, rstd, TB)
            for j in range(TB // 128):
                pa, pb = pp.get(), pp.get()
                y = yo.get()
                for c in range(KC):
                    ps = pa if c < 4 else pb
                    em.op("pe", lambda e, c=c, ps=ps, xn=xn, j=j: e.transpose(ps[:, (c % 4) * 128:(c % 4 + 1) * 128],
                                                                             xn[:, c, j * 128:(j + 1) * 128], ident[:]),
                          r=[xn, ident], w=[ps])
                em.op("act", lambda e, y=y, pa=pa: e.copy(y[:, 0:512], pa[:]), r=[pa], w=[y])
                em.op("dve", lambda e, y=y, pb=pb: e.tensor_copy(y[:, 512:1024], pb[:]), r=[pb], w=[y])
                yr = Res("y")
                em.dma(self.y_out[t0 + j * 128:t0 + (j + 1) * 128, :], y[:], r=[y], w=[yr])

    def mixer_a(self, li, j):
        self.pass_a1(li, j)
        self.pass_a2(li, j)

    def pass_a1(self, li, j):
        em = self.em
        self.reset()
        TB = 512
        Tn = self.Tn
        xT, xTr = self.scratch("xT", [D, Tn])
        xTv = xT.rearrange("(c p) t -> p c t", p=128)
        w_in = self.Wb["a_w_in"][j]
        pads = [64 * d for (_, d) in A_GROUPS]
        QT, KT, VV = [], [], []
        for g in range(3):
            QT.append(self.scratch("QT%d" % g, [D, Tn + 2 * pads[g]], BF16))
            KT.append(self.scratch("KT%d" % g, [D, Tn + 2 * pads[g]], BF16))
            VV.append(self.scratch("VV%d" % g, [Tn + 2 * pads[g], D], BF16))
        ones = self.ones_tile()
        rblk = self.load_const("rblk", mm=True)
        gm = self.load_cols(self.W["g_mix"].rearrange("l d -> (l d)"), 4 * KC, "gm")
        zt = self.alloc([128, 1024], "zero", mm=True)
        em.op("pool", lambda e: e.memset(zt[:], 0.0), w=[zt])
        for g in range(3):
            pad = pads[g]
            for side in range(2):
                c0 = 0 if side == 0 else pad + Tn
                for c in range(KC):
                    em.dma(KT[g][0][c * 128:(c + 1) * 128, c0:c0 + pad], zt[:, 0:pad], r=[zt], w=[KT[g][1]])
                for r0 in range(0, pad, 128):
                    rn = min(128, pad - r0)
                    em.dma(VV[g][0][c0 + r0:c0 + r0 + rn, :], zt[0:rn, :], r=[zt], w=[VV[g][1]])
        xrot = self.rot([128, KC, TB], 2, "x")
        xn_t = self.alloc([128, KC, TB], "xn", mm=True)
        sq_t = self.alloc([128, KC, TB], "sq", mm=True)
        rstd = self.alloc([128, TB], "rstd")
        wrot = self.rot([128, 4096], 9, "w", mm=True)
        cbr = self.rot([128, TB], 2, "cb")
        sbr = self.rot([128, TB], 2, "sb")
        qraw = self.rot([128, TB], 3, "qraw", mm=True)
        t1r = self.rot([128, TB], 3, "t1")
        t2r = self.rot([128, TB], 3, "t2")
        stg = self.rot([128, KC, TB], 3, "stg", mm=True)
        vst = self.rot([128, D], 6, "vst", mm=True)
        pp = self.psum(8)
        for b in range(Tn // TB):
            t0 = b * TB
            x = xrot.get()
            em.dma(x[:], xTv[:, :, t0:t0 + TB], r=[xTr], w=[x])
            self.rmsnorm(x, xn_t, sq_t, gm, li, ones, pp, rstd, TB)
            cb, sb = cbr.get(), sbr.get()
            em.dma(cb[:], self.C["ropec"][:, t0:t0 + TB], w=[cb])
            em.dma(sb[:], self.C["ropes"][:, t0:t0 + TB], w=[sb])
            for g in range(3):
                pad = pads[g]
                for jj in range(2):
                    st_ = stg.get()

                    pend = []

                    def rope(m, qr, st_=st_, cb=cb, sb=sb):
                        p2 = pp.get()
                        em.op("pe", lambda e: e.matmul(p2[:, 0:TB], rblk[:], qr[:], start=True, stop=True), r=[rblk, qr], w=[p2])
                        t1, t2 = t1r.get(), t2r.get()
                        em.op("pool", lambda e: e.tensor_tensor(t1[:], qr[:], cb[:], ALU.mult), r=[qr, cb], w=[t1])
                        em.op("dve", lambda e: e.tensor_tensor(t2[:], p2[:, 0:TB], sb[:], ALU.mult), r=[p2, sb], w=[t2])
                        em.op("pool", lambda e: e.tensor_tensor(st_[:, m, :], t1[:], t2[:], ALU.add), r=[t1, t2], w=[st_])

                    def epi(m, ps, pend=pend, rope=rope):
                        qr = qraw.get()
                        em.op("act", lambda e: e.copy(qr[:], ps[:, 0:TB]), r=[ps], w=[qr])
                        if pend:
                            pend.pop()()
                        pend.append(lambda: rope(m, qr))
                    self.linear_fm(w_in, g * 3072 + jj * 1024, 1024, xn_t, KC, TB, wrot, pp, epi)
                    while pend:
                        pend.pop()()
                    dst, dres = (QT[g] if jj == 0 else KT[g])
                    self.defer(lambda dst=dst, dres=dres, st_=st_, pad=pad, t0=t0: em.dma(
                        dst.rearrange("(c p) t -> p c t", p=128)[:, :, pad + t0:pad + t0 + TB], st_[:], r=[st_], w=[dres]))
                wv = w_in.rearrange("(c p) n -> p c n", p=128)
                wts = []
                for h2 in range(2):
                    wt = wrot.get()
                    wtv = wt[:].rearrange("p (c n) -> p c n", c=KC)
                    c0 = g * 3072 + 2048 + h2 * 512
                    em.dma(wtv, wv[:, :, c0:c0 + 512], w=[wt])
                    wts.append((wt, wtv))
                for tl in range(TB // 128):
                    vs = vst.get()
                    for h2 in range(2):
                        wt, wtv = wts[h2]
                        ps = pp.get()
                        for k in range(KC):
                            em.op("pe", lambda e, k=k: e.matmul(ps[:], xn_t[:, k, tl * 128:(tl + 1) * 128], wtv[:, k, :],
                                                                start=(k == 0), stop=(k == KC - 1)), r=[xn_t, wt], w=[ps])
                        em.op("act", lambda e: e.copy(vs[:, h2 * 512:(h2 + 1) * 512], ps[:]), r=[ps], w=[vs])
                    r0 = pad + t0 + tl * 128
                    self.defer(lambda g=g, r0=r0, vs=vs: em.dma(VV[g][0][r0:r0 + 128, :], vs[:], r=[vs], w=[VV[g][1]]))

    def pass_a2(self, li, j):
        em = self.em
        self.reset()
        Tn = self.Tn
        RG = 2048
        pads = [64 * d for (_, d) in A_GROUPS]
        QT, KT, VV = [], [], []
        for g in range(3):
            QT.append(self.scratch("QT%d" % g, [D, Tn + 2 * pads[g]], BF16))
            KT.append(self.scratch("KT%d" % g, [D, Tn + 2 * pads[g]], BF16))
            VV.append(self.scratch("VV%d" % g, [Tn + 2 * pads[g], D], BF16))
        OT, OTr = self.scratch("OT", [D, Tn], BF16)
        Ur = self.rot([128, RG], 2, "U")
        Lr = self.rot([128, RG], 2, "L")
        rl = self.alloc([128, RG], "rl")
        otr = self.rot([128, RG], 2, "ot", mm=True)
        identb = self.load_const("identb", mm=True)
        qz0 = self.rot([128, RG], 2, "qz0", mm=True)
        qz1 = self.rot([128, RG], 2, "qz1", mm=True)
        ktr = self.rot([128, 2 * RG], 2, "kt", mm=True)
        vzr = self.rot([128, 2, 128], 8, "vz", mm=True)
        mkr = self.rot([128, 2, 128], 3, "mk", mm=True)
        pMr = self.rot([128, 4, 128], 6, "pM", mm=True)
        ob = [self.alloc([128, 128], "ob%d" % e, mm=True) for e in range(2)]
        for e_ in range(2):
            em.dma(ob[e_][:], self.C["obm"][e_], w=[ob[e_]])
        for tl in qz0.tiles + qz1.tiles:
            self.zero_fill(tl, RG)
        pss = self.psum(3)
        psu = self.psum(3)
        psl = self.psum(2)
        nv = 0
        for R in range(Tn // RG):
            for hc in range(KC):
                U, L, ot = Ur.get(), Lr.get(), otr.get()
                em.op("pool", lambda e: e.memset(U[:], 0.0), w=[U])
                em.op("pool", lambda e: e.memset(L[:], 0.0), w=[L])
                pend = []
                for g, (window, d) in enumerate(A_GROUPS):
                    pad = pads[g]
                    span = 128 * d
                    ntr = RG // span
                    for tt in range(ntr):
                        t = R * ntr + tt
                        base = t * span
                        q0, q1, kt = qz0.get(), qz1.get(), ktr.get()
                        em.dma(q0[0:64, 0:span], QT[g][0][hc * 128:hc * 128 + 64, pad + base:pad + base + span], r=[QT[g][1]], w=[q0])
                        em.dma(q1[64:128, 0:span], QT[g][0][hc * 128 + 64:hc * 128 + 128, pad + base:pad + base + span], r=[QT[g][1]], w=[q1])
                        em.dma(kt[:, 0:2 * span], KT[g][0][hc * 128:(hc + 1) * 128, pad + base - 64 * d:pad + base + 192 * d],
                               r=[KT[g][1]], w=[kt])
                        mk = mkr.get()
                        em.dma(mk[:], self.C["abias%d" % g][t].rearrange("x k q -> k x q"), w=[mk])
                        qz = (q0, q1)
                        for r in range(d):
                            vz = vzr.get()
                            rs = pad + base - 64 * d + r
                            src = VV[g][0][rs:rs + 255 * d + 1:d, hc * 128:(hc + 1) * 128].rearrange("(x k) c -> k x c", x=2)
                            em.dma(vz[:], src, r=[VV[g][1]], w=[vz], eng=("sp", "act", "pool")[nv % 3])
                            nv += 1
                            ps = pss.get()
                            for e_ in range(2):
                                for X in range(2):
                                    k0 = 128 * d * X + r
                                    em.op("pe", lambda e: e.matmul(ps[:, (e_ * 2 + X) * 128:(e_ * 2 + X + 1) * 128],
                                                                   kt[:, k0:k0 + 127 * d + 1:d], qz[e_][:, r:r + 127 * d + 1:d], start=True, stop=False),
                                          r=[kt, qz[e_]], w=[ps])
                                    em.op("pe", lambda e: e.matmul(ps[:, (e_ * 2 + X) * 128:(e_ * 2 + X + 1) * 128],
                                                                   identb[:], mk[:, X, :], start=False, stop=True),
                                          r=[identb, mk], w=[ps])
                            pM = pMr.get()
                            em.op("act", lambda e: e.activation(pM[:].rearrange("p a b -> p (a b)"), ps[:], AF.Exp, scale=0.125),
                                  r=[ps], w=[pM])
                            off = tt * span + r
                            pend.append((vz, pM, ob, psu, psl, U, L, off, d))
                            if len(pend) > 2:
                                self._a2_pv(*pend.pop(0))
                while pend:
                    self._a2_pv(*pend.pop(0))
                em.op("dve", lambda e: e.reciprocal(rl[:], L[:]), r=[L], w=[rl])
                em.op("pool", lambda e: e.tensor_tensor(ot[:], U[:], rl[:], ALU.mult), r=[U, rl], w=[ot])
                em.dma(OT[hc * 128:(hc + 1) * 128, R * RG:(R + 1) * RG], ot[:], r=[ot], w=[OTr], eng="pool")

    def _a2_pv(self, vz, pM, ob, psu, psl, U, L, off, d):
        em = self.em
        pus = []
        for e_ in range(2):
            pu = psu.get()
            for X in range(2):
                em.op("pe", lambda e: e.matmul(pu[:, 0:128], vz[:, X, :], pM[:, e_ * 2 + X, :],
                                               start=(X == 0), stop=(X == 1)), r=[vz, pM], w=[pu])
            pus.append(pu)
        pl = psl.get()
        n = 0
        for e_ in range(2):
            for X in range(2):
                em.op("pe", lambda e: e.matmul(pl[:, 0:128], ob[e_][:], pM[:, e_ * 2 + X, :],
                                               start=(n == 0), stop=(n == 3)), r=[ob[e_], pM], w=[pl])
                n += 1
        for e_ in range(2):
            rows = slice(e_ * 64, (e_ + 1) * 64)
            em.op("dve", lambda e: e.tensor_tensor(U[rows, off:off + 127 * d + 1:d], U[rows, off:off + 127 * d + 1:d],
                                                   pus[e_][rows, 0:128], ALU.add), r=[U, pus[e_]], w=[U], ns=True)
        em.op("dve", lambda e: e.tensor_tensor(L[:, off:off + 127 * d + 1:d], L[:, off:off + 127 * d + 1:d],
                                               pl[:, 0:128], ALU.add), r=[L, pl], w=[L], ns=True)

    def mixer_b(self, li, j):
        stop = DBG.get("b_stop", 99)
        AD = [self.scratch("AD%d" % i, [128, 2, 128, D], BF16) for i in range(3)]
        VD = self.scratch("VD", [self.Tn, D], BF16)
        HD = self.scratch("HD", [self.Tn, 2, D], BF16)
        hv = HD[0].rearrange("(i b) r c -> b r i c", b=128)
        steps = [
            lambda: self.pass_b0(j),
            lambda: self.pass_b1(li, j),
            lambda: self.pass_b2(j),
            lambda: self.pass_f1(VD[0].rearrange("(i b) c -> b i c", b=128), VD[1], "wselc", "wsels", "twi", AD[0]),
            lambda: self.pass_f1(hv[:, 0], HD[1], "hselc", "hsels", "twi", AD[1]),
            lambda: self.pass_f1(hv[:, 1], HD[1], "hselc", "hselsn", "twin", AD[2]),
            lambda: self.pass_f2h(j),
            lambda: self.pass_mid(),
            lambda: self.pass_i2(j),
        ]
        for i, f in enumerate(steps):
            if i < stop:
                f()

    def sin_layer(self, ps, hid, freq, fb, tmpr, n):
        em = self.em
        a, c = tmpr.get(), tmpr.get()
        em.op("dve", lambda e: e.tensor_scalar(a[0:64, 0:n], ps[0:64, 0:n], freq[0:64, 0:1], fb[0:64, 0:1], ALU.mult, ALU.add),
              r=[ps, freq, fb], w=[a])
        ci = c[0:64, 0:n].bitcast(mybir.dt.int32)
        em.op("dve", lambda e: e.tensor_copy(ci, a[0:64, 0:n]), r=[a], w=[c])
        em.op("dve", lambda e: e.tensor_copy(c[0:64, 0:n], ci), r=[c], w=[c])
        em.op("dve", lambda e: e.tensor_tensor(a[0:64, 0:n], a[0:64, 0:n], c[0:64, 0:n], ALU.subtract), r=[a, c], w=[a])
        em.op("act", lambda e: e.activation(hid[0:64, 0:n], a[0:64, 0:n], AF.Sin, scale=6.283179), r=[a], w=[hid])

    def pass_b0(self, j):
        em = self.em
        self.reset()
        Tn, nt = self.Tn, self.nt
        TBf = 512
        HD, HDr = self.scratch("HD", [Tn, 2, D], BF16)
        RN, RNr = self.scratch("RN", [128, D])
        W = self.W
        ones = self.alloc([128, 128], "onesf")
        em.op("dve", lambda e: e.memset(ones[:], 1.0), w=[ones])
        w1 = self.alloc([128, 64], "w1")
        em.dma(w1[0:33, :], W["b_f_w1"][j], w=[w1])
        w2 = self.alloc([128, 64], "w2")
        em.dma(w2[0:64, :], W["b_f_w2"][j], w=[w2])
        w3 = self.alloc([128, 64], "w3")
        em.dma(w3[0:64, :], W["b_f_w3"][j], w=[w3])
        wout = self.alloc([128, 2 * D], "wout")
        em.dma(wout[0:64, :], W["b_f_wout"][j], w=[wout])
        freq = self.alloc([128, 1], "freq")
        em.dma(freq[0:64, :], W["b_f_freq"][j].rearrange("(p o) -> p o", o=1), w=[freq], slow=True)
        fbs = []
        for nm in ("b_f_b1", "b_f_b2", "b_f_b3"):
            fb = self.alloc([128, 1], nm)
            em.dma(fb[0:64, :], W[nm][j].rearrange("(p o) -> p o", o=1), w=[fb], slow=True)
            em.op("dve", lambda e: e.tensor_tensor(fb[0:64, :], fb[0:64, :], freq[0:64, :], ALU.mult), r=[fb, freq], w=[fb])
            em.op("dve", lambda e: e.tensor_scalar(fb[0:64, :], fb[0:64, :], 1.0 / TWO_PI, None, ALU.mult), r=[fb], w=[fb])
            fbs.append(fb)
        em.op("dve", lambda e: e.tensor_scalar(freq[0:64, :], freq[0:64, :], 1.0 / TWO_PI, None, ALU.mult), r=[freq] + fbs, w=[freq])
        absd = self.load_const("absdelta")
        negt = self.load_const("negt")
        vmask = self.load_const("vmask")
        zr = self.rot([128, TBf], 2, "z")
        hidr = self.rot([128, TBf], 3, "hidf")
        tmpr = self.rot([128, TBf], 3, "tmp")
        decr = self.rot([128, D], 1, "dec")
        hr = self.rot([128, 2, D], 2, "hflt")
        abr = self.rot([128, 2, D], 1, "abf")
        hbr = self.rot([128, 2, D], 2, "hb", mm=True)
        nps = self.psum(2).tiles
        pp = self.psum(6)
        first = True
        for b in range(Tn // TBf):
            t0 = b * TBf
            z = zr.get()
            em.dma(z[0:33, :], self.C["zT"][:, t0:t0 + TBf], w=[z])
            ps = pp.get()
            em.op("pe", lambda e: e.matmul(ps[0:64, :], w1[0:33, :], z[0:33, :], start=True, stop=True), r=[w1, z], w=[ps])
            h1 = hidr.get()
            self.sin_layer(ps, h1, freq, fbs[0], tmpr, TBf)
            ps = pp.get()
            em.op("pe", lambda e: e.matmul(ps[0:64, :], w2[0:64, :], h1[0:64, :], start=True, stop=True), r=[w2, h1], w=[ps])
            h2 = hidr.get()
            self.sin_layer(ps, h2, freq, fbs[1], tmpr, TBf)
            ps = pp.get()
            em.op("pe", lambda e: e.matmul(ps[0:64, :], w3[0:64, :], h2[0:64, :], start=True, stop=True), r=[w3, h2], w=[ps])
            h3 = hidr.get()
            self.sin_layer(ps, h3, freq, fbs[2], tmpr, TBf)
            for tl in range(TBf // 128):
                i = b * (TBf // 128) + tl
                dec = decr.get()
                em.op("act", lambda e: e.activation(dec[:], absd[:], AF.Exp, scale=negt[:, i:i + 1]), r=[absd, negt], w=[dec])
                h = hr.get()
                for cg in range(4):
                    ps = pp.get()
                    em.op("pe", lambda e: e.matmul(ps[:], h3[0:64, tl * 128:(tl + 1) * 128], wout[0:64, cg * 512:(cg + 1) * 512],
                                                   start=True, stop=True), r=[h3, wout], w=[ps])
                    em.op("dve", lambda e: e.tensor_tensor(h[:, cg // 2, (cg % 2) * 512:(cg % 2 + 1) * 512], ps[:],
                                                           dec[:, (cg % 2) * 512:(cg % 2 + 1) * 512], ALU.mult), r=[ps, dec], w=[h])
                if i == 0:
                    em.op("dve", lambda e: e.tensor_tensor(h[0:1, 0, :], h[0:1, 0, :], h[0:1, 1, :], ALU.add), r=[h], w=[h])
                    em.op("dve", lambda e: e.memset(h[0:1, 1, :], 0.0), w=[h])
                ab = abr.get()
                em.op("act", lambda e: e.activation(ab[:].rearrange("p a b -> p (a b)"), h[:].rearrange("p a b -> p (a b)"),
                                                    AF.Abs, scale=vmask[:, i:i + 1]), r=[h, vmask], w=[ab])
                last = (i == nt - 1)
                for dr in range(2):
                    for hf in range(2):
                        em.op("pe", lambda e: e.matmul(nps[hf][:], ones[:], ab[:, dr, hf * 512:(hf + 1) * 512],
                                                       start=(first and dr == 0), stop=(last and dr == 1)), r=[ones, ab], w=[nps[hf]])
                first = False
                hb = hbr.get()
                em.op("pool", lambda e: e.tensor_copy(hb[:], h[:]), r=[h], w=[hb])
                em.dma(HD[i * 128:(i + 1) * 128], hb[:], r=[hb], w=[HDr])
        rn = decr.get()
        for hf in range(2):
            em.op("dve", lambda e: e.reciprocal(rn[:, hf * 512:(hf + 1) * 512], nps[hf][:]), r=[nps[hf]], w=[rn])
        em.dma(RN, rn[:], r=[rn], w=[RNr])

    def pass_b1(self, li, j):
        em = self.em
        self.reset()
        TB = 512
        Tn = self.Tn
        xT, xTr = self.scratch("xT", [D, Tn])
        xTv = xT.rearrange("(c p) t -> p c t", p=128)
        U0, U0r = self.scratch("U0T", [D, Tn + 2])
        U12, U12r = self.scratch("U12", [Tn + 2, 2 * D], BF16)
        w_in = self.Wb["b_w_in"][j]
        ones = self.ones_tile()
        gm = self.load_cols(self.W["g_mix"].rearrange("l d -> (l d)"), 4 * KC, "gm")
        zt = self.alloc([128, 8], "zero")
        em.op("pool", lambda e: e.memset(zt[:], 0.0), w=[zt])
        ztb = self.alloc([128, 2 * D], "zerob", mm=True)
        em.op("pool", lambda e: e.memset(ztb[:], 0.0), w=[ztb])
        for c in range(KC):
            em.dma(U0[c * 128:(c + 1) * 128, 0:1], zt[:, 0:1], r=[zt], w=[U0r], slow=True)
            em.dma(U0[c * 128:(c + 1) * 128, Tn + 1:Tn + 2], zt[:, 0:1], r=[zt], w=[U0r], slow=True)
        em.dma(U12[0:1, :], ztb[0:1, :], r=[ztb], w=[U12r])
        em.dma(U12[Tn + 1:Tn + 2, :], ztb[0:1, :], r=[ztb], w=[U12r])
        xrot = self.rot([128, KC, TB], 2, "x")
        xn_t = self.alloc([128, KC, TB], "xn", mm=True)
        sq_t = self.alloc([128, KC, TB], "sq", mm=True)
        rstd = self.alloc([128, TB], "rstd")
        wrot = self.rot([128, 4096], 8, "w", mm=True)
        stg = self.rot([128, KC, TB], 1, "stg")
        ust = self.rot([128, 2 * D], 8, "ust", mm=True)
        pp = self.psum(8)
        wv = w_in.rearrange("(c p) n -> p c n", p=128)
        for b in range(Tn // TB):
            t0 = b * TB
            x = xrot.get()
            em.dma(x[:], xTv[:, :, t0:t0 + TB], r=[xTr], w=[x])
            self.rmsnorm(x, xn_t, sq_t, gm, li, ones, pp, rstd, TB)
            st_ = stg.get()

            def epi(m, ps, st_=st_):
                em.op("act", lambda e: e.copy(st_[:, m, :], ps[:, 0:TB]), r=[ps], w=[st_])
            self.linear_fm(w_in, 0, D, xn_t, KC, TB, wrot, pp, epi)
            self.defer(lambda st_=st_, t0=t0: em.dma(U0.rearrange("(c p) t -> p c t", p=128)[:, :, 1 + t0:1 + t0 + TB], st_[:], r=[st_], w=[U0r]))
            us = [ust.get() for _ in range(TB // 128)]
            for cg in range(4):
                wt = wrot.get()
                wtv = wt[:].rearrange("p (c n) -> p c n", c=KC)
                em.dma(wtv, wv[:, :, D + cg * 512:D + (cg + 1) * 512], w=[wt])
                for tl in range(TB // 128):
                    ps = pp.get()
                    for k in range(KC):
                        em.op("pe", lambda e: e.matmul(ps[:], xn_t[:, k, tl * 128:(tl + 1) * 128], wtv[:, k, :],
                                                       start=(k == 0), stop=(k == KC - 1)), r=[xn_t, wt], w=[ps])
                    if (cg + tl) % 2 == 0:
                        em.op("act", lambda e: e.copy(us[tl][:, cg * 512:(cg + 1) * 512], ps[:]), r=[ps], w=[us[tl]])
                    else:
                        em.op("dve", lambda e: e.tensor_copy(us[tl][:, cg * 512:(cg + 1) * 512], ps[:]), r=[ps], w=[us[tl]])
            for tl in range(TB // 128):
                r0 = 1 + t0 + tl * 128
                self.defer(lambda r0=r0, u=us[tl]: em.dma(U12[r0:r0 + 128, :], u[:], r=[u], w=[U12r]))

    def pass_b2(self, j):
        em = self.em
        self.reset()
        Tn, nt = self.Tn, self.nt
        U12, U12r = self.scratch("U12", [Tn + 2, 2 * D], BF16)
        VD, VDr = self.scratch("VD", [Tn, D], BF16)
        cw = self.W["b_conv_w"][j]
        w0R = self.load_rep(cw[0, D:3 * D], 2 * D, "w0R")
        w1R = self.load_rep(cw[1, D:3 * D], 2 * D, "w1R")
        w2R = self.load_rep(cw[2, D:3 * D], 2 * D, "w2R")
        bR = self.load_rep(self.W["b_conv_b"][j, D:3 * D], 2 * D, "bR")
        mp = self.load_const("mp")
        mn = self.load_const("mn")
        ucr = self.rot([128, 2 * D], 3, "uc", mm=True)
        upr = self.rot([128, 2 * D], 3, "up", mm=True)
        unr = self.rot([128, 2 * D], 3, "un", mm=True)
        cr = self.rot([128, 2 * D], 2, "c", mm=True)
        tr = self.rot([128, 2 * D], 2, "t", mm=True)
        vr = self.rot([128, D], 2, "v", mm=True)
        def b2_load(i):
            uc, up, un = ucr.get(), upr.get(), unr.get()
            em.dma(uc[:], U12[1 + i * 128:1 + (i + 1) * 128, :], r=[U12r], w=[uc])
            em.dma(up[:], U12[i * 128:(i + 1) * 128, :], r=[U12r], w=[up])
            em.dma(un[:], U12[2 + i * 128:2 + (i + 1) * 128, :], r=[U12r], w=[un])
            return uc, up, un

        def b2_body(i, tl):
            uc, up, un = tl
            c = cr.get()
            em.op("dve", lambda e: e.tensor_tensor(c[:], uc[:], w1R[:], ALU.mult), r=[uc, w1R], w=[c])
            em.op("pool", lambda e: e.tensor_tensor(c[:], c[:], bR[:], ALU.add), r=[c, bR], w=[c])
            t1 = tr.get()
            em.op("pool", lambda e: e.tensor_tensor(t1[:], up[:], w0R[:], ALU.mult), r=[up, w0R], w=[t1])
            em.op("dve", lambda e: e.scalar_tensor_tensor(c[:], t1[:], mp[:, i:i + 1], c[:], ALU.mult, ALU.add), r=[t1, mp, c], w=[c])
            t2 = tr.get()
            em.op("dve", lambda e: e.tensor_tensor(t2[:], un[:], w2R[:], ALU.mult), r=[un, w2R], w=[t2])
            em.op("dve", lambda e: e.scalar_tensor_tensor(c[:], t2[:], mn[:, i:i + 1], c[:], ALU.mult, ALU.add), r=[t2, mn, c], w=[c])
            v = vr.get()
            em.op("dve", lambda e: e.tensor_tensor(v[:], c[:, D:2 * D], c[:, 0:D], ALU.mult), r=[c], w=[v])
            em.dma(VD[i * 128:(i + 1) * 128, :], v[:], r=[v], w=[VDr])

        self.prefetch_loop(nt, b2_load, b2_body)

    def pass_f1(self, src_b, src_res, selc_n, sels_n, twi_n, AD):
        em = self.em
        self.reset()
        nt = self.nt
        ADa, ADr = AD
        selc = self.alloc([128, 128], "selc", mm=True)
        self.zero_fill(selc, 128)
        em.dma(selc[0:nt, :], self.C[selc_n], w=[selc])
        sels = self.alloc([128, 128], "sels", mm=True)
        self.zero_fill(sels, 128)
        em.dma(sels[0:nt, :], self.C[sels_n], w=[sels])
        twr = self.load_const("twr")
        twi = self.load_const(twi_n)
        sr = self.rot([128, D], 3, "s", mm=True)
        for tl in sr.tiles:
            self.zero_fill(tl, D)
        ar = self.rot([128, 2, D], 2, "a", mm=True)
        tmpr = self.rot([128, 512], 4, "tmp")
        pp = self.psum(8)
        mode = DBG.get("f1_mode", 9)

        def f1_load(b):
            s_ = sr.get()
            em.dma(s_[0:nt, :], src_b[b], r=[src_res], w=[s_])
            return s_

        def f1_body(b, s_):
            a = ar.get()
            for hf in range(2):
                if mode < 1:
                    continue
                pre, pim = pp.get(), pp.get()
                em.op("pe", lambda e: e.matmul(pre[:], selc[:], s_[:, hf * 512:(hf + 1) * 512], start=True, stop=True),
                      r=[selc, s_], w=[pre])
                em.op("pe", lambda e: e.matmul(pim[:], sels[:], s_[:, hf * 512:(hf + 1) * 512], start=True, stop=True),
                      r=[sels, s_], w=[pim])
                if mode < 2:
                    continue
                self.twiddle(pre, pim, twr[:, b:b + 1], twi[:, b:b + 1], [twr, twi], a, hf, tmpr, 128)
            if mode >= 3:
                em.dma(ADa[b].rearrange("r k c -> k r c"), a[:], r=[a], w=[ADr], eng="pool")

        self.prefetch_loop(128, f1_load, f1_body)

    def twiddle(self, pre, pim, cr, ci, tabs, out, hf, tmpr, np_):
        em = self.em
        t1, t2 = tmpr.get(), tmpr.get()
        sl = slice(hf * 512, (hf + 1) * 512)
        twm = DBG.get("tw_mode", 3)
        if twm & 1:
            em.op("act", lambda e: e.activation(t1[0:np_, :], pim[0:np_, :], AF.Identity, scale=ci), r=[pim] + tabs, w=[t1])
            em.op("act", lambda e: e.activation(t2[0:np_, :], pre[0:np_, :], AF.Identity, scale=ci), r=[pre] + tabs, w=[t2])
        if twm & 2:
            em.op("dve", lambda e: e.scalar_tensor_tensor(out[0:np_, 0, sl], pre[0:np_, :], cr, t1[0:np_, :], ALU.mult, ALU.subtract),
                  r=[pre, t1] + tabs, w=[out])
            em.op("dve", lambda e: e.scalar_tensor_tensor(out[0:np_, 1, sl], pim[0:np_, :], cr, t2[0:np_, :], ALU.mult, ALU.add),
                  r=[pim, t2] + tabs, w=[out])

    def pass_f2h(self, j):
        em = self.em
        self.reset()
        ADf, ADfr = self.scratch("AD1", [128, 2, 128, D], BF16)
        ADb, ADbr = self.scratch("AD2", [128, 2, 128, D], BF16)
        HH, HHr = self.scratch("HH", [128, 2, 65, D])
        RN, RNr = self.scratch("RN", [128, D])
        c2 = self.load_const("c2", mm=True)
        s2 = self.load_const("s2", mm=True)
        ns2 = self.load_const("ns2", mm=True)
        wk = self.alloc([128, 128], "wk")
        em.dma(wk[0:65, :], self.C["wk"], w=[wk])
        rn = self.alloc([128, D], "rn")
        em.dma(rn[:], RN, r=[RNr], w=[rn])
        bd = self.load_rep(self.W["b_bias_d"][j], D, "bd")
        afr = self.rot([128, 2, D], 3, "af", mm=True)
        abr = self.rot([128, 2, D], 3, "ab", mm=True)
        hhr = self.rot([128, 2, D], 2, "hh")
        tmpr = self.rot([128, 512], 4, "tmp")
        pp = self.psum(8)
        def f2_load(K1):
            af, ab = afr.get(), abr.get()
            em.dma(af[:], ADf[:, :, K1, :], r=[ADfr], w=[af])
            em.dma(ab[:], ADb[:, :, K1, :], r=[ADbr], w=[ab])
            return af, ab

        def f2_body(K1, tl):
            af, ab = tl
            hh = hhr.get()
            for hf in range(2):
                sl = slice(hf * 512, (hf + 1) * 512)
                pre, pim = pp.get(), pp.get()
                terms_re = [(c2, af, 0), (s2, af, 1), (c2, ab, 0), (ns2, ab, 1)]
                terms_im = [(c2, af, 1), (ns2, af, 0), (c2, ab, 1), (s2, ab, 0)]
                for n, (tb, src, ri) in enumerate(terms_re):
                    em.op("pe", lambda e: e.matmul(pre[0:65, :], tb[:, 0:65], src[:, ri, sl], start=(n == 0), stop=(n == 3)),
                          r=[tb, src], w=[pre])
                for n, (tb, src, ri) in enumerate(terms_im):
                    em.op("pe", lambda e: e.matmul(pim[0:65, :], tb[:, 0:65], src[:, ri, sl], start=(n == 0), stop=(n == 3)),
                          r=[tb, src], w=[pim])
                t1, t2 = tmpr.get(), tmpr.get()
                em.op("dve", lambda e: e.tensor_tensor(t1[0:65, :], pre[0:65, :], rn[0:65, sl], ALU.mult), r=[pre, rn], w=[t1])
                em.op("pool", lambda e: e.tensor_tensor(t1[0:65, :], t1[0:65, :], bd[0:65, sl], ALU.add), r=[t1, bd], w=[t1])
                em.op("act", lambda e: e.activation(hh[0:65, 0, sl], t1[0:65, :], AF.Identity, scale=wk[0:65, K1:K1 + 1]), r=[t1, wk], w=[hh])
                em.op("dve", lambda e: e.tensor_tensor(t2[0:65, :], pim[0:65, :], rn[0:65, sl], ALU.mult), r=[pim, rn], w=[t2])
                em.op("act", lambda e: e.activation(hh[0:65, 1, sl], t2[0:65, :], AF.Identity, scale=wk[0:65, K1:K1 + 1]), r=[t2, wk], w=[hh])
            em.dma(HH[K1].rearrange("r k c -> k r c"), hh[0:65, :, :], r=[hh], w=[HHr], eng="pool")

        self.prefetch_loop(128, f2_load, f2_body)

    def pass_mid(self):
        em = self.em
        self.reset()
        ADv, ADvr = self.scratch("AD0", [128, 2, 128, D], BF16)
        HH, HHr = self.scratch("HH", [128, 2, 65, D])
        ZD, ZDr = self.scratch("ZD", [128, 2, 128, D], BF16)
        c2 = self.load_const("c2", mm=True)
        s2 = self.load_const("s2", mm=True)
        ns2 = self.load_const("ns2", mm=True)
        ic = self.alloc([128, 128], "ic", mm=True)
        em.dma(ic[0:65, :], self.C["ic"], w=[ic])
        is_ = self.alloc([128, 128], "is", mm=True)
        em.dma(is_[0:65, :], self.C["is"], w=[is_])
        nis = self.alloc([128, 128], "nis", mm=True)
        em.dma(nis[0:65, :], self.C["nis"], w=[nis])
        tir = self.load_const("tir")
        tii = self.load_const("tii")
        avr = self.rot([128, 2, D], 3, "av", mm=True)
        hhr = self.rot([128, 2, D], 3, "hh")
        yr = self.rot([128, 2, 512], 3, "y", mm=True)
        zr = self.rot([128, 2, D], 3, "z", mm=True)
        tmpr = self.rot([128, 512], 12, "tmp")
        pp = self.psum(8)
        def mid_load(K1):
            av, hh = avr.get(), hhr.get()
            em.dma(av[:], ADv[:, :, K1, :], r=[ADvr], w=[av])
            em.dma(hh[0:65, :, :], HH[K1].rearrange("r k c -> k r c"), r=[HHr], w=[hh])
            return av, hh

        pend = []

        def stage2(K1, hf, y, z):
            pzr, pzi = pp.get(), pp.get()
            for n, (tb, ri) in enumerate([(ic, 0), (nis, 1)]):
                em.op("pe", lambda e: e.matmul(pzr[:], tb[0:65, :], y[0:65, ri, :], start=(n == 0), stop=(n == 1)), r=[tb, y], w=[pzr])
            for n, (tb, ri) in enumerate([(ic, 1), (is_, 0)]):
                em.op("pe", lambda e: e.matmul(pzi[:], tb[0:65, :], y[0:65, ri, :], start=(n == 0), stop=(n == 1)), r=[tb, y], w=[pzi])
            self.twiddle(pzr, pzi, tir[:, K1:K1 + 1], tii[:, K1:K1 + 1], [tir, tii], z, hf, tmpr, 128)
            if hf == 1:
                em.dma(ZD[K1].rearrange("r b c -> b r c"), z[:], r=[z], w=[ZDr], eng="pool")

        def mid_body(K1, tl):
            av, hh = tl
            z = zr.get()
            for hf in range(2):
                sl = slice(hf * 512, (hf + 1) * 512)
                pxr, pxi = pp.get(), pp.get()
                for n, (tb, ri) in enumerate([(c2, 0), (s2, 1)]):
                    em.op("pe", lambda e: e.matmul(pxr[0:65, :], tb[:, 0:65], av[:, ri, sl], start=(n == 0), stop=(n == 1)),
                          r=[tb, av], w=[pxr])
                for n, (tb, ri) in enumerate([(c2, 1), (ns2, 0)]):
                    em.op("pe", lambda e: e.matmul(pxi[0:65, :], tb[:, 0:65], av[:, ri, sl], start=(n == 0), stop=(n == 1)),
                          r=[tb, av], w=[pxi])
                if pend:
                    stage2(*pend.pop(0))
                ta, tb_, tc, td = tmpr.get(), tmpr.get(), tmpr.get(), tmpr.get()
                y = yr.get()
                em.op("dve", lambda e: e.tensor_tensor(ta[0:65, :], pxr[0:65, :], hh[0:65, 0, sl], ALU.mult), r=[pxr, hh], w=[ta])
                em.op("dve", lambda e: e.tensor_tensor(tb_[0:65, :], pxi[0:65, :], hh[0:65, 1, sl], ALU.mult), r=[pxi, hh], w=[tb_])
                em.op("pool", lambda e: e.tensor_tensor(y[0:65, 0, :], ta[0:65, :], tb_[0:65, :], ALU.subtract), r=[ta, tb_], w=[y])
                em.op("dve", lambda e: e.tensor_tensor(tc[0:65, :], pxr[0:65, :], hh[0:65, 1, sl], ALU.mult), r=[pxr, hh], w=[tc])
                em.op("dve", lambda e: e.tensor_tensor(td[0:65, :], pxi[0:65, :], hh[0:65, 0, sl], ALU.mult), r=[pxi, hh], w=[td])
                em.op("pool", lambda e: e.tensor_tensor(y[0:65, 1, :], tc[0:65, :], td[0:65, :], ALU.add), r=[tc, td], w=[y])
                pend.append((K1, hf, y, z))

        self.prefetch_loop(128, mid_load, mid_body)
        while pend:
            stage2(*pend.pop(0))

    def pass_i2(self, j):
        em = self.em
        self.reset()
        Tn, nt = self.Tn, self.nt
        ZD, ZDr = self.scratch("ZD", [128, 2, 128, D], BF16)
        U0, U0r = self.scratch("U0T", [D, Tn + 2])
        GT, GTr = self.scratch("GT", [D, Tn], BF16)
        iwc = self.load_const("iwc", mm=True)
        iwsn = self.load_const("iwsn", mm=True)
        cw = self.W["b_conv_w"][j]
        w0c = self.load_cols(cw[0, 0:D], KC, "w0c")
        w1c = self.load_cols(cw[1, 0:D], KC, "w1c")
        w2c = self.load_cols(cw[2, 0:D], KC, "w2c")
        bc = self.load_cols(self.W["b_conv_b"][j, 0:D], KC, "bc")
        yT = self.alloc([128, Tn], "yT")
        yTv = yT[:].rearrange("p (i b) -> p b i", b=128)
        NB = 512 // nt
        zzr = self.rot([128, 2, NB, 128], 3, "zz", mm=True)
        TB = 512
        u0r = self.rot([128, TB + 2], 3, "u0")
        mpr = self.rot([128, TB], 3, "mp")
        mnr = self.rot([128, TB], 3, "mn")
        x0r = self.rot([128, TB], 2, "x0")
        ttr = self.rot([128, TB], 2, "tt")
        gr = self.rot([128, TB], 2, "g", mm=True)
        pp = self.psum(8)
        for cc in range(KC):
            def zz_load(ib):
                b0 = ib * NB
                zz = zzr.get()
                em.dma(zz[:], ZD[:, :, b0:b0 + NB, cc * 128:(cc + 1) * 128], r=[ZDr], w=[zz])
                return zz

            def zz_body(ib, zz):
                b0 = ib * NB
                ps = pp.get()
                for bb in range(NB):
                    em.op("pe", lambda e: e.matmul(ps[:, bb * nt:(bb + 1) * nt], zz[:, 0, bb, :], iwc[:, 0:nt], start=True, stop=False),
                          r=[zz, iwc], w=[ps])
                    em.op("pe", lambda e: e.matmul(ps[:, bb * nt:(bb + 1) * nt], zz[:, 1, bb, :], iwsn[:, 0:nt], start=False, stop=True),
                          r=[zz, iwsn], w=[ps])
                if ib % 2 == 0:
                    em.op("act", lambda e: e.copy(yTv[:, b0:b0 + NB, :], ps[:, 0:NB * nt].rearrange("p (b i) -> p b i", b=NB)), r=[ps], w=[yT])
                else:
                    em.op("dve", lambda e: e.tensor_copy(yTv[:, b0:b0 + NB, :], ps[:, 0:NB * nt].rearrange("p (b i) -> p b i", b=NB)), r=[ps], w=[yT])

            self.prefetch_loop(128 // NB, zz_load, zz_body)

            def g_load(blk):
                t0 = blk * TB
                u0 = u0r.get()
                em.dma(u0[:], U0[cc * 128:(cc + 1) * 128, t0:t0 + TB + 2], r=[U0r], w=[u0])
                mpt, mnt = mpr.get(), mnr.get()
                em.dma(mpt[:], self.C["mpR"][:, t0:t0 + TB], w=[mpt])
                em.dma(mnt[:], self.C["mnR"][:, t0:t0 + TB], w=[mnt])
                return u0, mpt, mnt

            def g_body(blk, tl):
                u0, mpt, mnt = tl
                t0 = blk * TB
                x0 = x0r.get()
                em.op("dve", lambda e: e.tensor_scalar(x0[:], u0[:, 1:TB + 1], w1c[:, cc:cc + 1], bc[:, cc:cc + 1], ALU.mult, ALU.add),
                      r=[u0, w1c, bc], w=[x0])
                t1 = ttr.get()
                em.op("pool", lambda e: e.tensor_tensor(t1[:], u0[:, 0:TB], mpt[:], ALU.mult), r=[u0, mpt], w=[t1])
                em.op("dve", lambda e: e.scalar_tensor_tensor(x0[:], t1[:], w0c[:, cc:cc + 1], x0[:], ALU.mult, ALU.add), r=[t1, w0c, x0], w=[x0])
                t2 = ttr.get()
                em.op("dve", lambda e: e.tensor_tensor(t2[:], u0[:, 2:TB + 2], mnt[:], ALU.mult), r=[u0, mnt], w=[t2])
                em.op("dve", lambda e: e.scalar_tensor_tensor(x0[:], t2[:], w2c[:, cc:cc + 1], x0[:], ALU.mult, ALU.add), r=[t2, w2c, x0], w=[x0])
                g = gr.get()
                em.op("pool", lambda e: e.tensor_tensor(g[:], x0[:], yT[:, t0:t0 + TB], ALU.mult), r=[x0, yT], w=[g])
                em.dma(GT[cc * 128:(cc + 1) * 128, t0:t0 + TB], g[:], r=[g], w=[GTr], eng="pool")

            self.prefetch_loop(Tn // TB, g_load, g_body)


_CACHE = {}


def run_cores(xs, mems, lsegs, weights, kinds, debug_out=()):
    nt = xs[0].shape[0] // 128
    consts = [build_consts(nt, l) for l in lsegs]
    wshapes = {k: weights[k].shape for k in WEIGHT_NAMES}
    cshapes = {k: v.shape for k, v in consts[0].items()}
    key = (nt, tuple(kinds), tuple(debug_out))
    if key not in _CACHE:
        _CACHE[key] = Builder(nt, kinds, wshapes, cshapes, debug_out).build()
    nc = _CACHE[key]
    in_maps = []
    for ci in range(len(xs)):
        m = {k: np.ascontiguousarray(weights[k], dtype=np.float32) for k in WEIGHT_NAMES}
        for k, v in consts[ci].items():
            m["c_" + k] = v.astype(ml_dtypes.bfloat16) if k in BF_CONSTS else v
        m["x"] = np.ascontiguousarray(xs[ci], dtype=np.float32)
        m["mem"] = np.ascontiguousarray(mems[ci], dtype=np.float32)
        in_maps.append(m)
    res = run_bass_kernel_spmd(nc, in_maps, core_ids=list(range(len(xs))))
    return res.results


def kernel(**inputs):
    xp = np.asarray(inputs["x_prompt"], dtype=np.float32)
    xs_ = np.asarray(inputs["x_sample"], dtype=np.float32)
    mp = np.asarray(inputs["mem_prompt"], dtype=np.float32)
    ms = np.asarray(inputs["mem_sample"], dtype=np.float32)
    weights = {k: np.asarray(inputs[k], dtype=np.float32) for k in WEIGHT_NAMES}
    xs, mems, lsegs = [], [], []
    for c in range(4):
        xs.append(xp[4 * c:4 * c + 4].reshape(8192, D))
        mems.append(mp[4 * c:4 * c + 4])
        lsegs.append(2048)
    for c in range(4):
        xs.append(xs_[c])
        mems.append(np.broadcast_to(ms[c][None], (4, MEM, D)))
        lsegs.append(8192)
    res = run_cores(xs, mems, lsegs, weights, ["a", "b", "c", "a"])
    yp = np.stack([res[c]["y"] for c in range(4)]).reshape(16, 2048, D)
    ys = np.stack([res[4 + c]["y"] for c in range(4)]).reshape(4, 8192, D)
    return (yp.astype(np.float32), ys.astype(np.float32))
```

```python
import contextlib
import math
import numpy as np
import ml_dtypes
import concourse.bass as bass
import concourse.mybir as mybir
from concourse.bass_utils import run_bass_kernel_spmd

F32 = mybir.dt.float32
BF16 = mybir.dt.bfloat16
AF = mybir.ActivationFunctionType
ALU = mybir.AluOpType
AX = mybir.AxisListType

D = 1024
KC = 8
DFF = 2816
FC = 22
MEM = 256
EPS = 1e-6
NFFT = 16384
TWO_PI = 2.0 * math.pi

DBG = {}
EPOCH = 30000
NRING = 16
DEPOCH = 1800


class Res:
    __slots__ = ("lw", "rd", "name", "excl")

    def __init__(self, name="", excl=False):
        self.lw = None
        self.rd = {}
        self.name = name
        self.excl = excl


class T:
    def __init__(self, t, name="", mm=False):
        self.t = t
        self.r = Res(name)
        self.mm = mm

    def __getitem__(self, idx):
        return self.t[idx]


def _res(x):
    return x.r if isinstance(x, T) else x


class _Rec:
    def __init__(self):
        self.call = None

    def __getattr__(self, name):
        def f(*a, **k):
            self.call = (name, a, k)
            return self
        return f


class Emitter:
    ENGS = ("pe", "act", "dve", "pool", "sp")

    def __init__(self, nc, stack):
        self.nc = nc
        self.stack = stack
        self.ops = {e: [] for e in self.ENGS}
        self.ccount = {e: 0 for e in self.ENGS}
        self.dcount = {e: 0 for e in self.ENGS}
        self.seen = {e: {} for e in self.ENGS}
        self.sems = {}
        self.nsem = 0

    def sem(self, key):
        s = self.sems.get(key)
        if s is None:
            s = self.stack.enter_context(self.nc.semaphore("s%d" % self.nsem))
            self.nsem += 1
            self.sems[key] = s
        return s

    def _target(self, ident):
        k, e, i = ident
        if k == "c":
            return ("c", e, i // EPOCH), (i % EPOCH) + 1
        slot = i % NRING
        j = i // NRING
        return ("d", e, slot, j // DEPOCH), 16 * ((j % DEPOCH) + 1)

    def _need(self, eng, ident, waits):
        if ident is None:
            return
        key, val = self._target(ident)
        if self.seen[eng].get(key, 0) >= val:
            return
        self.seen[eng][key] = val
        waits.append((key, val))

    def op(self, eng, fn, r=(), w=(), dma=False, ns=False):
        waits = []
        deps = []
        for x in r:
            res = _res(x)
            deps.append(res.lw)
            if res.excl:
                deps.extend(v for k, v in res.rd.items() if k[0] != eng)
        for x in w:
            res = _res(x)
            deps.append(res.lw)
            deps.extend(res.rd.values())
        for d in deps:
            if d is None:
                continue
            if d[1] == eng and d[0] == "c" and not dma and (eng == "pe" or ns):
                continue
            self._need(eng, d, waits)
        if dma:
            i = self.dcount[eng]
            self.dcount[eng] += 1
            ident = ("d", eng, i)
            if i >= NRING:
                self._need(eng, ("d", eng, i - NRING), waits)
        else:
            i = self.ccount[eng]
            self.ccount[eng] += 1
            ident = ("c", eng, i)
        key, val = self._target(ident)
        rec = _Rec()
        fn(rec)
        assert rec.call is not None
        call = rec.call
        self.ops[eng].append((call, waits, key, 16 if dma else 1))
        for x in r:
            _res(x).rd[(eng, dma)] = ident
        for x in w:
            res = _res(x)
            res.lw = ident
            res.rd = {}
        return ident

    def dma(self, out, in_, r=(), w=(), slow=False, eng="sp"):
        if slow:
            return self.op(eng, lambda e: e.dma_start(out=out, in_=in_, allow_slow_non_contiguous=True),
                           r=r, w=w, dma=True)
        return self.op(eng, lambda e: e.dma_start(out=out, in_=in_), r=r, w=w, dma=True)

    def barrier(self, engs=None):
        for x in (engs or self.ENGS):
            waits = []
            for e in self.ENGS:
                if self.ccount[e] and e != x:
                    self._need(x, ("c", e, self.ccount[e] - 1), waits)
                n = self.dcount[e]
                for i in range(max(0, n - NRING), n):
                    self._need(x, ("d", e, i), waits)
            if self.ccount[x]:
                self._need(x, ("c", x, self.ccount[x] - 1), waits)
            if waits:
                self.ops[x].append((None, waits, None, 0))

    def replay(self):
        nc = self.nc
        for e in self.ENGS:
            for fn, waits, key, inc in self.ops[e]:
                for k, v in waits:
                    self.sem(k)
                if key is not None:
                    self.sem(key)
        engmap = {"pe": "tensor", "act": "scalar", "dve": "vector", "pool": "gpsimd", "sp": "sync"}
        with nc.Block() as block:
            for e in self.ENGS:
                if not self.ops[e]:
                    continue
                ops = self.ops[e]

                def body(engine, ops=ops):
                    for fn, waits, key, inc in ops:
                        for k, v in waits:
                            engine.wait_ge(self.sems[k], v)
                        if fn is not None:
                            name, a, k = fn
                            getattr(engine, name)(*a, **k).then_inc(self.sems[key], inc)

                getattr(block, engmap[e])(body)


class TiledW:
    def __init__(self, ap4, gcols):
        self.ap = ap4
        self.g = gcols

    def rearrange(self, *a, **k):
        return self

    def __getitem__(self, idx):
        sp, sc, sn = idx
        c0, n = sn.start, sn.stop - sn.start
        g, off = c0 // self.g, c0 % self.g
        assert off + n <= self.g, (c0, n, self.g)
        return self.ap[g][:, :, off:off + n]


TILE_SPECS = {
    "a_w_in": ("a_w_in", 8, 512, 0, 9216), "a_w_out": ("a_w_out", 8, 512, 0, 1024),
    "b_w_in": ("b_w_in", 8, 512, 0, 3072), "b_w_out": ("b_w_out", 8, 512, 0, 1024),
    "c_w_in": ("c_w_in", 8, 512, 0, 2048), "c_w_out": ("c_w_out", 8, 512, 0, 1024),
    "x_w_q": ("x_w_q", 8, 512, 0, 1024), "x_w_kv": ("x_w_kv", 8, 512, 0, 2048), "x_w_o": ("x_w_o", 8, 512, 0, 1024),
    "f_w_g": ("f_w_gu", 8, 512, 0, 2816), "f_w_u": ("f_w_gu", 8, 512, 2816, 2816),
    "f_w_down": ("f_w_down", 22, 128, 0, 1024),
}


class Rot:
    def __init__(self, tiles):
        self.tiles = tiles
        self.i = 0

    def get(self):
        t = self.tiles[self.i % len(self.tiles)]
        self.i += 1
        return t


A_GROUPS = ((128, 1), (512, 4), (2048, 16))


def _coarse_index(nt, lseg):
    tiles_per_seg = lseg // 128
    i = np.arange(nt)
    if lseg >= nt * 128:
        return i
    return (i // tiles_per_seg) * (2 * tiles_per_seg) + (i % tiles_per_seg)


def build_consts(nt, lseg):
    Tn = nt * 128
    f32 = np.float32
    c = {}
    c["ident"] = np.eye(128, dtype=f32)
    c["identb"] = np.eye(128, dtype=f32)
    obm = np.zeros((2, 128, 128), f32)
    obm[0, :, 0:64] = 1.0
    obm[1, :, 64:128] = 1.0
    c["obm"] = obm
    R = np.zeros((64, 64), f32)
    for j in range(32):
        R[j + 32, j] = -1.0
        R[j, j + 32] = 1.0
    rb = np.zeros((128, 128), f32)
    rb[:64, :64] = R
    rb[64:, 64:] = R
    c["rblk"] = rb
    pos = (np.arange(Tn) % lseg).astype(np.float64)
    inv = 10000.0 ** (-np.arange(0, 64, 2, dtype=np.float64) / 64)
    inv32 = inv.astype(f32).astype(np.float64)
    ang = (pos.astype(f32)[:, None] * inv32.astype(f32)[None, :]).astype(f32)
    cos = np.cos(ang.astype(np.float64)).astype(f32)
    sin = np.sin(ang.astype(np.float64)).astype(f32)
    rows = np.arange(128) % 32
    c["ropec"] = np.ascontiguousarray(cos[:, rows].T)
    c["ropes"] = np.ascontiguousarray(sin[:, rows].T)
    for gi, (window, d) in enumerate(A_GROUPS):
        nsub = Tn // d
        ntile = nsub // 128
        nseg_sub = lseg // d
        m = np.zeros((ntile, 2, 128, 128), f32)
        kk = np.arange(128)[:, None]
        qq = np.arange(128)[None, :]
        for t in range(ntile):
            nq = 128 * t + qq
            for X in range(2):
                nk = 128 * t - 64 + 128 * X + kk
                ok = (np.abs(nq - nk) <= 64) & (nk >= 0) & (nk < nsub) & ((nq // nseg_sub) == (nk // nseg_sub))
                m[t, X] = ok.astype(f32)
        c["abias%d" % gi] = ((m - 1.0) * 30000.0).astype(f32)
    L = lseg
    tl = np.linspace(0.0, 1.0, L, dtype=f32)
    w = (2.0 * math.pi * np.arange(L, dtype=f32) / L).astype(f32)
    bands = 16
    fb = np.linspace(1e-4, bands - 1, bands, dtype=f32)[None, :]
    fw = (fb * w[:, None]).astype(f32)
    z = np.concatenate([tl[:, None], np.cos(fw.astype(np.float64)).astype(f32),
                        -np.sin(fw.astype(np.float64)).astype(f32)], axis=-1)
    reps = Tn // L
    zfull = np.tile(z, (reps, 1))
    c["zT"] = np.ascontiguousarray(zfull.T)
    tfull = np.tile(tl, reps)
    c["negt"] = np.ascontiguousarray((-tfull).reshape(nt, 128).T)
    vm = np.zeros((128, nt), f32)
    vm[:, : L // 128] = 1.0
    c["vmask"] = vm
    max_decay = math.log(1e-2) / 0.3
    min_decay = math.log(1e-2) / 1.5
    deltas = np.linspace(min_decay, max_decay, D, dtype=f32)
    c["absdelta"] = np.ascontiguousarray(np.broadcast_to(np.abs(deltas)[None, :], (128, D)))
    a = _coarse_index(nt, lseg).astype(np.float64)
    ah = np.arange(nt, dtype=np.float64)
    K1 = np.arange(128, dtype=np.float64)
    hvalid = (np.arange(nt) < L // 128).astype(np.float64)

    def cs(x):
        xm = np.mod(x, 1.0)
        return np.cos(TWO_PI * xm).astype(f32), np.sin(TWO_PI * xm).astype(f32)

    cc, ss = cs(np.outer(a, K1) / 128.0)
    c["wselc"] = cc
    c["wsels"] = -ss
    cc, ss = cs(np.outer(ah, K1) / 128.0)
    c["hselc"] = (cc * hvalid[:, None]).astype(f32)
    c["hsels"] = (-ss * hvalid[:, None]).astype(f32)
    c["hselsn"] = (ss * hvalid[:, None]).astype(f32)
    b = np.arange(128, dtype=np.float64)
    cc, ss = cs(np.outer(K1, b) / NFFT)
    c["twr"] = cc
    c["twi"] = -ss
    c["twin"] = ss
    K2 = np.arange(65, dtype=np.float64)
    cc, ss = cs(np.outer(b, K2) / 128.0)
    c["c2"] = cc
    c["s2"] = ss
    c["ns2"] = -ss
    wk = np.zeros((65, 128), f32)
    for k2 in range(65):
        for k1 in range(128):
            k = k1 + 128 * k2
            if k == 0 or k == NFFT // 2:
                wk[k2, k1] = 1.0 / NFFT
            elif k < NFFT // 2:
                wk[k2, k1] = 2.0 / NFFT
    c["wk"] = wk
    cc, ss = cs(np.outer(K2, b) / 128.0)
    c["ic"] = cc
    c["is"] = ss
    c["nis"] = -ss
    cc, ss = cs(np.outer(b, K1) / NFFT)
    c["tir"] = cc
    c["tii"] = ss
    cc, ss = cs(np.outer(K1, a) / 128.0)
    c["iwc"] = cc
    c["iwsn"] = -ss
    tok = np.arange(Tn)
    mp = ((tok % L) != 0).astype(f32)
    mn = ((tok % L) != (L - 1)).astype(f32)
    c["mp"] = np.ascontiguousarray(mp.reshape(nt, 128).T)
    c["mn"] = np.ascontiguousarray(mn.reshape(nt, 128).T)
    c["mpR"] = np.ascontiguousarray(np.broadcast_to(mp[None, :], (128, Tn)))
    c["mnR"] = np.ascontiguousarray(np.broadcast_to(mn[None, :], (128, Tn)))
    return c


BF_CONSTS = {"rblk", "obm", "identb", "abias0", "abias1", "abias2", "c2", "s2", "ns2", "ic", "is", "nis", "iwc", "iwsn", "wselc", "wsels", "hselc", "hsels", "hselsn"}
BIG_WEIGHTS = ["a_w_in", "a_w_out", "b_w_in", "b_w_out", "c_w_in", "c_w_out", "c_w_s", "x_w_q", "x_w_kv", "x_w_o", "f_w_gu", "f_w_down"]

WEIGHT_NAMES = ["g_mix", "g_cross", "g_ffn", "g_final", "a_w_in", "a_w_out", "b_w_in", "b_conv_w", "b_conv_b",
                "b_f_w1", "b_f_b1", "b_f_w2", "b_f_b2", "b_f_w3", "b_f_b3", "b_f_wout", "b_f_freq", "b_bias_d",
                "b_w_out", "c_w_in", "c_ln_g", "c_ln_b", "c_w_s", "c_b_s", "c_w_out", "x_w_q", "x_w_kv", "x_w_o",
                "f_w_gu", "f_w_down"]


class Builder:
    def __init__(self, nt, kinds, wshapes, cshapes, debug_out=()):
        self.nt = nt
        self.Tn = nt * 128
        self.kinds = kinds
        self.debug_out = set(debug_out)
        nc = bass.Bass("TRN2", target_bir_lowering=False)
        self.nc = nc
        self.W = {}
        for name in WEIGHT_NAMES:
            self.W[name] = nc.dram_tensor(name, list(wshapes[name]), F32, kind="ExternalInput").ap()
        self.C = {}
        for name, shp in cshapes.items():
            self.C[name] = nc.dram_tensor("c_" + name, list(shp), BF16 if name in BF_CONSTS else F32, kind="ExternalInput").ap()
        self.x_in = nc.dram_tensor("x", [self.Tn, D], F32, kind="ExternalInput").ap()
        self.mem_in = nc.dram_tensor("mem", [4, MEM, D], F32, kind="ExternalInput").ap()
        self.y_out = nc.dram_tensor("y", [self.Tn, D], F32, kind="ExternalOutput").ap()
        self.scr = {}
        self.scr_res = {}

    def scratch(self, name, shape, dt=F32):
        if name not in self.scr:
            kind = "ExternalOutput" if name in self.debug_out else "Internal"
            self.scr[name] = self.nc.dram_tensor("scr_" + name, list(shape), dt, kind=kind).ap()
            self.scr_res[name] = Res(name)
        return self.scr[name], self.scr_res[name]

    def dump(self, name, tile, shape):
        if name not in self.debug_out or name in self.scr:
            return
        d, dr = self.scratch(name, shape)
        self.em.dma(d, tile[:], r=[tile], w=[dr])

    def defer(self, fn):
        self.deferred.append(fn)

    def run_deferred(self):
        d, self.deferred = self.deferred, []
        for fn in d:
            fn()

    def reset(self):
        self.run_deferred()
        self.em.barrier()
        self.off = 0
        self.off_mm = 0
        self.psi = 0

    def alloc(self, shape, name="", mm=False):
        n = int(np.prod(shape[1:]))
        if mm:
            assert self.off_mm + n <= self.mm_words, ("mm arena overflow", name, self.off_mm, n)
            ap = self.arena_mm[0:shape[0], self.off_mm:self.off_mm + n]
            self.off_mm += n
        else:
            assert self.off + n <= self.arena_words, ("arena overflow", name, self.off, n)
            ap = self.arena[0:shape[0], self.off:self.off + n]
            self.off += n
        if len(shape) == 3:
            ap = ap.rearrange("p (a b) -> p a b", a=shape[1])
        elif len(shape) == 4:
            ap = ap.rearrange("p (a b c) -> p a b c", a=shape[1], b=shape[2])
        return T(ap, name, mm)

    def rot(self, shape, n, name="", mm=False):
        return Rot([self.alloc(shape, "%s%d" % (name, i), mm) for i in range(n)])

    def psum(self, n):
        assert self.psi + n <= 8
        r = Rot(self.pbanks[self.psi:self.psi + n])
        self.psi += n
        return r

    def build(self):
        nc = self.nc
        with contextlib.ExitStack() as st:
            self.em = Emitter(nc, st)
            em = self.em
            self.arena_words = 16600
            self.mm_words = 72800
            self.arena = st.enter_context(nc.sbuf_tensor("arena", [128, self.arena_words], F32))
            self.arena_mm = st.enter_context(nc.sbuf_tensor("arena_mm", [128, self.mm_words], BF16))
            self.off_mm = 0
            self.pbanks = [T(st.enter_context(nc.psum_tensor("pb%d" % i, [128, 512], F32)), "pb%d" % i)
                           for i in range(8)]
            for pb in self.pbanks:
                pb.r.excl = True
            self.off = 0
            self.psi = 0
            self.deferred = []
            self.program()
            self.run_deferred()
            em.barrier()
            em.replay()
        return nc

    def load_const(self, name, shape=None, slow=False, mm=False):
        ap = self.C[name]
        t = self.alloc(list(ap.shape) if shape is None else shape, name, mm)
        self.em.dma(t[:], ap, w=[t], slow=slow)
        return t

    @staticmethod
    def prefetch_loop(n, load_fn, body_fn):
        nxt = load_fn(0) if n > 0 else None
        for i in range(n):
            cur = nxt
            nxt = load_fn(i + 1) if i + 1 < n else None
            body_fn(i, cur)

    def ones_tile(self):
        t = self.alloc([128, 128], "ones", mm=True)
        self.em.op("dve", lambda e: e.memset(t[:], 1.0), w=[t])
        return t

    def zero_fill(self, t, n):
        self.em.op("pool", lambda e: e.memset(t[:], 0.0), w=[t])

    def load_cols(self, vec_ap, nchunk, name):
        t = self.alloc([128, nchunk], name)
        self.em.dma(t[:], vec_ap.rearrange("(c p) -> p c", p=128), w=[t], slow=True)
        return t

    def load_rep(self, vec_ap, n, name):
        t = self.alloc([128, n], name)
        self.em.dma(t[:], vec_ap.partition_broadcast(128), w=[t], slow=True)
        return t

    def rmsnorm(self, x, xn, sq, gcols, gi, ones, pp, rstd, TB):
        em = self.em
        em.op("act", lambda e: e.activation(sq[:], x[:], AF.Square), r=[x], w=[sq])
        ps = pp.get()
        for c in range(KC):
            em.op("pe", lambda e, c=c: e.matmul(ps[:, 0:TB], ones[:], sq[:, c, :], start=(c == 0), stop=(c == KC - 1)),
                  r=[ones, sq], w=[ps])
        em.op("dve", lambda e: e.tensor_scalar(rstd[:], ps[:, 0:TB], 1.0 / D, EPS, ALU.mult, ALU.add), r=[ps], w=[rstd])
        em.op("act", lambda e: e.activation(rstd[:], rstd[:], AF.Sqrt), r=[rstd], w=[rstd])
        em.op("dve", lambda e: e.reciprocal(rstd[:], rstd[:]), r=[rstd], w=[rstd])
        for c in range(KC):
            em.op("dve", lambda e, c=c: e.scalar_tensor_tensor(xn[:, c, :], x[:, c, :], gcols[:, gi * KC + c:gi * KC + c + 1],
                                                               rstd[:], ALU.mult, ALU.mult),
                  r=[x, gcols, rstd], w=[xn], ns=(c > 0))

    def linear_fm(self, w_ap, col0, ncols, src, kch, TB, wrot, pp, epi, group=512):
        em = self.em
        wv = w_ap.rearrange("(c p) n -> p c n", p=128)
        for g0 in range(0, ncols, group):
            gn = min(group, ncols - g0)
            wt = wrot.get()
            wtv = wt[:, 0:kch * gn].rearrange("p (c n) -> p c n", c=kch)
            em.dma(wtv, wv[:, :, col0 + g0:col0 + g0 + gn], w=[wt])
            self.run_deferred()
            for m in range(gn // 128):
                ps = pp.get()
                for k in range(kch):
                    em.op("pe", lambda e, k=k, m=m, ps=ps, wtv=wtv: e.matmul(
                        ps[:, 0:TB], wtv[:, k, m * 128:(m + 1) * 128], src[:, k, :], start=(k == 0), stop=(k == kch - 1)),
                        r=[wt, src], w=[ps])
                epi((g0 // 128) + m, ps)

    def pass_cast_weights(self):
        em = self.em
        self.reset()
        self.Wb = {}
        ldr = self.rot([128, 4096], 3, "wld")
        cvr = self.rot([128, 4096], 4, "wcv", mm=True)
        n = 0

        def cast(ld, cv, sz):
            nonlocal n
            eng = ("act", "dve")[n % 2]
            n += 1
            if eng == "act":
                em.op("act", lambda e: e.copy(cv[:, 0:sz], ld[:, 0:sz]), r=[ld], w=[cv])
            else:
                em.op(eng, lambda e: e.tensor_copy(cv[:, 0:sz], ld[:, 0:sz]), r=[ld], w=[cv])

        w = self.W["c_w_s"]
        wb, wbr = self.scratch("wb_c_w_s", list(w.shape), BF16)
        self.Wb["c_w_s"] = wb
        wf = w.rearrange("l h p q -> (l h p q)").rearrange("(p f) -> p f", p=128)
        wbf = wb.rearrange("l h p q -> (l h p q)").rearrange("(p f) -> p f", p=128)
        F = wf.shape[1]
        for f0 in range(0, F, 4096):
            fn = min(4096, F - f0)
            ld, cv = ldr.get(), cvr.get()
            em.dma(ld[:, 0:fn], wf[:, f0:f0 + fn], w=[ld])
            cast(ld, cv, fn)
            em.dma(wbf[:, f0:f0 + fn], cv[:, 0:fn], r=[cv], w=[wbr], eng="pool")
        for oname, (src, kch, gcols, base, ncols) in TILE_SPECS.items():
            w = self.W[src]
            Lw = w.shape[0]
            ng = -(-ncols // gcols)
            wt, wtr = self.scratch("wt_" + oname, [Lw, ng, 128, kch, gcols], BF16)
            self.Wb[oname] = [TiledW(wt[l], gcols) for l in range(Lw)]
            for l in range(Lw):
                wv = w[l].rearrange("(c p) n -> p c n", p=128)
                for g in range(ng):
                    gn = min(gcols, ncols - g * gcols)
                    ld, cv = ldr.get(), cvr.get()
                    ldv = ld[:, 0:kch * gn].rearrange("p (c n) -> p c n", c=kch)
                    cvv = cv[:, 0:kch * gn].rearrange("p (c n) -> p c n", c=kch)
                    em.dma(ldv, wv[:, :, base + g * gcols:base + g * gcols + gn], w=[ld])
                    cast(ld, cv, kch * gn)
                    em.dma(wt[l, g][:, :, 0:gn], cvv, r=[cv], w=[wtr], eng="pool")

    def program(self):
        self.pass_cast_weights()
        self.pass_transpose_in()
        self.pass_mem()
        nA = nB = nC = 0
        for li, kind in enumerate(self.kinds):
            if kind == "a":
                self.mixer_a(li, nA)
                self.pass_post(li, "OT", self.Wb["a_w_out"][nA])
                nA += 1
            elif kind == "b":
                self.mixer_b(li, nB)
                self.pass_post(li, "GT", self.Wb["b_w_out"][nB])
                nB += 1
            elif kind == "c":
                self.pass_post(li, None, self.Wb["c_w_out"][nC], sgu=nC)
                nC += 1
            else:
                self.pass_post(li, None, None)
        self.pass_final()

    def pass_transpose_in(self):
        em = self.em
        self.reset()
        xT, xTr = self.scratch("xT", [D, self.Tn])
        ident = self.load_const("ident")
        xin = self.rot([128, D], 3, "xin")
        stg = self.rot([128, KC, 512], 2, "stg")
        pp = self.psum(8)
        xTv = xT.rearrange("(c p) t -> p c t", p=128)
        for b4 in range(0, self.nt, 4):
            nb = min(4, self.nt - b4)
            s = stg.get()
            for j in range(nb):
                i = b4 + j
                xt = xin.get()
                em.dma(xt[:], self.x_in[i * 128:(i + 1) * 128, :], w=[xt])
                pa, pb = pp.get(), pp.get()
                for c in range(KC):
                    ps = pa if c < 4 else pb
                    em.op("pe", lambda e, c=c, ps=ps, xt=xt: e.transpose(ps[:, (c % 4) * 128:(c % 4 + 1) * 128],
                                                                          xt[:, c * 128:(c + 1) * 128], ident[:]),
                          r=[xt, ident], w=[ps])
                em.op("act", lambda e, s=s, j=j, pa=pa: e.copy(s[:, 0:4, j * 128:(j + 1) * 128],
                                                               pa[:].rearrange("p (c t) -> p c t", c=4)), r=[pa], w=[s])
                em.op("dve", lambda e, s=s, j=j, pb=pb: e.tensor_copy(s[:, 4:8, j * 128:(j + 1) * 128],
                                                                      pb[:].rearrange("p (c t) -> p c t", c=4)), r=[pb], w=[s])
            em.dma(xTv[:, :, b4 * 128:(b4 + nb) * 128], s[:, :, 0:nb * 128], r=[s], w=[xTr], eng="pool")

    def pass_mem(self):
        em = self.em
        self.reset()
        mT, mTr = self.scratch("memT", [4, D, MEM], BF16)
        ident = self.load_const("ident")
        xin = self.rot([128, D], 3, "min")
        stg = self.rot([128, KC, 128], 2, "mstg", mm=True)
        pp = self.psum(8)
        for s4 in range(4):
            for j in range(2):
                xt = xin.get()
                em.dma(xt[:], self.mem_in[s4, j * 128:(j + 1) * 128, :], w=[xt])
                pa, pb = pp.get(), pp.get()
                s = stg.get()
                for c in range(KC):
                    ps = pa if c < 4 else pb
                    em.op("pe", lambda e, c=c, ps=ps, xt=xt: e.transpose(ps[:, (c % 4) * 128:(c % 4 + 1) * 128],
                                                                          xt[:, c * 128:(c + 1) * 128], ident[:]),
                          r=[xt, ident], w=[ps])
                em.op("act", lambda e, s=s, pa=pa: e.copy(s[:, 0:4, :], pa[:].rearrange("p (c t) -> p c t", c=4)), r=[pa], w=[s])
                em.op("dve", lambda e, s=s, pb=pb: e.tensor_copy(s[:, 4:8, :], pb[:].rearrange("p (c t) -> p c t", c=4)), r=[pb], w=[s])
                em.dma(mT[s4].rearrange("(c p) t -> p c t", p=128)[:, :, j * 128:(j + 1) * 128], s[:], r=[s], w=[mTr], eng="pool")

    def pass_post(self, li, mixname, wout_ap, sgu=None):
        em = self.em
        self.reset()
        TB = 512
        nblk = self.Tn // TB
        blk_per_seg = self.Tn // 4 // TB
        xT, xTr = self.scratch("xT", [D, self.Tn])
        xTv = xT.rearrange("(c p) t -> p c t", p=128)
        mT, mTr = self.scratch("memT", [4, D, MEM], BF16)
        if mixname is not None:
            mx, mxr = self.scratch(mixname, [D, self.Tn], BF16)
            mxv = mx.rearrange("(c p) t -> p c t", p=128)
        ones = self.ones_tile()
        gm = self.load_cols(self.W["g_mix"].rearrange("l d -> (l d)"), 4 * KC, "gm")
        gc = self.load_cols(self.W["g_cross"].rearrange("l d -> (l d)"), 4 * KC, "gc")
        gf = self.load_cols(self.W["g_ffn"].rearrange("l d -> (l d)"), 4 * KC, "gf")
        xrot = self.rot([128, KC, TB], 2 if sgu is not None else 3, "x")
        a8 = self.rot([128, KC, TB], 3, "a8", mm=True)
        xn_t = self.alloc([128, KC, TB], "xn", mm=True)
        sq_t = self.alloc([128, KC, TB], "sq", mm=True)
        rstd = self.alloc([128, TB], "rstd")
        wrot = self.rot([128, 4096], 5, "w", mm=True)
        kt = self.alloc([128, KC, MEM], "kt", mm=True)
        vt = self.alloc([128, 2, D], "vt", mm=True)
        memt = self.alloc([128, KC, MEM], "memt", mm=True)
        prot = self.rot([128, 2, TB], 3, "pT", mm=True)
        rden = self.rot([128, TB], 2, "rden")
        hbuf = self.alloc([128, FC, TB], "h", mm=True)
        tmpr = self.rot([128, TB], 3, "tmp")
        pp = self.psum(8)
        if sgu is not None:
            j = sgu
            lngR = self.load_rep(self.W["c_ln_g"][j], D, "lng")
            lnbR = self.load_rep(self.W["c_ln_b"][j], D, "lnb")
            wsT = self.alloc([128, 8, 128], "wsT", mm=True)
            em.dma(wsT[:], self.Wb["c_w_s"][j].rearrange("h p q -> q h p"), w=[wsT], slow=True)
            bsR = self.alloc([128, 8, 128], "bsR")
            em.dma(bsR[:].rearrange("p h q -> p (h q)"),
                   self.W["c_b_s"][j].rearrange("h p -> (h p)").partition_broadcast(128), w=[bsR], slow=True)
            zv = self.rot([128, D], 2, "zv")
            zvn = self.rot([128, D], 2, "zvn", mm=True)
            st6 = self.rot([128, 8], 3, "st6")
        wq = self.Wb["x_w_q"][li]
        wkv = self.Wb["x_w_kv"][li]
        wo = self.Wb["x_w_o"][li]
        wg_t = self.Wb["f_w_g"][li]
        wu_t = self.Wb["f_w_u"][li]
        wdn = self.Wb["f_w_down"][li]
        mtr = self.rot([128, KC, TB], 2, "mt", mm=True) if mixname is not None else None

        def post_load(b):
            t0 = b * TB
            x = xrot.get()
            em.dma(x[:], xTv[:, :, t0:t0 + TB], r=[xTr], w=[x])
            mt = None
            if mixname is not None:
                mt = mtr.get()
                em.dma(mt[:], mxv[:, :, t0:t0 + TB], r=[mxr], w=[mt])
            return x, mt

        nxt = post_load(0)
        for b in range(nblk):
            t0 = b * TB
            x, mt = nxt
            if sgu is None:
                nxt = post_load(b + 1) if b + 1 < nblk else None

            def resid(m, ps, x=x):
                em.op("dve", lambda e: e.tensor_tensor(x[:, m, :], x[:, m, :], ps[:, 0:TB], ALU.add), r=[x, ps], w=[x], ns=True)

            if mixname is not None:
                self.linear_fm(wout_ap, 0, D, mt, KC, TB, wrot, pp, resid)
            elif sgu is not None:
                j = sgu
                self.rmsnorm(x, xn_t, sq_t, gm, li, ones, pp, rstd, TB)
                zu = a8.get()

                def epi_zu(m, ps, zu=zu):
                    em.op("act", lambda e: e.activation(zu[:, m, :], ps[:, 0:TB], AF.Gelu), r=[ps], w=[zu])
                self.linear_fm(self.Wb["c_w_in"][j], 0, D, xn_t, KC, TB, wrot, pp, epi_zu)
                nxt = post_load(b + 1) if b + 1 < nblk else None
                gate = a8.get()
                wv = self.Wb["c_w_in"][j].rearrange("(c p) n -> p c n", p=128)
                wts = []
                for h2 in range(2):
                    wt = wrot.get()
                    wtv = wt[:].rearrange("p (c n) -> p c n", c=KC)
                    em.dma(wtv, wv[:, :, D + h2 * 512:D + (h2 + 1) * 512], w=[wt])
                    wts.append((wt, wtv))
                sg_pend = []

                def sgu_b(tl, zn, gate=gate, zu=zu):
                    pa, pb = pp.get(), pp.get()
                    for h in range(8):
                        ps = pa if h < 4 else pb
                        em.op("pe", lambda e: e.matmul(ps[:, (h % 4) * 128:(h % 4 + 1) * 128],
                                                       zn[:, h * 128:(h + 1) * 128], wsT[:, h, :], start=True, stop=True),
                              r=[zn, wsT], w=[ps])
                    for half, ps in ((0, pa), (1, pb)):
                        for hh in range(4):
                            h = half * 4 + hh
                            tt = tmpr.get()
                            em.op("dve", lambda e: e.tensor_tensor(
                                tt[:, 0:128], ps[:, hh * 128:(hh + 1) * 128], bsR[:, h, :], ALU.add), r=[ps, bsR], w=[tt])
                            em.op("pool", lambda e: e.tensor_tensor(
                                gate[:, h, tl * 128:(tl + 1) * 128], zu[:, h, tl * 128:(tl + 1) * 128], tt[:, 0:128], ALU.mult),
                                r=[tt, zu], w=[gate], ns=True)

                for tl in range(TB // 128):
                    z = zv.get()
                    for h2 in range(2):
                        wt, wtv = wts[h2]
                        ps = pp.get()
                        for k in range(KC):
                            em.op("pe", lambda e, k=k, ps=ps, wtv=wtv, tl=tl: e.matmul(
                                ps[:], xn_t[:, k, tl * 128:(tl + 1) * 128], wtv[:, k, :], start=(k == 0), stop=(k == KC - 1)),
                                r=[xn_t, wt], w=[ps])
                        em.op("act", lambda e, ps=ps, z=z, h2=h2: e.activation(z[:, h2 * 512:(h2 + 1) * 512], ps[:], AF.Gelu),
                              r=[ps], w=[z])
                    s6 = st6.get()
                    zn = zvn.get()
                    em.op("dve", lambda e, z=z, s6=s6: e.reduce_sum(s6[:, 0:1], z[:], axis=AX.X), r=[z], w=[s6])
                    em.op("dve", lambda e, s6=s6: e.tensor_scalar(s6[:, 1:2], s6[:, 0:1], -1.0 / D, None, ALU.mult), r=[s6], w=[s6])
                    em.op("dve", lambda e, z=z, s6=s6: e.tensor_scalar(z[:], z[:], s6[:, 1:2], None, ALU.add), r=[z, s6], w=[z])
                    em.op("act", lambda e, z=z, zn=zn, s6=s6: e.activation(zn[:], z[:], AF.Square, accum_out=s6[:, 2:3]),
                          r=[z], w=[zn, s6])
                    em.op("dve", lambda e, s6=s6: e.tensor_scalar(s6[:, 3:4], s6[:, 2:3], 1.0 / D, EPS, ALU.mult, ALU.add), r=[s6], w=[s6])
                    em.op("act", lambda e, s6=s6: e.activation(s6[:, 4:5], s6[:, 3:4], AF.Sqrt), r=[s6], w=[s6])
                    em.op("dve", lambda e, s6=s6: e.reciprocal(s6[:, 5:6], s6[:, 4:5]), r=[s6], w=[s6])
                    em.op("dve", lambda e, z=z, zn=zn, s6=s6: e.scalar_tensor_tensor(zn[:], z[:], s6[:, 5:6], lngR[:], ALU.mult, ALU.mult),
                          r=[z, s6, lngR], w=[zn])
                    em.op("pool", lambda e, zn=zn: e.tensor_tensor(zn[:], zn[:], lnbR[:], ALU.add), r=[zn, lnbR], w=[zn])
                    if sg_pend:
                        sgu_b(*sg_pend.pop(0))
                    sg_pend.append((tl, zn))
                while sg_pend:
                    sgu_b(*sg_pend.pop(0))
                self.linear_fm(wout_ap, 0, D, gate, KC, TB, wrot, pp, resid)

            if not DBG.get('skip_cross'):
                seg = b // blk_per_seg
                if b % blk_per_seg == 0:
                    em.dma(memt[:], mT[seg].rearrange("(c p) t -> p c t", p=128), r=[mTr], w=[memt])

                    def epi_k(m, ps):
                        em.op("act", lambda e: e.copy(kt[:, m, :], ps[:, 0:MEM]), r=[ps], w=[kt])
                    self.linear_fm(wkv, 0, D, memt, KC, MEM, wrot, pp, epi_k)
                    wv = wkv.rearrange("(c p) n -> p c n", p=128)
                    for h2 in range(2):
                        wt = wrot.get()
                        wtv = wt[:].rearrange("p (c n) -> p c n", c=KC)
                        em.dma(wtv, wv[:, :, D + h2 * 512:D + (h2 + 1) * 512], w=[wt])
                        for mc in range(2):
                            ps = pp.get()
                            for k in range(KC):
                                em.op("pe", lambda e, k=k, ps=ps, wtv=wtv, mc=mc: e.matmul(
                                    ps[:], memt[:, k, mc * 128:(mc + 1) * 128], wtv[:, k, :], start=(k == 0), stop=(k == KC - 1)),
                                    r=[memt, wt], w=[ps])
                            em.op("act", lambda e, ps=ps, mc=mc, h2=h2: e.copy(vt[:, mc, h2 * 512:(h2 + 1) * 512], ps[:]), r=[ps], w=[vt])
                self.dump('d_kt', kt, [128, KC, MEM])
                self.dump('d_vt', vt, [128, 2, D])
                self.rmsnorm(x, xn_t, sq_t, gc, li, ones, pp, rstd, TB)
                self.dump('d_xn', xn_t, [128, KC, TB])
                q = a8.get()

                def epi_q(m, ps, q=q):
                    em.op("act", lambda e: e.copy(q[:, m, :], ps[:, 0:TB]), r=[ps], w=[q])
                self.linear_fm(wq, 0, D, xn_t, KC, TB, wrot, pp, epi_q)
                o = a8.get()

                def att_a(h, q=q):
                    pT = prot.get()
                    for mc in range(2):
                        ps = pp.get()
                        for dc in range(2):
                            em.op("pe", lambda e: e.matmul(
                                ps[:, 0:TB], kt[:, 2 * h + dc, mc * 128:(mc + 1) * 128], q[:, 2 * h + dc, :], start=(dc == 0), stop=(dc == 1)),
                                r=[kt, q], w=[ps])
                        em.op("act", lambda e: e.activation(pT[:, mc, :], ps[:, 0:TB], AF.Exp, scale=1.0 / 16.0),
                              r=[ps], w=[pT])
                    return pT

                def att_b(h, pT, o=o):
                    ps = pp.get()
                    for mc in range(2):
                        em.op("pe", lambda e: e.matmul(ps[:, 0:TB], ones[:], pT[:, mc, :], start=(mc == 0), stop=(mc == 1)),
                              r=[ones, pT], w=[ps])
                    rd = rden.get()
                    em.op("dve", lambda e: e.reciprocal(rd[:], ps[:, 0:TB]), r=[ps], w=[rd])
                    for dc in range(2):
                        ps2 = pp.get()
                        for mc in range(2):
                            em.op("pe", lambda e: e.matmul(
                                ps2[:, 0:TB], vt[:, mc, (2 * h + dc) * 128:(2 * h + dc + 1) * 128], pT[:, mc, :], start=(mc == 0), stop=(mc == 1)),
                                r=[vt, pT], w=[ps2])
                        em.op("dve", lambda e: e.tensor_tensor(o[:, 2 * h + dc, :], ps2[:, 0:TB], rd[:], ALU.mult),
                              r=[ps2, rd], w=[o], ns=(dc > 0))

                pTs = att_a(0)
                for h in range(4):
                    pTn = att_a(h + 1) if h < 3 else None
                    att_b(h, pTs)
                    pTs = pTn
                self.dump('d_q', q, [128, KC, TB])
                self.dump('d_o', o, [128, KC, TB])
                self.linear_fm(wo, 0, D, o, KC, TB, wrot, pp, resid)

            if not DBG.get('skip_ffn'):
                self.rmsnorm(x, xn_t, sq_t, gf, li, ones, pp, rstd, TB)
                for g0 in range(0, FC, 4):
                    gn = min(4, FC - g0)
                    wg = wrot.get()
                    wgv = wg[:, 0:KC * gn * 128].rearrange("p (c n) -> p c n", c=KC)
                    em.dma(wgv, wg_t[:, :, g0 * 128:(g0 + gn) * 128], w=[wg])
                    wu = wrot.get()
                    wuv = wu[:, 0:KC * gn * 128].rearrange("p (c n) -> p c n", c=KC)
                    em.dma(wuv, wu_t[:, :, g0 * 128:(g0 + gn) * 128], w=[wu])
                    for m in range(gn):
                        pg, pu = pp.get(), pp.get()
                        for k in range(KC):
                            em.op("pe", lambda e, k=k, m=m, pg=pg, wgv=wgv: e.matmul(
                                pg[:, 0:TB], wgv[:, k, m * 128:(m + 1) * 128], xn_t[:, k, :], start=(k == 0), stop=(k == KC - 1)),
                                r=[wg, xn_t], w=[pg])
                        for k in range(KC):
                            em.op("pe", lambda e, k=k, m=m, pu=pu, wuv=wuv: e.matmul(
                                pu[:, 0:TB], wuv[:, k, m * 128:(m + 1) * 128], xn_t[:, k, :], start=(k == 0), stop=(k == KC - 1)),
                                r=[wu, xn_t], w=[pu])
                        tt = tmpr.get()
                        em.op("act", lambda e, pg=pg, tt=tt: e.activation(tt[:], pg[:, 0:TB], AF.Silu), r=[pg], w=[tt])
                        em.op("dve", lambda e, pu=pu, tt=tt, mm=g0 + m: e.tensor_tensor(hbuf[:, mm, :], tt[:], pu[:, 0:TB], ALU.mult),
                              r=[tt, pu], w=[hbuf], ns=True)
                self.linear_fm(wdn, 0, D, hbuf, FC, TB, wrot, pp, resid, group=128)
            self.defer(lambda x=x, t0=t0: em.dma(xTv[:, :, t0:t0 + TB], x[:], r=[x], w=[xTr]))

    def pass_final(self):
        em = self.em
        self.reset()
        TB = 256
        xT, xTr = self.scratch("xT", [D, self.Tn])
        xTv = xT.rearrange("(c p) t -> p c t", p=128)
        ident = self.load_const("ident")
        ones = self.ones_tile()
        gfin = self.load_cols(self.W["g_final"], KC, "gfin")
        xrot = self.rot([128, KC, TB], 2, "x")
        xnr = self.rot([128, KC, TB], 2, "xnf")
        sq_t = self.alloc([128, KC, TB], "sq", mm=True)
        rstd = self.alloc([128, TB], "rstd")
        yo = self.rot([128, D], 3, "yo")
        pp = self.psum(8)
        for b in range(self.Tn // TB):
            t0 = b * TB
            x = xrot.get()
            em.dma(x[:], xTv[:, :, t0:t0 + TB], r=[xTr], w=[x])
            xn = xnr.get()
            self.rmsnorm(x, xn, sq_t, gfin, 0, ones, pp, rstd, TB)
            for j in range(TB // 128):
                pa, pb = pp.get(), pp.get()
                y = yo.get()
                for c in range(KC):
                    ps = pa if c < 4 else pb
                    em.op("pe", lambda e, c=c, ps=ps, xn=xn, j=j: e.transpose(ps[:, (c % 4) * 128:(c % 4 + 1) * 128],
                                                                             xn[:, c, j * 128:(j + 1) * 128], ident[:]),
                          r=[xn, ident], w=[ps])
                em.op("act", lambda e, y=y, pa=pa: e.copy(y[:, 0:512], pa[:]), r=[pa], w=[y])
                em.op("dve", lambda e, y=y, pb=pb: e.tensor_copy(y[:, 512:1024], pb[:]), r=[pb], w=[y])
                yr = Res("y")
                em.dma(self.y_out[t0 + j * 128:t0 + (j + 1) * 128, :], y[:], r=[y], w=[yr], eng="pool")

    def mixer_a(self, li, j):
        self.pass_a1(li, j)
        self.pass_a2(li, j)

    def pass_a1(self, li, j):
        em = self.em
        self.reset()
        TB = 512
        Tn = self.Tn
        xT, xTr = self.scratch("xT", [D, Tn])
        xTv = xT.rearrange("(c p) t -> p c t", p=128)
        w_in = self.Wb["a_w_in"][j]
        pads = [64 * d for (_, d) in A_GROUPS]
        QT, KT, VV = [], [], []
        for g in range(3):
            QT.append(self.scratch("QT%d" % g, [D, Tn + 2 * pads[g]], BF16))
            KT.append(self.scratch("KT%d" % g, [D, Tn + 2 * pads[g]], BF16))
            VV.append(self.scratch("VV%d" % g, [Tn + 2 * pads[g], D], BF16))
        ones = self.ones_tile()
        rblk = self.load_const("rblk", mm=True)
        gm = self.load_cols(self.W["g_mix"].rearrange("l d -> (l d)"), 4 * KC, "gm")
        zt = self.alloc([128, 1024], "zero", mm=True)
        em.op("pool", lambda e: e.memset(zt[:], 0.0), w=[zt])
        for g in range(3):
            pad = pads[g]
            for side in range(2):
                c0 = 0 if side == 0 else pad + Tn
                for c in range(KC):
                    em.dma(KT[g][0][c * 128:(c + 1) * 128, c0:c0 + pad], zt[:, 0:pad], r=[zt], w=[KT[g][1]])
                for r0 in range(0, pad, 128):
                    rn = min(128, pad - r0)
                    em.dma(VV[g][0][c0 + r0:c0 + r0 + rn, :], zt[0:rn, :], r=[zt], w=[VV[g][1]])
        xrot = self.rot([128, KC, TB], 2, "x")
        xn_t = self.alloc([128, KC, TB], "xn", mm=True)
        sq_t = self.alloc([128, KC, TB], "sq", mm=True)
        rstd = self.alloc([128, TB], "rstd")
        wrot = self.rot([128, 4096], 9, "w", mm=True)
        cbr = self.rot([128, TB], 2, "cb")
        sbr = self.rot([128, TB], 2, "sb")
        qraw = self.rot([128, TB], 3, "qraw", mm=True)
        t1r = self.rot([128, TB], 3, "t1")
        t2r = self.rot([128, TB], 3, "t2")
        stg = self.rot([128, KC, TB], 3, "stg", mm=True)
        vst = self.rot([128, D], 6, "vst", mm=True)
        pp = self.psum(8)
        for b in range(Tn // TB):
            t0 = b * TB
            x = xrot.get()
            em.dma(x[:], xTv[:, :, t0:t0 + TB], r=[xTr], w=[x])
            self.rmsnorm(x, xn_t, sq_t, gm, li, ones, pp, rstd, TB)
            cb, sb = cbr.get(), sbr.get()
            em.dma(cb[:], self.C["ropec"][:, t0:t0 + TB], w=[cb])
            em.dma(sb[:], self.C["ropes"][:, t0:t0 + TB], w=[sb])
            for g in range(3):
                pad = pads[g]
                for jj in range(2):
                    st_ = stg.get()

                    pend = []

                    def rope(m, qr, st_=st_, cb=cb, sb=sb):
                        p2 = pp.get()
                        em.op("pe", lambda e: e.matmul(p2[:, 0:TB], rblk[:], qr[:], start=True, stop=True), r=[rblk, qr], w=[p2])
                        t1, t2 = t1r.get(), t2r.get()
                        em.op("pool", lambda e: e.tensor_tensor(t1[:], qr[:], cb[:], ALU.mult), r=[qr, cb], w=[t1])
                        em.op("dve", lambda e: e.tensor_tensor(t2[:], p2[:, 0:TB], sb[:], ALU.mult), r=[p2, sb], w=[t2])
                        em.op("pool", lambda e: e.tensor_tensor(st_[:, m, :], t1[:], t2[:], ALU.add), r=[t1, t2], w=[st_])

                    def epi(m, ps, pend=pend, rope=rope):
                        qr = qraw.get()
                        em.op("act", lambda e: e.copy(qr[:], ps[:, 0:TB]), r=[ps], w=[qr])
                        if pend:
                            pend.pop()()
                        pend.append(lambda: rope(m, qr))
                    self.linear_fm(w_in, g * 3072 + jj * 1024, 1024, xn_t, KC, TB, wrot, pp, epi)
                    while pend:
                        pend.pop()()
                    dst, dres = (QT[g] if jj == 0 else KT[g])
                    self.defer(lambda dst=dst, dres=dres, st_=st_, pad=pad, t0=t0: em.dma(
                        dst.rearrange("(c p) t -> p c t", p=128)[:, :, pad + t0:pad + t0 + TB], st_[:], r=[st_], w=[dres]))
                wv = w_in.rearrange("(c p) n -> p c n", p=128)
                wts = []
                for h2 in range(2):
                    wt = wrot.get()
                    wtv = wt[:].rearrange("p (c n) -> p c n", c=KC)
                    c0 = g * 3072 + 2048 + h2 * 512
                    em.dma(wtv, wv[:, :, c0:c0 + 512], w=[wt])
                    wts.append((wt, wtv))
                for tl in range(TB // 128):
                    vs = vst.get()
                    for h2 in range(2):
                        wt, wtv = wts[h2]
                        ps = pp.get()
                        for k in range(KC):
                            em.op("pe", lambda e, k=k: e.matmul(ps[:], xn_t[:, k, tl * 128:(tl + 1) * 128], wtv[:, k, :],
                                                                start=(k == 0), stop=(k == KC - 1)), r=[xn_t, wt], w=[ps])
                        em.op("act", lambda e: e.copy(vs[:, h2 * 512:(h2 + 1) * 512], ps[:]), r=[ps], w=[vs])
                    r0 = pad + t0 + tl * 128
                    self.defer(lambda g=g, r0=r0, vs=vs: em.dma(VV[g][0][r0:r0 + 128, :], vs[:], r=[vs], w=[VV[g][1]]))

    def pass_a2(self, li, j):
        em = self.em
        self.reset()
        Tn = self.Tn
        RG = 2048
        pads = [64 * d for (_, d) in A_GROUPS]
        QT, KT, VV = [], [], []
        for g in range(3):
            QT.append(self.scratch("QT%d" % g, [D, Tn + 2 * pads[g]], BF16))
            KT.append(self.scratch("KT%d" % g, [D, Tn + 2 * pads[g]], BF16))
            VV.append(self.scratch("VV%d" % g, [Tn + 2 * pads[g], D], BF16))
        OT, OTr = self.scratch("OT", [D, Tn], BF16)
        Ur = self.rot([128, RG], 2, "U")
        Lr = self.rot([128, RG], 2, "L")
        rl = self.alloc([128, RG], "rl")
        otr = self.rot([128, RG], 2, "ot", mm=True)
        identb = self.load_const("identb", mm=True)
        qz0 = self.rot([128, RG], 2, "qz0", mm=True)
        qz1 = self.rot([128, RG], 2, "qz1", mm=True)
        ktr = self.rot([128, 2 * RG], 2, "kt", mm=True)
        vzr = self.rot([128, 2, 128], 8, "vz", mm=True)
        mkr = self.rot([128, 2, 128], 3, "mk", mm=True)
        pMr = self.rot([128, 4, 128], 6, "pM", mm=True)
        ob = [self.alloc([128, 128], "ob%d" % e, mm=True) for e in range(2)]
        for e_ in range(2):
            em.dma(ob[e_][:], self.C["obm"][e_], w=[ob[e_]])
        for tl in qz0.tiles + qz1.tiles:
            self.zero_fill(tl, RG)
        pss = self.psum(3)
        psu = self.psum(3)
        psl = self.psum(2)
        nv = 0
        for R in range(Tn // RG):
            for hc in range(KC):
                U, L, ot = Ur.get(), Lr.get(), otr.get()
                em.op("pool", lambda e: e.memset(U[:], 0.0), w=[U])
                em.op("pool", lambda e: e.memset(L[:], 0.0), w=[L])
                pend = []
                for g, (window, d) in enumerate(A_GROUPS):
                    pad = pads[g]
                    span = 128 * d
                    ntr = RG // span
                    for tt in range(ntr):
                        t = R * ntr + tt
                        base = t * span
                        q0, q1, kt = qz0.get(), qz1.get(), ktr.get()
                        em.dma(q0[0:64, 0:span], QT[g][0][hc * 128:hc * 128 + 64, pad + base:pad + base + span], r=[QT[g][1]], w=[q0])
                        em.dma(q1[64:128, 0:span], QT[g][0][hc * 128 + 64:hc * 128 + 128, pad + base:pad + base + span], r=[QT[g][1]], w=[q1])
                        em.dma(kt[:, 0:2 * span], KT[g][0][hc * 128:(hc + 1) * 128, pad + base - 64 * d:pad + base + 192 * d],
                               r=[KT[g][1]], w=[kt])
                        mk = mkr.get()
                        em.dma(mk[:], self.C["abias%d" % g][t].rearrange("x k q -> k x q"), w=[mk])
                        qz = (q0, q1)
                        for r in range(d):
                            vz = vzr.get()
                            rs = pad + base - 64 * d + r
                            src = VV[g][0][rs:rs + 255 * d + 1:d, hc * 128:(hc + 1) * 128].rearrange("(x k) c -> k x c", x=2)
                            em.dma(vz[:], src, r=[VV[g][1]], w=[vz], eng=("sp", "act", "pool")[nv % 3])
                            nv += 1
                            ps = pss.get()
                            for e_ in range(2):
                                for X in range(2):
                                    k0 = 128 * d * X + r
                                    em.op("pe", lambda e: e.matmul(ps[:, (e_ * 2 + X) * 128:(e_ * 2 + X + 1) * 128],
                                                                   kt[:, k0:k0 + 127 * d + 1:d], qz[e_][:, r:r + 127 * d + 1:d], start=True, stop=False),
                                          r=[kt, qz[e_]], w=[ps])
                                    em.op("pe", lambda e: e.matmul(ps[:, (e_ * 2 + X) * 128:(e_ * 2 + X + 1) * 128],
                                                                   identb[:], mk[:, X, :], start=False, stop=True),
                                          r=[identb, mk], w=[ps])
                            pM = pMr.get()
                            em.op("act", lambda e: e.activation(pM[:].rearrange("p a b -> p (a b)"), ps[:], AF.Exp, scale=0.125),
                                  r=[ps], w=[pM])
                            off = tt * span + r
                            pend.append((vz, pM, ob, psu, psl, U, L, off, d))
                            if len(pend) > 2:
                                self._a2_pv(*pend.pop(0))
                while pend:
                    self._a2_pv(*pend.pop(0))
                em.op("dve", lambda e: e.reciprocal(rl[:], L[:]), r=[L], w=[rl])
                em.op("pool", lambda e: e.tensor_tensor(ot[:], U[:], rl[:], ALU.mult), r=[U, rl], w=[ot])
                em.dma(OT[hc * 128:(hc + 1) * 128, R * RG:(R + 1) * RG], ot[:], r=[ot], w=[OTr], eng="pool")

    def _a2_pv(self, vz, pM, ob, psu, psl, U, L, off, d):
        em = self.em
        pus = []
        for e_ in range(2):
            pu = psu.get()
            for X in range(2):
                em.op("pe", lambda e: e.matmul(pu[:, 0:128], vz[:, X, :], pM[:, e_ * 2 + X, :],
                                               start=(X == 0), stop=(X == 1)), r=[vz, pM], w=[pu])
            pus.append(pu)
        pl = psl.get()
        n = 0
        for e_ in range(2):
            for X in range(2):
                em.op("pe", lambda e: e.matmul(pl[:, 0:128], ob[e_][:], pM[:, e_ * 2 + X, :],
                                               start=(n == 0), stop=(n == 3)), r=[ob[e_], pM], w=[pl])
                n += 1
        for e_ in range(2):
            rows = slice(e_ * 64, (e_ + 1) * 64)
            em.op("dve", lambda e: e.tensor_tensor(U[rows, off:off + 127 * d + 1:d], U[rows, off:off + 127 * d + 1:d],
                                                   pus[e_][rows, 0:128], ALU.add), r=[U, pus[e_]], w=[U], ns=True)
        em.op("dve", lambda e: e.tensor_tensor(L[:, off:off + 127 * d + 1:d], L[:, off:off + 127 * d + 1:d],
                                               pl[:, 0:128], ALU.add), r=[L, pl], w=[L], ns=True)

    def mixer_b(self, li, j):
        stop = DBG.get("b_stop", 99)
        AD = [self.scratch("AD%d" % i, [128, 2, 128, D], BF16) for i in range(3)]
        VD = self.scratch("VD", [self.Tn, D], BF16)
        HD = self.scratch("HD", [self.Tn, 2, D], BF16)
        hv = HD[0].rearrange("(i b) r c -> b r i c", b=128)
        steps = [
            lambda: self.pass_b0(j),
            lambda: self.pass_b1(li, j),
            lambda: self.pass_b2(j),
            lambda: self.pass_f1(VD[0].rearrange("(i b) c -> b i c", b=128), VD[1], "wselc", "wsels", "twi", AD[0]),
            lambda: self.pass_f1(hv[:, 0], HD[1], "hselc", "hsels", "twi", AD[1]),
            lambda: self.pass_f1(hv[:, 1], HD[1], "hselc", "hselsn", "twin", AD[2]),
            lambda: self.pass_f2h(j),
            lambda: self.pass_mid(),
            lambda: self.pass_i2(j),
        ]
        for i, f in enumerate(steps):
            if i < stop:
                f()

    def sin_layer(self, ps, hid, freq, fb, tmpr, n):
        em = self.em
        a, c = tmpr.get(), tmpr.get()
        em.op("dve", lambda e: e.tensor_scalar(a[0:64, 0:n], ps[0:64, 0:n], freq[0:64, 0:1], fb[0:64, 0:1], ALU.mult, ALU.add),
              r=[ps, freq, fb], w=[a])
        ci = c[0:64, 0:n].bitcast(mybir.dt.int32)
        em.op("dve", lambda e: e.tensor_copy(ci, a[0:64, 0:n]), r=[a], w=[c])
        em.op("dve", lambda e: e.tensor_copy(c[0:64, 0:n], ci), r=[c], w=[c])
        em.op("dve", lambda e: e.tensor_tensor(a[0:64, 0:n], a[0:64, 0:n], c[0:64, 0:n], ALU.subtract), r=[a, c], w=[a])
        em.op("act", lambda e: e.activation(hid[0:64, 0:n], a[0:64, 0:n], AF.Sin, scale=6.283179), r=[a], w=[hid])

    def pass_b0(self, j):
        em = self.em
        self.reset()
        Tn, nt = self.Tn, self.nt
        TBf = 512
        HD, HDr = self.scratch("HD", [Tn, 2, D], BF16)
        RN, RNr = self.scratch("RN", [128, D])
        W = self.W
        ones = self.alloc([128, 128], "onesf")
        em.op("dve", lambda e: e.memset(ones[:], 1.0), w=[ones])
        w1 = self.alloc([128, 64], "w1")
        em.dma(w1[0:33, :], W["b_f_w1"][j], w=[w1])
        w2 = self.alloc([128, 64], "w2")
        em.dma(w2[0:64, :], W["b_f_w2"][j], w=[w2])
        w3 = self.alloc([128, 64], "w3")
        em.dma(w3[0:64, :], W["b_f_w3"][j], w=[w3])
        wout = self.alloc([128, 2 * D], "wout")
        em.dma(wout[0:64, :], W["b_f_wout"][j], w=[wout])
        freq = self.alloc([128, 1], "freq")
        em.dma(freq[0:64, :], W["b_f_freq"][j].rearrange("(p o) -> p o", o=1), w=[freq], slow=True)
        fbs = []
        for nm in ("b_f_b1", "b_f_b2", "b_f_b3"):
            fb = self.alloc([128, 1], nm)
            em.dma(fb[0:64, :], W[nm][j].rearrange("(p o) -> p o", o=1), w=[fb], slow=True)
            em.op("dve", lambda e: e.tensor_tensor(fb[0:64, :], fb[0:64, :], freq[0:64, :], ALU.mult), r=[fb, freq], w=[fb])
            em.op("dve", lambda e: e.tensor_scalar(fb[0:64, :], fb[0:64, :], 1.0 / TWO_PI, None, ALU.mult), r=[fb], w=[fb])
            fbs.append(fb)
        em.op("dve", lambda e: e.tensor_scalar(freq[0:64, :], freq[0:64, :], 1.0 / TWO_PI, None, ALU.mult), r=[freq] + fbs, w=[freq])
        absd = self.load_const("absdelta")
        negt = self.load_const("negt")
        vmask = self.load_const("vmask")
        zr = self.rot([128, TBf], 2, "z")
        hidr = self.rot([128, TBf], 3, "hidf")
        tmpr = self.rot([128, TBf], 3, "tmp")
        decr = self.rot([128, D], 1, "dec")
        hr = self.rot([128, 2, D], 2, "hflt")
        abr = self.rot([128, 2, D], 1, "abf")
        hbr = self.rot([128, 2, D], 2, "hb", mm=True)
        nps = self.psum(2).tiles
        pp = self.psum(6)
        first = True
        for b in range(Tn // TBf):
            t0 = b * TBf
            z = zr.get()
            em.dma(z[0:33, :], self.C["zT"][:, t0:t0 + TBf], w=[z])
            ps = pp.get()
            em.op("pe", lambda e: e.matmul(ps[0:64, :], w1[0:33, :], z[0:33, :], start=True, stop=True), r=[w1, z], w=[ps])
            h1 = hidr.get()
            self.sin_layer(ps, h1, freq, fbs[0], tmpr, TBf)
            ps = pp.get()
            em.op("pe", lambda e: e.matmul(ps[0:64, :], w2[0:64, :], h1[0:64, :], start=True, stop=True), r=[w2, h1], w=[ps])
            h2 = hidr.get()
            self.sin_layer(ps, h2, freq, fbs[1], tmpr, TBf)
            ps = pp.get()
            em.op("pe", lambda e: e.matmul(ps[0:64, :], w3[0:64, :], h2[0:64, :], start=True, stop=True), r=[w3, h2], w=[ps])
            h3 = hidr.get()
            self.sin_layer(ps, h3, freq, fbs[2], tmpr, TBf)
            for tl in range(TBf // 128):
                i = b * (TBf // 128) + tl
                dec = decr.get()
                em.op("act", lambda e: e.activation(dec[:], absd[:], AF.Exp, scale=negt[:, i:i + 1]), r=[absd, negt], w=[dec])
                h = hr.get()
                for cg in range(4):
                    ps = pp.get()
                    em.op("pe", lambda e: e.matmul(ps[:], h3[0:64, tl * 128:(tl + 1) * 128], wout[0:64, cg * 512:(cg + 1) * 512],
                                                   start=True, stop=True), r=[h3, wout], w=[ps])
                    em.op("dve", lambda e: e.tensor_tensor(h[:, cg // 2, (cg % 2) * 512:(cg % 2 + 1) * 512], ps[:],
                                                           dec[:, (cg % 2) * 512:(cg % 2 + 1) * 512], ALU.mult), r=[ps, dec], w=[h])
                if i == 0:
                    em.op("dve", lambda e: e.tensor_tensor(h[0:1, 0, :], h[0:1, 0, :], h[0:1, 1, :], ALU.add), r=[h], w=[h])
                    em.op("dve", lambda e: e.memset(h[0:1, 1, :], 0.0), w=[h])
                ab = abr.get()
                em.op("act", lambda e: e.activation(ab[:].rearrange("p a b -> p (a b)"), h[:].rearrange("p a b -> p (a b)"),
                                                    AF.Abs, scale=vmask[:, i:i + 1]), r=[h, vmask], w=[ab])
                last = (i == nt - 1)
                for dr in range(2):
                    for hf in range(2):
                        em.op("pe", lambda e: e.matmul(nps[hf][:], ones[:], ab[:, dr, hf * 512:(hf + 1) * 512],
                                                       start=(first and dr == 0), stop=(last and dr == 1)), r=[ones, ab], w=[nps[hf]])
                first = False
                hb = hbr.get()
                em.op("pool", lambda e: e.tensor_copy(hb[:], h[:]), r=[h], w=[hb])
                em.dma(HD[i * 128:(i + 1) * 128], hb[:], r=[hb], w=[HDr], eng="act")
        rn = decr.get()
        for hf in range(2):
            em.op("dve", lambda e: e.reciprocal(rn[:, hf * 512:(hf + 1) * 512], nps[hf][:]), r=[nps[hf]], w=[rn])
        em.dma(RN, rn[:], r=[rn], w=[RNr])

    def pass_b1(self, li, j):
        em = self.em
        self.reset()
        TB = 512
        Tn = self.Tn
        xT, xTr = self.scratch("xT", [D, Tn])
        xTv = xT.rearrange("(c p) t -> p c t", p=128)
        U0, U0r = self.scratch("U0T", [D, Tn + 2])
        U12, U12r = self.scratch("U12", [Tn + 2, 2 * D], BF16)
        w_in = self.Wb["b_w_in"][j]
        ones = self.ones_tile()
        gm = self.load_cols(self.W["g_mix"].rearrange("l d -> (l d)"), 4 * KC, "gm")
        zt = self.alloc([128, 8], "zero")
        em.op("pool", lambda e: e.memset(zt[:], 0.0), w=[zt])
        ztb = self.alloc([128, 2 * D], "zerob", mm=True)
        em.op("pool", lambda e: e.memset(ztb[:], 0.0), w=[ztb])
        for c in range(KC):
            em.dma(U0[c * 128:(c + 1) * 128, 0:1], zt[:, 0:1], r=[zt], w=[U0r], slow=True)
            em.dma(U0[c * 128:(c + 1) * 128, Tn + 1:Tn + 2], zt[:, 0:1], r=[zt], w=[U0r], slow=True)
        em.dma(U12[0:1, :], ztb[0:1, :], r=[ztb], w=[U12r])
        em.dma(U12[Tn + 1:Tn + 2, :], ztb[0:1, :], r=[ztb], w=[U12r])
        xrot = self.rot([128, KC, TB], 2, "x")
        xn_t = self.alloc([128, KC, TB], "xn", mm=True)
        sq_t = self.alloc([128, KC, TB], "sq", mm=True)
        rstd = self.alloc([128, TB], "rstd")
        wrot = self.rot([128, 4096], 8, "w", mm=True)
        stg = self.rot([128, KC, TB], 1, "stg")
        ust = self.rot([128, 2 * D], 8, "ust", mm=True)
        pp = self.psum(8)
        wv = w_in.rearrange("(c p) n -> p c n", p=128)
        for b in range(Tn // TB):
            t0 = b * TB
            x = xrot.get()
            em.dma(x[:], xTv[:, :, t0:t0 + TB], r=[xTr], w=[x])
            self.rmsnorm(x, xn_t, sq_t, gm, li, ones, pp, rstd, TB)
            st_ = stg.get()

            def epi(m, ps, st_=st_):
                em.op("act", lambda e: e.copy(st_[:, m, :], ps[:, 0:TB]), r=[ps], w=[st_])
            self.linear_fm(w_in, 0, D, xn_t, KC, TB, wrot, pp, epi)
            self.defer(lambda st_=st_, t0=t0: em.dma(U0.rearrange("(c p) t -> p c t", p=128)[:, :, 1 + t0:1 + t0 + TB], st_[:], r=[st_], w=[U0r]))
            us = [ust.get() for _ in range(TB // 128)]
            for cg in range(4):
                wt = wrot.get()
                wtv = wt[:].rearrange("p (c n) -> p c n", c=KC)
                em.dma(wtv, wv[:, :, D + cg * 512:D + (cg + 1) * 512], w=[wt])
                for tl in range(TB // 128):
                    ps = pp.get()
                    for k in range(KC):
                        em.op("pe", lambda e: e.matmul(ps[:], xn_t[:, k, tl * 128:(tl + 1) * 128], wtv[:, k, :],
                                                       start=(k == 0), stop=(k == KC - 1)), r=[xn_t, wt], w=[ps])
                    if (cg + tl) % 2 == 0:
                        em.op("act", lambda e: e.copy(us[tl][:, cg * 512:(cg + 1) * 512], ps[:]), r=[ps], w=[us[tl]])
                    else:
                        em.op("dve", lambda e: e.tensor_copy(us[tl][:, cg * 512:(cg + 1) * 512], ps[:]), r=[ps], w=[us[tl]])
            for tl in range(TB // 128):
                r0 = 1 + t0 + tl * 128
                self.defer(lambda r0=r0, u=us[tl]: em.dma(U12[r0:r0 + 128, :], u[:], r=[u], w=[U12r]))

    def pass_b2(self, j):
        em = self.em
        self.reset()
        Tn, nt = self.Tn, self.nt
        U12, U12r = self.scratch("U12", [Tn + 2, 2 * D], BF16)
        VD, VDr = self.scratch("VD", [Tn, D], BF16)
        cw = self.W["b_conv_w"][j]
        w0R = self.load_rep(cw[0, D:3 * D], 2 * D, "w0R")
        w1R = self.load_rep(cw[1, D:3 * D], 2 * D, "w1R")
        w2R = self.load_rep(cw[2, D:3 * D], 2 * D, "w2R")
        bR = self.load_rep(self.W["b_conv_b"][j, D:3 * D], 2 * D, "bR")
        mp = self.load_const("mp")
        mn = self.load_const("mn")
        ucr = self.rot([128, 2 * D], 3, "uc", mm=True)
        upr = self.rot([128, 2 * D], 3, "up", mm=True)
        unr = self.rot([128, 2 * D], 3, "un", mm=True)
        cr = self.rot([128, 2 * D], 2, "c", mm=True)
        tr = self.rot([128, 2 * D], 2, "t", mm=True)
        vr = self.rot([128, D], 2, "v", mm=True)
        def b2_load(i):
            uc, up, un = ucr.get(), upr.get(), unr.get()
            em.dma(uc[:], U12[1 + i * 128:1 + (i + 1) * 128, :], r=[U12r], w=[uc])
            em.dma(up[:], U12[i * 128:(i + 1) * 128, :], r=[U12r], w=[up])
            em.dma(un[:], U12[2 + i * 128:2 + (i + 1) * 128, :], r=[U12r], w=[un])
            return uc, up, un

        def b2_body(i, tl):
            uc, up, un = tl
            c = cr.get()
            em.op("dve", lambda e: e.tensor_tensor(c[:], uc[:], w1R[:], ALU.mult), r=[uc, w1R], w=[c])
            em.op("pool", lambda e: e.tensor_tensor(c[:], c[:], bR[:], ALU.add), r=[c, bR], w=[c])
            t1 = tr.get()
            em.op("pool", lambda e: e.tensor_tensor(t1[:], up[:], w0R[:], ALU.mult), r=[up, w0R], w=[t1])
            em.op("dve", lambda e: e.scalar_tensor_tensor(c[:], t1[:], mp[:, i:i + 1], c[:], ALU.mult, ALU.add), r=[t1, mp, c], w=[c])
            t2 = tr.get()
            em.op("dve", lambda e: e.tensor_tensor(t2[:], un[:], w2R[:], ALU.mult), r=[un, w2R], w=[t2])
            em.op("dve", lambda e: e.scalar_tensor_tensor(c[:], t2[:], mn[:, i:i + 1], c[:], ALU.mult, ALU.add), r=[t2, mn, c], w=[c])
            v = vr.get()
            em.op("dve", lambda e: e.tensor_tensor(v[:], c[:, D:2 * D], c[:, 0:D], ALU.mult), r=[c], w=[v])
            em.dma(VD[i * 128:(i + 1) * 128, :], v[:], r=[v], w=[VDr], eng="act")

        self.prefetch_loop(nt, b2_load, b2_body)

    def pass_f1(self, src_b, src_res, selc_n, sels_n, twi_n, AD):
        em = self.em
        self.reset()
        nt = self.nt
        ADa, ADr = AD
        selc = self.alloc([128, 128], "selc", mm=True)
        self.zero_fill(selc, 128)
        em.dma(selc[0:nt, :], self.C[selc_n], w=[selc])
        sels = self.alloc([128, 128], "sels", mm=True)
        self.zero_fill(sels, 128)
        em.dma(sels[0:nt, :], self.C[sels_n], w=[sels])
        twr = self.load_const("twr")
        twi = self.load_const(twi_n)
        sr = self.rot([128, D], 3, "s", mm=True)
        for tl in sr.tiles:
            self.zero_fill(tl, D)
        ar = self.rot([128, 2, D], 2, "a", mm=True)
        tmpr = self.rot([128, 512], 4, "tmp")
        pp = self.psum(8)
        mode = DBG.get("f1_mode", 9)

        def f1_load(b):
            s_ = sr.get()
            em.dma(s_[0:nt, :], src_b[b], r=[src_res], w=[s_])
            return s_

        def f1_body(b, s_):
            a = ar.get()
            for hf in range(2):
                if mode < 1:
                    continue
                pre, pim = pp.get(), pp.get()
                em.op("pe", lambda e: e.matmul(pre[:], selc[:], s_[:, hf * 512:(hf + 1) * 512], start=True, stop=True),
                      r=[selc, s_], w=[pre])
                em.op("pe", lambda e: e.matmul(pim[:], sels[:], s_[:, hf * 512:(hf + 1) * 512], start=True, stop=True),
                      r=[sels, s_], w=[pim])
                if mode < 2:
                    continue
                self.twiddle(pre, pim, twr[:, b:b + 1], twi[:, b:b + 1], [twr, twi], a, hf, tmpr, 128)
            if mode >= 3:
                em.dma(ADa[b].rearrange("r k c -> k r c"), a[:], r=[a], w=[ADr], eng="pool")

        self.prefetch_loop(128, f1_load, f1_body)

    def twiddle(self, pre, pim, cr, ci, tabs, out, hf, tmpr, np_):
        em = self.em
        t1, t2 = tmpr.get(), tmpr.get()
        sl = slice(hf * 512, (hf + 1) * 512)
        twm = DBG.get("tw_mode", 3)
        if twm & 1:
            em.op("act", lambda e: e.activation(t1[0:np_, :], pim[0:np_, :], AF.Identity, scale=ci), r=[pim] + tabs, w=[t1])
            em.op("act", lambda e: e.activation(t2[0:np_, :], pre[0:np_, :], AF.Identity, scale=ci), r=[pre] + tabs, w=[t2])
        if twm & 2:
            em.op("dve", lambda e: e.scalar_tensor_tensor(out[0:np_, 0, sl], pre[0:np_, :], cr, t1[0:np_, :], ALU.mult, ALU.subtract),
                  r=[pre, t1] + tabs, w=[out])
            em.op("dve", lambda e: e.scalar_tensor_tensor(out[0:np_, 1, sl], pim[0:np_, :], cr, t2[0:np_, :], ALU.mult, ALU.add),
                  r=[pim, t2] + tabs, w=[out])

    def pass_f2h(self, j):
        em = self.em
        self.reset()
        ADf, ADfr = self.scratch("AD1", [128, 2, 128, D], BF16)
        ADb, ADbr = self.scratch("AD2", [128, 2, 128, D], BF16)
        HH, HHr = self.scratch("HH", [128, 2, 65, D])
        RN, RNr = self.scratch("RN", [128, D])
        c2 = self.load_const("c2", mm=True)
        s2 = self.load_const("s2", mm=True)
        ns2 = self.load_const("ns2", mm=True)
        wk = self.alloc([128, 128], "wk")
        em.dma(wk[0:65, :], self.C["wk"], w=[wk])
        rn = self.alloc([128, D], "rn")
        em.dma(rn[:], RN, r=[RNr], w=[rn])
        bd = self.load_rep(self.W["b_bias_d"][j], D, "bd")
        afr = self.rot([128, 2, D], 3, "af", mm=True)
        abr = self.rot([128, 2, D], 3, "ab", mm=True)
        hhr = self.rot([128, 2, D], 2, "hh")
        tmpr = self.rot([128, 512], 4, "tmp")
        pp = self.psum(8)
        def f2_load(K1):
            af, ab = afr.get(), abr.get()
            em.dma(af[:], ADf[:, :, K1, :], r=[ADfr], w=[af])
            em.dma(ab[:], ADb[:, :, K1, :], r=[ADbr], w=[ab])
            return af, ab

        def f2_body(K1, tl):
            af, ab = tl
            hh = hhr.get()
            for hf in range(2):
                sl = slice(hf * 512, (hf + 1) * 512)
                pre, pim = pp.get(), pp.get()
                terms_re = [(c2, af, 0), (s2, af, 1), (c2, ab, 0), (ns2, ab, 1)]
                terms_im = [(c2, af, 1), (ns2, af, 0), (c2, ab, 1), (s2, ab, 0)]
                for n, (tb, src, ri) in enumerate(terms_re):
                    em.op("pe", lambda e: e.matmul(pre[0:65, :], tb[:, 0:65], src[:, ri, sl], start=(n == 0), stop=(n == 3)),
                          r=[tb, src], w=[pre])
                for n, (tb, src, ri) in enumerate(terms_im):
                    em.op("pe", lambda e: e.matmul(pim[0:65, :], tb[:, 0:65], src[:, ri, sl], start=(n == 0), stop=(n == 3)),
                          r=[tb, src], w=[pim])
                t1, t2 = tmpr.get(), tmpr.get()
                em.op("dve", lambda e: e.tensor_tensor(t1[0:65, :], pre[0:65, :], rn[0:65, sl], ALU.mult), r=[pre, rn], w=[t1])
                em.op("pool", lambda e: e.tensor_tensor(t1[0:65, :], t1[0:65, :], bd[0:65, sl], ALU.add), r=[t1, bd], w=[t1])
                em.op("act", lambda e: e.activation(hh[0:65, 0, sl], t1[0:65, :], AF.Identity, scale=wk[0:65, K1:K1 + 1]), r=[t1, wk], w=[hh])
                em.op("dve", lambda e: e.tensor_tensor(t2[0:65, :], pim[0:65, :], rn[0:65, sl], ALU.mult), r=[pim, rn], w=[t2])
                em.op("act", lambda e: e.activation(hh[0:65, 1, sl], t2[0:65, :], AF.Identity, scale=wk[0:65, K1:K1 + 1]), r=[t2, wk], w=[hh])
            em.dma(HH[K1].rearrange("r k c -> k r c"), hh[0:65, :, :], r=[hh], w=[HHr], eng="pool")

        self.prefetch_loop(128, f2_load, f2_body)

    def pass_mid(self):
        em = self.em
        self.reset()
        ADv, ADvr = self.scratch("AD0", [128, 2, 128, D], BF16)
        HH, HHr = self.scratch("HH", [128, 2, 65, D])
        ZD, ZDr = self.scratch("ZD", [128, 2, 128, D], BF16)
        c2 = self.load_const("c2", mm=True)
        s2 = self.load_const("s2", mm=True)
        ns2 = self.load_const("ns2", mm=True)
        ic = self.alloc([128, 128], "ic", mm=True)
        em.dma(ic[0:65, :], self.C["ic"], w=[ic])
        is_ = self.alloc([128, 128], "is", mm=True)
        em.dma(is_[0:65, :], self.C["is"], w=[is_])
        nis = self.alloc([128, 128], "nis", mm=True)
        em.dma(nis[0:65, :], self.C["nis"], w=[nis])
        tir = self.load_const("tir")
        tii = self.load_const("tii")
        avr = self.rot([128, 2, D], 3, "av", mm=True)
        hhr = self.rot([128, 2, D], 3, "hh")
        yr = self.rot([128, 2, 512], 3, "y", mm=True)
        zr = self.rot([128, 2, D], 3, "z", mm=True)
        tmpr = self.rot([128, 512], 12, "tmp")
        pp = self.psum(8)
        def mid_load(K1):
            av, hh = avr.get(), hhr.get()
            em.dma(av[:], ADv[:, :, K1, :], r=[ADvr], w=[av])
            em.dma(hh[0:65, :, :], HH[K1].rearrange("r k c -> k r c"), r=[HHr], w=[hh])
            return av, hh

        pend = []

        def stage2(K1, hf, y, z):
            pzr, pzi = pp.get(), pp.get()
            for n, (tb, ri) in enumerate([(ic, 0), (nis, 1)]):
                em.op("pe", lambda e: e.matmul(pzr[:], tb[0:65, :], y[0:65, ri, :], start=(n == 0), stop=(n == 1)), r=[tb, y], w=[pzr])
            for n, (tb, ri) in enumerate([(ic, 1), (is_, 0)]):
                em.op("pe", lambda e: e.matmul(pzi[:], tb[0:65, :], y[0:65, ri, :], start=(n == 0), stop=(n == 1)), r=[tb, y], w=[pzi])
            self.twiddle(pzr, pzi, tir[:, K1:K1 + 1], tii[:, K1:K1 + 1], [tir, tii], z, hf, tmpr, 128)
            if hf == 1:
                em.dma(ZD[K1].rearrange("r b c -> b r c"), z[:], r=[z], w=[ZDr], eng="pool")

        def mid_body(K1, tl):
            av, hh = tl
            z = zr.get()
            for hf in range(2):
                sl = slice(hf * 512, (hf + 1) * 512)
                pxr, pxi = pp.get(), pp.get()
                for n, (tb, ri) in enumerate([(c2, 0), (s2, 1)]):
                    em.op("pe", lambda e: e.matmul(pxr[0:65, :], tb[:, 0:65], av[:, ri, sl], start=(n == 0), stop=(n == 1)),
                          r=[tb, av], w=[pxr])
                for n, (tb, ri) in enumerate([(c2, 1), (ns2, 0)]):
                    em.op("pe", lambda e: e.matmul(pxi[0:65, :], tb[:, 0:65], av[:, ri, sl], start=(n == 0), stop=(n == 1)),
                          r=[tb, av], w=[pxi])
                if pend:
                    stage2(*pend.pop(0))
                ta, tb_, tc, td = tmpr.get(), tmpr.get(), tmpr.get(), tmpr.get()
                y = yr.get()
                em.op("dve", lambda e: e.tensor_tensor(ta[0:65, :], pxr[0:65, :], hh[0:65, 0, sl], ALU.mult), r=[pxr, hh], w=[ta])
                em.op("dve", lambda e: e.tensor_tensor(tb_[0:65, :], pxi[0:65, :], hh[0:65, 1, sl], ALU.mult), r=[pxi, hh], w=[tb_])
                em.op("pool", lambda e: e.tensor_tensor(y[0:65, 0, :], ta[0:65, :], tb_[0:65, :], ALU.subtract), r=[ta, tb_], w=[y])
                em.op("dve", lambda e: e.tensor_tensor(tc[0:65, :], pxr[0:65, :], hh[0:65, 1, sl], ALU.mult), r=[pxr, hh], w=[tc])
                em.op("dve", lambda e: e.tensor_tensor(td[0:65, :], pxi[0:65, :], hh[0:65, 0, sl], ALU.mult), r=[pxi, hh], w=[td])
                em.op("pool", lambda e: e.tensor_tensor(y[0:65, 1, :], tc[0:65, :], td[0:65, :], ALU.add), r=[tc, td], w=[y])
                pend.append((K1, hf, y, z))

        self.prefetch_loop(128, mid_load, mid_body)
        while pend:
            stage2(*pend.pop(0))

    def pass_i2(self, j):
        em = self.em
        self.reset()
        Tn, nt = self.Tn, self.nt
        ZD, ZDr = self.scratch("ZD", [128, 2, 128, D], BF16)
        U0, U0r = self.scratch("U0T", [D, Tn + 2])
        GT, GTr = self.scratch("GT", [D, Tn], BF16)
        iwc = self.load_const("iwc", mm=True)
        iwsn = self.load_const("iwsn", mm=True)
        cw = self.W["b_conv_w"][j]
        w0c = self.load_cols(cw[0, 0:D], KC, "w0c")
        w1c = self.load_cols(cw[1, 0:D], KC, "w1c")
        w2c = self.load_cols(cw[2, 0:D], KC, "w2c")
        bc = self.load_cols(self.W["b_conv_b"][j, 0:D], KC, "bc")
        yT = self.alloc([128, Tn], "yT")
        yTv = yT[:].rearrange("p (i b) -> p b i", b=128)
        NB = 512 // nt
        zzr = self.rot([128, 2, NB, 128], 3, "zz", mm=True)
        TB = 512
        u0r = self.rot([128, TB + 2], 3, "u0")
        mpr = self.rot([128, TB], 3, "mp")
        mnr = self.rot([128, TB], 3, "mn")
        x0r = self.rot([128, TB], 2, "x0")
        ttr = self.rot([128, TB], 2, "tt")
        gr = self.rot([128, TB], 2, "g", mm=True)
        pp = self.psum(8)
        for cc in range(KC):
            def zz_load(ib):
                b0 = ib * NB
                zz = zzr.get()
                em.dma(zz[:], ZD[:, :, b0:b0 + NB, cc * 128:(cc + 1) * 128], r=[ZDr], w=[zz])
                return zz

            def zz_body(ib, zz):
                b0 = ib * NB
                ps = pp.get()
                for bb in range(NB):
                    em.op("pe", lambda e: e.matmul(ps[:, bb * nt:(bb + 1) * nt], zz[:, 0, bb, :], iwc[:, 0:nt], start=True, stop=False),
                          r=[zz, iwc], w=[ps])
                    em.op("pe", lambda e: e.matmul(ps[:, bb * nt:(bb + 1) * nt], zz[:, 1, bb, :], iwsn[:, 0:nt], start=False, stop=True),
                          r=[zz, iwsn], w=[ps])
                if ib % 2 == 0:
                    em.op("act", lambda e: e.copy(yTv[:, b0:b0 + NB, :], ps[:, 0:NB * nt].rearrange("p (b i) -> p b i", b=NB)), r=[ps], w=[yT])
                else:
                    em.op("dve", lambda e: e.tensor_copy(yTv[:, b0:b0 + NB, :], ps[:, 0:NB * nt].rearrange("p (b i) -> p b i", b=NB)), r=[ps], w=[yT])

            self.prefetch_loop(128 // NB, zz_load, zz_body)

            def g_load(blk):
                t0 = blk * TB
                u0 = u0r.get()
                em.dma(u0[:], U0[cc * 128:(cc + 1) * 128, t0:t0 + TB + 2], r=[U0r], w=[u0])
                mpt, mnt = mpr.get(), mnr.get()
                em.dma(mpt[:], self.C["mpR"][:, t0:t0 + TB], w=[mpt])
                em.dma(mnt[:], self.C["mnR"][:, t0:t0 + TB], w=[mnt])
                return u0, mpt, mnt

            def g_body(blk, tl):
                u0, mpt, mnt = tl
                t0 = blk * TB
                x0 = x0r.get()
                em.op("dve", lambda e: e.tensor_scalar(x0[:], u0[:, 1:TB + 1], w1c[:, cc:cc + 1], bc[:, cc:cc + 1], ALU.mult, ALU.add),
                      r=[u0, w1c, bc], w=[x0])
                t1 = ttr.get()
                em.op("pool", lambda e: e.tensor_tensor(t1[:], u0[:, 0:TB], mpt[:], ALU.mult), r=[u0, mpt], w=[t1])
                em.op("dve", lambda e: e.scalar_tensor_tensor(x0[:], t1[:], w0c[:, cc:cc + 1], x0[:], ALU.mult, ALU.add), r=[t1, w0c, x0], w=[x0])
                t2 = ttr.get()
                em.op("dve", lambda e: e.tensor_tensor(t2[:], u0[:, 2:TB + 2], mnt[:], ALU.mult), r=[u0, mnt], w=[t2])
                em.op("dve", lambda e: e.scalar_tensor_tensor(x0[:], t2[:], w2c[:, cc:cc + 1], x0[:], ALU.mult, ALU.add), r=[t2, w2c, x0], w=[x0])
                g = gr.get()
                em.op("pool", lambda e: e.tensor_tensor(g[:], x0[:], yT[:, t0:t0 + TB], ALU.mult), r=[x0, yT], w=[g])
                em.dma(GT[cc * 128:(cc + 1) * 128, t0:t0 + TB], g[:], r=[g], w=[GTr], eng="pool")

            self.prefetch_loop(Tn // TB, g_load, g_body)


_CACHE = {}


def run_cores(xs, mems, lsegs, weights, kinds, debug_out=()):
    nt = xs[0].shape[0] // 128
    consts = [build_consts(nt, l) for l in lsegs]
    wshapes = {k: weights[k].shape for k in WEIGHT_NAMES}
    cshapes = {k: v.shape for k, v in consts[0].items()}
    key = (nt, tuple(kinds), tuple(debug_out))
    if key not in _CACHE:
        _CACHE[key] = Builder(nt, kinds, wshapes, cshapes, debug_out).build()
    nc = _CACHE[key]
    in_maps = []
    for ci in range(len(xs)):
        m = {k: np.ascontiguousarray(weights[k], dtype=np.float32) for k in WEIGHT_NAMES}
        for k, v in consts[ci].items():
            m["c_" + k] = v.astype(ml_dtypes.bfloat16) if k in BF_CONSTS else v
        m["x"] = np.ascontiguousarray(xs[ci], dtype=np.float32)
        m["mem"] = np.ascontiguousarray(mems[ci], dtype=np.float32)
        in_maps.append(m)
    res = run_bass_kernel_spmd(nc, in_maps, core_ids=list(range(len(xs))))
    return res.results


def kernel(**inputs):
    xp = np.asarray(inputs["x_prompt"], dtype=np.float32)
    xs_ = np.asarray(inputs["x_sample"], dtype=np.float32)
    mp = np.asarray(inputs["mem_prompt"], dtype=np.float32)
    ms = np.asarray(inputs["mem_sample"], dtype=np.float32)
    weights = {k: np.asarray(inputs[k], dtype=np.float32) for k in WEIGHT_NAMES}
    xs, mems, lsegs = [], [], []
    for c in range(4):
        xs.append(xp[4 * c:4 * c + 4].reshape(8192, D))
        mems.append(mp[4 * c:4 * c + 4])
        lsegs.append(2048)
    for c in range(4):
        xs.append(xs_[c])
        mems.append(np.broadcast_to(ms[c][None], (4, MEM, D)))
        lsegs.append(8192)
    res = run_cores(xs, mems, lsegs, weights, ["a", "b", "c", "a"])
    yp = np.stack([res[c]["y"] for c in range(4)]).reshape(16, 2048, D)
    ys = np.stack([res[4 + c]["y"] for c in range(4)]).reshape(4, 8192, D)
    return (yp.astype(np.float32), ys.astype(np.float32))
```
